# Optimizing a Trainium2 kernel written in Bass

```python
import jax, jax.numpy as jnp
from jax import lax
import numpy as np

D_MODEL = 2048
BATCH = 4
SEQ = 8192
DEPTH = 1

PLE_DIM = 256
SB_HEADS = 8
SB_HEAD_DIM = 128
MLA_HEADS = 8
MLA_NOPE_DIM = 128
MLA_ROPE_DIM = 64
MLA_V_DIM = 128
MLA_Q_RANK = 512
MLA_KV_RANK = 512
D_FF = 4 * D_MODEL
BLOCK_Q = 128
ROPE_THETA = 10000.0
EPS = 1e-6
SB_WIDTH = SB_HEADS * SB_HEAD_DIM
MLA_WIDTH = MLA_HEADS * MLA_V_DIM
MLA_QK_DIM = MLA_NOPE_DIM + MLA_ROPE_DIM
IN_SPLITS = (SB_WIDTH, SB_WIDTH, SB_WIDTH, MLA_Q_RANK, MLA_KV_RANK, MLA_ROPE_DIM, D_MODEL, D_MODEL)
IN_WIDTH = sum(IN_SPLITS)

kernel_name = "hybrid_stickbreak_mla_gated_block"


def rmsnorm(x, g):
    xf = x.astype(jnp.float32)
    y = xf * lax.rsqrt(jnp.mean(xf * xf, axis=-1, keepdims=True) + EPS)
    return (y * g.astype(jnp.float32)).astype(x.dtype)


def rope(x, cos, sin):
    xf = x.astype(jnp.float32)
    x1, x2 = jnp.split(xf, 2, axis=-1)
    out = jnp.concatenate([x1 * cos - x2 * sin, x1 * sin + x2 * cos], axis=-1)
    return out.astype(x.dtype)


def to_blocks(t):
    b, s = t.shape[0], t.shape[1]
    return jnp.moveaxis(t.reshape(b, s // BLOCK_Q, BLOCK_Q, *t.shape[2:]), 1, 0)


def from_blocks(t):
    t = jnp.moveaxis(t, 0, 1)
    return t.reshape(t.shape[0], t.shape[1] * t.shape[2], *t.shape[3:])


def stick_breaking_attention(q, k, v):
    s_len = q.shape[1]
    scale = SB_HEAD_DIM ** -0.5
    kpos = jnp.arange(s_len)

    def block(args):
        qb, i = args
        qpos = i * BLOCK_Q + jnp.arange(BLOCK_Q)
        z = jnp.einsum('bqhd,bkhd->bhqk', qb, k, preferred_element_type=jnp.float32) * scale
        mask = kpos[None, :] < qpos[:, None]
        log_fail = jnp.where(mask, jax.nn.log_sigmoid(-z), 0.0)
        later = lax.cumsum(log_fail, axis=3, reverse=True) - log_fail
        w = jnp.where(mask, jnp.exp(jax.nn.log_sigmoid(z) + later), 0.0)
        return jnp.einsum('bhqk,bkhd->bqhd', w.astype(v.dtype), v)

    out = lax.map(block, (to_blocks(q), jnp.arange(s_len // BLOCK_Q)))
    return from_blocks(out)


def mla_attention(q_nope, q_rope, k_nope, k_rope, v):
    s_len = q_nope.shape[1]
    scale = MLA_QK_DIM ** -0.5
    kpos = jnp.arange(s_len)

    def block(args):
        qn, qr, i = args
        qpos = i * BLOCK_Q + jnp.arange(BLOCK_Q)
        s = (jnp.einsum('bqhd,bkhd->bhqk', qn, k_nope, preferred_element_type=jnp.float32)
             + jnp.einsum('bqhr,bkr->bhqk', qr, k_rope, preferred_element_type=jnp.float32)) * scale
        mask = kpos[None, :] <= qpos[:, None]
        pr = jax.nn.softmax(jnp.where(mask, s, -jnp.inf), axis=-1)
        return jnp.einsum('bhqk,bkhd->bqhd', pr.astype(v.dtype), v)

    out = lax.map(block, (to_blocks(q_nope), to_blocks(q_rope), jnp.arange(s_len // BLOCK_Q)))
    return from_blocks(out)


def hybrid_layer(x, p_i, cos, sin, g_pre_mix, w_in, g_cq, g_ckv, w_q_up, w_kv_up,
                 w_sb_o, w_mla_o, w_out, g_post_mix, g_pre_mlp, w_up, w_down,
                 g_post_mlp, w_ple, g_ple, w_ple_gate):
    b, s, _ = x.shape
    h = rmsnorm(x, g_pre_mix)
    proj = h @ w_in
    offsets = [int(o) for o in np.cumsum(IN_SPLITS)[:-1]]
    sb_q, sb_k, sb_v, c_q, c_kv, k_r, gate_sb, gate_mla = jnp.split(proj, offsets, axis=-1)

    o_sb = stick_breaking_attention(sb_q.reshape(b, s, SB_HEADS, SB_HEAD_DIM),
                                    sb_k.reshape(b, s, SB_HEADS, SB_HEAD_DIM),
                                    sb_v.reshape(b, s, SB_HEADS, SB_HEAD_DIM))
    o_sb = o_sb.reshape(b, s, SB_WIDTH) @ w_sb_o

    q = (rmsnorm(c_q, g_cq) @ w_q_up).reshape(b, s, MLA_HEADS, MLA_QK_DIM)
    q_nope = q[..., :MLA_NOPE_DIM]
    q_rope = rope(q[..., MLA_NOPE_DIM:], cos[:, :, None, :], sin[:, :, None, :])
    kv = (rmsnorm(c_kv, g_ckv) @ w_kv_up).reshape(b, s, MLA_HEADS, MLA_NOPE_DIM + MLA_V_DIM)
    k_nope, v = kv[..., :MLA_NOPE_DIM], kv[..., MLA_NOPE_DIM:]
    k_rope = rope(k_r, cos, sin)
    o_mla = mla_attention(q_nope, q_rope, k_nope, k_rope, v)
    o_mla = o_mla.reshape(b, s, MLA_WIDTH) @ w_mla_o

    mixed = jax.nn.sigmoid(gate_sb) * o_sb + jax.nn.sigmoid(gate_mla) * o_mla
    x = x + rmsnorm(mixed @ w_out, g_post_mix)

    h = rmsnorm(x, g_pre_mlp)
    u = jnp.square(jax.nn.relu(h @ w_up))
    x = x + rmsnorm(u @ w_down, g_post_mlp)

    e = rmsnorm(p_i @ w_ple, g_ple)
    x = x + jax.nn.sigmoid(x @ w_ple_gate) * e
    return x


def setup_inputs(seed: int = 0) -> dict:
    key = jax.random.key(seed)
    ks = jax.random.split(key, 24)
    f32 = jnp.float32

    def dense(k, fan_in, fan_out):
        return jax.random.normal(k, (DEPTH, fan_in, fan_out), f32) * fan_in ** -0.5

    def gain(k, n):
        return 1.0 + 0.02 * jax.random.normal(k, (DEPTH, n), f32)

    x = jax.random.normal(ks[0], (BATCH, SEQ, D_MODEL), f32)
    p = jax.random.normal(ks[1], (DEPTH, BATCH, SEQ, PLE_DIM), f32)
    offset = jax.random.randint(ks[2], (BATCH, 1), 0, 1024, dtype=jnp.int32)
    positions = (jnp.arange(SEQ, dtype=jnp.int32)[None, :] + offset).astype(jnp.int32)
    return {
        "x": x,
        "p": p,
        "positions": positions,
        "g_pre_mix": gain(ks[3], D_MODEL),
        "w_in": dense(ks[4], D_MODEL, IN_WIDTH),
        "g_cq": gain(ks[5], MLA_Q_RANK),
        "g_ckv": gain(ks[6], MLA_KV_RANK),
        "w_q_up": dense(ks[7], MLA_Q_RANK, MLA_HEADS * MLA_QK_DIM),
        "w_kv_up": dense(ks[8], MLA_KV_RANK, MLA_HEADS * (MLA_NOPE_DIM + MLA_V_DIM)),
        "w_sb_o": dense(ks[9], SB_WIDTH, D_MODEL),
        "w_mla_o": dense(ks[10], MLA_WIDTH, D_MODEL),
        "w_out": dense(ks[11], D_MODEL, D_MODEL),
        "g_post_mix": gain(ks[12], D_MODEL),
        "g_pre_mlp": gain(ks[13], D_MODEL),
        "w_up": dense(ks[14], D_MODEL, D_FF),
        "w_down": dense(ks[15], D_FF, D_MODEL),
        "g_post_mlp": gain(ks[16], D_MODEL),
        "w_ple": dense(ks[17], PLE_DIM, D_MODEL),
        "g_ple": gain(ks[18], D_MODEL),
        "w_ple_gate": dense(ks[19], D_MODEL, D_MODEL),
    }


def reference(x, p, positions, g_pre_mix, w_in, g_cq, g_ckv, w_q_up, w_kv_up,
              w_sb_o, w_mla_o, w_out, g_post_mix, g_pre_mlp, w_up, w_down,
              g_post_mlp, w_ple, g_ple, w_ple_gate):
    half = MLA_ROPE_DIM // 2
    inv_freq = ROPE_THETA ** (-jnp.arange(half, dtype=jnp.float32) / half)
    ang = positions.astype(jnp.float32)[..., None] * inv_freq
    cos, sin = jnp.cos(ang), jnp.sin(ang)
    for i in range(DEPTH):
        x = hybrid_layer(x, p[i], cos, sin, g_pre_mix[i], w_in[i], g_cq[i], g_ckv[i],
                         w_q_up[i], w_kv_up[i], w_sb_o[i], w_mla_o[i], w_out[i],
                         g_post_mix[i], g_pre_mlp[i], w_up[i], w_down[i], g_post_mlp[i],
                         w_ple[i], g_ple[i], w_ple_gate[i])
    return x
```

```python
import numpy as np
from contextlib import ExitStack
import concourse.bass as bass
import concourse.mybir as mybir
from concourse.bass_utils import run_bass_kernel_spmd

F32 = mybir.dt.float32
BF16 = mybir.dt.bfloat16
I32 = mybir.dt.int32
AF = mybir.ActivationFunctionType
ALU = mybir.AluOpType
AX = mybir.AxisListType


class Buf:
    __slots__ = ("name", "writers", "readers", "war", "excl")

    def __init__(self, name="", excl=False):
        self.name = name
        self.writers = {}
        self.readers = {}
        self.war = {}
        self.excl = excl


class _Op:
    __slots__ = ("eng", "fn", "deps", "is_dma", "signal", "need_signal", "slot", "idx")


SEM_LIMIT = 30000


class FW:
    ENGS = ("pe", "act", "dve", "pool", "sp")
    NDMASEM = 24

    def __init__(self, nc, es):
        self.nc = nc
        self.es = es
        self.ops = []
        self.engobj = {"pe": nc.tensor, "act": nc.scalar, "dve": nc.vector, "pool": nc.gpsimd, "sp": nc.sync}
        self.nsem = 0
        self.barrier_idx = 0
        self.emitted = 0
        self.slot_prev = [None] * self.NDMASEM
        self.ndma = 0
        self.engsem = {}
        self.engcnt = {}
        self.waited = {e: {} for e in self.ENGS}
        self.nwait = 0
        self.nsig = 0
        self.last_on_eng = {}
        self.dma_since_barrier = []
        self.serial = False

    def new_sem(self, name):
        self.nsem += 1
        return self.es.enter_context(self.nc.semaphore(f"{name}_{self.nsem}"))

    def _record(self, eng, fn, reads, writes, is_dma, acc_writes=()):
        op = _Op()
        op.eng = eng
        op.fn = fn
        op.is_dma = is_dma
        op.signal = None
        op.need_signal = is_dma
        op.slot = None
        op.idx = idx = len(self.ops)
        key = ("dma", idx) if is_dma else eng
        deps = set()
        bi = self.barrier_idx
        for b in reads:
            for w in b.writers.values():
                if w >= bi:
                    deps.add(w)
            if b.excl:
                for r in b.readers.values():
                    if r >= bi:
                        deps.add(r)
        for b in writes:
            for r in b.readers.values():
                if r >= bi:
                    deps.add(r)
            for w in b.writers.values():
                if w >= bi:
                    deps.add(w)
            for r in b.war.values():
                if r >= bi:
                    deps.add(r)
        for b in acc_writes:
            for r in b.readers.values():
                if r >= bi:
                    deps.add(r)
            for r in b.war.values():
                if r >= bi:
                    deps.add(r)
        for b in reads:
            b.readers[key] = idx
        for b in writes:
            b.war = b.readers
            b.writers = {key: idx}
            b.readers = {}
        for b in acc_writes:
            if b.readers:
                b.war = b.readers
                b.writers = {key: idx}
                b.readers = {}
            else:
                b.writers[key] = idx
        if self.serial and idx > 0 and idx - 1 >= bi:
            deps.add(idx - 1)
        deps.discard(idx)
        op.deps = deps
        self.ops.append(op)
        if fn is not None:
            if is_dma:
                self.dma_since_barrier.append(idx)
            else:
                self.last_on_eng[eng] = idx
        return op

    def op(self, eng, fn, reads=(), writes=(), acc_writes=()):
        return self._record(eng, fn, reads, writes, False, acc_writes)

    def dma(self, eng, out, in_, reads=(), writes=(), acc_writes=()):
        e = self.engobj[eng]
        return self._record(eng, lambda: e.dma_start(out=out, in_=in_), reads, writes, True, acc_writes)

    def barrier(self, engs=None):
        deps = set(self.last_on_eng.values()) | set(self.dma_since_barrier)
        deps = {d for d in deps if d >= self.barrier_idx}
        for eng in (engs or self.ENGS):
            op = _Op()
            op.eng = eng
            op.fn = None
            op.is_dma = False
            op.signal = None
            op.need_signal = False
            op.slot = None
            op.idx = len(self.ops)
            op.deps = set(deps)
            self.ops.append(op)
        self.barrier_idx = len(self.ops)
        self.dma_since_barrier = []
        self.last_on_eng = {}

    def emit(self):
        ops = self.ops
        lo = self.emitted
        for op in ops[lo:]:
            if op.is_dma:
                s = self.ndma % self.NDMASEM
                if self.slot_prev[s] is not None:
                    op.deps.add(self.slot_prev[s])
                self.slot_prev[s] = op.idx
                op.slot = s
                self.ndma += 1
        for op in ops[lo:]:
            for d in op.deps:
                p = ops[d]
                if p.is_dma or op.is_dma or p.eng != op.eng or op.eng != "pe":
                    assert d >= lo or p.signal is not None or p.fn is None, "dep on already-emitted unsignalled op"
                    p.need_signal = True
        engsem, engcnt = self.engsem, self.engcnt
        for op in ops[lo:]:
            e = self.engobj[op.eng]
            need = {}
            for d in op.deps:
                p = ops[d]
                if p.signal is None:
                    continue
                if (not p.is_dma) and (not op.is_dma) and p.eng == op.eng and op.eng == "pe":
                    continue
                sem, val = p.signal
                k = id(sem)
                if k not in need or need[k][1] < val:
                    need[k] = (sem, val)
            wc = self.waited[op.eng]
            for k, (sem, val) in need.items():
                if wc.get(k, 0) >= val:
                    continue
                e.wait_ge(sem, val)
                wc[k] = val
                self.nwait += 1
            if op.fn is None:
                continue
            ins = op.fn()
            op.fn = True
            if op.need_signal:
                if op.is_dma:
                    key = ("dma", op.slot)
                    inc = 16
                else:
                    key = op.eng
                    inc = 1
                if key not in engsem or engcnt[key] + inc > SEM_LIMIT:
                    engsem[key] = self.new_sem("d" if op.is_dma else op.eng)
                    engcnt[key] = 0
                engcnt[key] += inc
                ins.then_inc(engsem[key], inc)
                op.signal = (engsem[key], engcnt[key])
                self.nsig += 1
        self.emitted = len(ops)
        return dict(nops=len(ops), nwait=self.nwait, nsig=self.nsig, nsem=self.nsem)


NCORES = 8
S = 8192
D = 2048
TA = 512
NT = S // TA
NSLOT = 8
TC = 256
EPS = 1e-6
SLOT_TILES = {0: [0, 3, 4, 7, 8, 11, 12, 15], 1: [1, 2, 5, 6, 9, 10, 13, 14]}
NEG = -30000.0
SB_SCALE = 128 ** -0.5
MLA_SCALE = 192 ** -0.5

WSPEC = [
    ("wq", 16, 1024, "gpm", []), ("wk", 16, 1024, "gpm", []), ("wv", 16, 1024, "gpm", []),
    ("wcq", 16, 512, "gpm", []), ("wckv", 16, 512, "gpm", []), ("wkr", 16, 128, "gpm", [(64, 96)]),
    ("wgs", 16, 2048, "gpm", []), ("wgm", 16, 2048, "gpm", []),
    ("wqn", 4, 1024, "gcq", []), ("wqr", 4, 1024, "gcq", [(h * 128 + 64, h * 128 + 96) for h in range(8)]),
    ("wkn", 4, 1024, "gckv", []), ("wvm", 4, 1024, "gckv", []),
    ("wsbo", 8, 2048, None, []), ("wmlao", 8, 2048, None, []), ("wout", 16, 2048, None, []),
    ("wup", 16, 8192, "gmlp", []), ("wdown", 64, 2048, None, []), ("wple", 2, 2048, None, []),
    ("wpg", 16, 2048, None, []),
]
GSPEC = {"gpm": 16, "gcq": 4, "gckv": 4, "gmlp": 16}


class Ring:
    def __init__(self, alloc, name, n, shape, dt, excl=False):
        self.items = [(alloc(f"{name}{i}", shape, dt), Buf(f"{name}{i}", excl)) for i in range(n)]
        self.i = 0

    def next(self):
        it = self.items[self.i % len(self.items)]
        self.i += 1
        return it


def build_program(cfg=None):
    cfg = dict(dict(nkt=NT, nslotA=NSLOT, hsb=8, hml=8, nslotB=NSLOT, ncs=2 * NSLOT, phases="0ABMC", dbg=()), **(cfg or {}))
    nc = bass.Bass("TRN2", target_bir_lowering=False)

    def din(name, shape, dt=F32):
        return nc.dram_tensor(name, shape, dt, kind="ExternalInput").ap()

    def dscr(name, shape, dt=BF16):
        if name in cfg["dbg"]:
            return nc.dram_tensor(name, shape, dt, kind="ExternalOutput").ap()
        return nc.dram_tensor(name, shape, dt).ap()

    x_all = din("x_all", [S, D])
    x_own = din("x_own", [S // 2, D])
    p_own = din("p_own", [S // 2, 256])
    pos_all = din("pos_all", [1, S], I32)
    pos_own = din("pos_own", [1, S // 2], I32)
    consts = din("consts", [128, 4 * 128])
    invf = din("invf", [64, 1])
    mask_sb = din("mask_sb", [128, 16 * 512])
    mask_ml = din("mask_ml", [128, 16 * 512])
    grow = din("grow", [3, D])
    gin = {g: din(g, [128, kc]) for g, kc in GSPEC.items()}
    win = {n: din(n + "_f", [128, kc, nn]) for n, kc, nn, _, _ in WSPEC}
    out = nc.dram_tensor("out", [S // 2, D], F32, kind="ExternalOutput").ap()

    wb = {n: dscr(n + "_b", [128, kc, nn]) for n, kc, nn, _, _ in WSPEC}
    Kscr = dscr("Kscr", [128, 8, S])
    Vscr = dscr("Vscr", [8, 128, 64, 128])
    KNscr = dscr("KNscr", [128, 8, S])
    VMscr = dscr("VMscr", [8, 128, 64, 128])
    KRscr = dscr("KRscr", [64, S])
    Qscr = dscr("Qscr", [128, 8, S // 2])
    QNscr = dscr("QNscr", [128, 8, S // 2])
    QRscr = dscr("QRscr", [64, 8, S // 2])
    OSscr = dscr("OSscr", [128, 8, S // 2])
    OMscr = dscr("OMscr", [128, 8, S // 2])
    dbgC = cfg.get("dbgC", False)
    if dbgC:
        d_mixed = nc.dram_tensor("d_mixed", [128, 16, TC], BF16, kind="ExternalOutput").ap()
        d_x1 = nc.dram_tensor("d_x1", [128, 2, D], F32, kind="ExternalOutput").ap()
        d_x2 = nc.dram_tensor("d_x2", [128, 2, D], F32, kind="ExternalOutput").ap()
        d_e = nc.dram_tensor("d_e", [128, 2, D], F32, kind="ExternalOutput").ap()
        d_y = nc.dram_tensor("d_y", [128, 2, D], F32, kind="ExternalOutput").ap()
        b_dbg = Buf("dbg")
    b_wb = {n: Buf(n) for n in wb}
    b_scr = {n: Buf(n) for n in ["K", "V", "KN", "VM", "KR", "Q", "QN", "QR", "OS", "OM", "out"]}

    _uid = [0]

    def uniq(n):
        _uid[0] += 1
        return f"{n}_{_uid[0]}"

    with ExitStack() as es:
        fw = FW(nc, es)
        fw.serial = bool(cfg.get('serial', False))
        V, A, P, T = nc.vector, nc.scalar, nc.gpsimd, nc.tensor

        galloc = lambda n, sh, dt: es.enter_context(nc.sbuf_tensor(n, sh, dt))
        cst = galloc("cst", [128, 512], BF16)
        ident, uincl, ubar, ones = cst[:, 0:128], cst[:, 128:256], cst[:, 256:384], cst[:, 384:512]
        gT = {g: galloc(g + "_t", [128, kc], F32) for g, kc in GSPEC.items()}
        gTn = {g: galloc(g + "_n", [128, GSPEC[g]], F32) for g in ("gpm", "gcq")}
        invf_t = galloc("invf_t", [64, 1], F32)
        b_cst = Buf("cst")
        with ExitStack() as ph:
            alloc = lambda n, sh, dt: ph.enter_context(nc.sbuf_tensor(uniq(n), sh, dt))
            cf = alloc("cf", [128, 512], F32)
            b_cf = Buf("cf")
            fw.dma("sp", cf[:], consts, writes=[b_cf])
            fw.op("dve", lambda: V.tensor_copy(cst[:], cf[:]), reads=[b_cf], writes=[b_cst])
            for g in GSPEC:
                fw.dma("sp", gT[g][:], gin[g], writes=[b_cst])
            fw.dma("sp", invf_t[:], invf, writes=[b_cst])
            fw.barrier()
            for g in gTn:
                fw.op("dve", lambda g=g: V.tensor_scalar(gTn[g][:], gT[g][:], -1.0, None, op0=ALU.mult),
                      reads=[b_cst], writes=[b_cst])

            stf = Ring(alloc, "stf", 2, [128, 4096], F32)
            stb = Ring(alloc, "stb", 2, [128, 4096], BF16)
            tog = 0
            for name, KC, N, gname, negs in (WSPEC if "0" in cfg["phases"] else []):
                if N >= 4096:
                    chunks = [(kc, 1, n0, min(4096, N - n0)) for kc in range(KC) for n0 in range(0, N, 4096)]
                else:
                    kcn = max(1, 4096 // N)
                    chunks = [(k0, min(kcn, KC - k0), 0, N) for k0 in range(0, KC, kcn)]
                for k0, kn, n0, nn in chunks:
                    (tf, bf), (tb, bb) = stf.next(), stb.next()
                    fv = tf[:, 0:kn * nn].rearrange("p (k n) -> p k n", k=kn)
                    bv = tb[:, 0:kn * nn].rearrange("p (k n) -> p k n", k=kn)
                    fw.dma("sp", fv, win[name][:, k0:k0 + kn, n0:n0 + nn], writes=[bf])
                    if gname is None:
                        if tog % 2 == 0:
                            fw.op("act", lambda tb=tb, tf=tf, m=kn * nn: A.copy(tb[:, 0:m], tf[:, 0:m]), reads=[bf], writes=[bb])
                        else:
                            fw.op("dve", lambda tb=tb, tf=tf, m=kn * nn: V.tensor_copy(tb[:, 0:m], tf[:, 0:m]), reads=[bf], writes=[bb])
                        tog += 1
                    else:
                        first = True
                        for kk in range(kn):
                            kc = k0 + kk
                            segs, c = [], 0
                            for lo, hi in negs:
                                if lo > c:
                                    segs.append((c, lo, 1))
                                segs.append((lo, hi, -1))
                                c = hi
                            if c < nn:
                                segs.append((c, nn, 1))
                            for lo, hi, sg in segs:
                                sc = (gT if sg > 0 else gTn)[gname][:, kc:kc + 1]
                                o_ap, i_ap = tb[:, kk * nn + lo:kk * nn + hi], tf[:, kk * nn + lo:kk * nn + hi]
                                kw = {"writes": [bb]} if first else {"acc_writes": [bb]}
                                first = False
                                if tog % 2 == 0:
                                    fw.op("act", lambda o=o_ap, i=i_ap, sc=sc: A.activation(o, i, AF.Copy, scale=sc), reads=[bf, b_cst], **kw)
                                else:
                                    fw.op("dve", lambda o=o_ap, i=i_ap, sc=sc: V.tensor_scalar(o, i, sc, None, op0=ALU.mult), reads=[bf, b_cst], **kw)
                                tog += 1
                    fw.dma("pool", wb[name][:, k0:k0 + kn, n0:n0 + nn], bv, reads=[bb], acc_writes=[b_wb[name]])
            fw.barrier()
            fw.emit()

        def front_end(alloc_ctx, xsrc, nsub, hT, hbufs, xdst=None, norm=True):
            c = alloc_ctx
            for st in range(nsub):
                if xdst is None:
                    xt, bx = c["xs"].next()
                    xap = xt[:]
                else:
                    xap, bx = xdst[st]
                if xsrc is not None:
                    fw.dma("sp", xap, xsrc(st), writes=[bx])
                xn, bxn = c["xn"].next()
                if not norm:
                    fw.op("dve", lambda xn=xn, xap=xap: V.tensor_copy(xn[:], xap), reads=[bx], writes=[bxn])
                ss, bss = c["ss"].next()
                jk, bjk = c["junk"].next()
                if norm:
                  fw.op("dve", lambda ss=ss: V.memset(ss[:], 0.0), writes=[bss])
                  fw.op("act", lambda jk=jk, xap=xap, ss=ss: A.activation(jk[:], xap, AF.Square, accum_out=ss[:]),
                      reads=[bx, bss], writes=[bjk, bss])
                  fw.op("act", lambda ss=ss: A.activation(ss[:], ss[:], AF.Sqrt, bias=EPS, scale=1.0 / D),
                      reads=[bss], writes=[bss])
                  fw.op("dve", lambda ss=ss: V.reciprocal(ss[:], ss[:]),
                      reads=[bss], writes=[bss])
                  fw.op("dve", lambda xn=xn, xap=xap, ss=ss: V.tensor_scalar(xn[:], xap, ss[:, 0:1], None, op0=ALU.mult),
                      reads=[bx, bss], writes=[bxn])
                for half in range(2):
                    tp, btp = c["tp"].next()
                    for q in range(8):
                        kc = half * 8 + q
                        fw.op("pe", lambda tp=tp, xn=xn, q=q, kc=kc: T.transpose(tp[:, q * 128:(q + 1) * 128], xn[:, kc * 128:(kc + 1) * 128], ident),
                              reads=[bxn, b_cst], **({"writes": [btp]} if q == 0 else {"acc_writes": [btp]}))
                    dst = hT[:, half * 8:half * 8 + 8, st * 128:(st + 1) * 128]
                    src = tp[:].rearrange("p (k t) -> p k t", k=8)
                    kw = {"writes": [hbufs[st]]} if half == 0 else {"acc_writes": [hbufs[st]]}
                    if half == 0:
                        fw.op("act", lambda dst=dst, src=src: A.copy(dst, src), reads=[btp], **kw)
                    else:
                        fw.op("dve", lambda dst=dst, src=src: V.tensor_copy(dst, src), reads=[btp], **kw)

        def rope_tables(c, pos_src, cos2, sin2, b_cs):
            TWO_PI = float(2 * np.pi)
            C1 = 6.28125
            C2 = TWO_PI - C1
            pi_t, bpi = c["posi"].next()
            fw.dma("sp", pi_t[:], pos_src.partition_broadcast(64), writes=[bpi])
            ang, bang = c["ang"].next()
            y, by = c["ang"].next()
            r, br = c["ang"].next()
            m, bm = c["ang"].next()
            ni, bni = c["posi"].next()
            fw.op("dve", lambda: V.tensor_copy(ang[:], pi_t[:]), reads=[bpi], writes=[bang])
            fw.op("dve", lambda: V.tensor_scalar(ang[:], ang[:], invf_t[:, 0:1], None, op0=ALU.mult), reads=[bang, b_cst], writes=[bang])
            fw.op("dve", lambda: V.tensor_scalar(y[:], ang[:], 1.0 / TWO_PI, 0.5, op0=ALU.mult, op1=ALU.add), reads=[bang], writes=[by])
            fw.op("dve", lambda: V.tensor_copy(ni[:], y[:]), reads=[by], writes=[bni])
            fw.op("dve", lambda: V.tensor_copy(y[:], ni[:]), reads=[bni], writes=[by])
            fw.op("dve", lambda: V.scalar_tensor_tensor(r[:], y[:], -C1, ang[:], op0=ALU.mult, op1=ALU.add), reads=[by, bang], writes=[br])
            fw.op("dve", lambda: V.scalar_tensor_tensor(r[:], y[:], -C2, r[:], op0=ALU.mult, op1=ALU.add), reads=[by, br], writes=[br])

            def wrap(t, bt):
                fw.op("dve", lambda: V.tensor_scalar(m[:], t[:], float(-np.pi), None, op0=ALU.is_lt), reads=[bt], writes=[bm])
                fw.op("dve", lambda: V.scalar_tensor_tensor(t[:], m[:], TWO_PI, t[:], op0=ALU.mult, op1=ALU.add), reads=[bm, bt], writes=[bt])
                fw.op("dve", lambda: V.tensor_scalar(m[:], t[:], float(np.pi), None, op0=ALU.is_gt), reads=[bt], writes=[bm])
                fw.op("dve", lambda: V.scalar_tensor_tensor(t[:], m[:], -TWO_PI, t[:], op0=ALU.mult, op1=ALU.add), reads=[bm, bt], writes=[bt])
                fw.op("dve", lambda: V.tensor_scalar(t[:], t[:], float(-np.pi), float(np.pi), op0=ALU.max, op1=ALU.min), reads=[bt], writes=[bt])

            wrap(r, br)
            fw.op("act", lambda: A.activation(sin2[:], r[:], AF.Sin), reads=[br], writes=[b_cs])
            fw.op("dve", lambda: V.tensor_scalar(y[:], r[:], float(np.pi / 2), None, op0=ALU.add), reads=[br], writes=[by])
            wrap(y, by)
            fw.op("act", lambda: A.activation(cos2[:], y[:], AF.Sin), reads=[by], writes=[b_cs])

        def featmajor_norm(c, ps_list, bps_list, outT, b_out, psb, bpsb):
            cf_t, bcf = c["cfm"].next()
            sq_t, bsq = c["sq"].next()
            for i in range(4):
                fw.op("dve", lambda i=i: V.tensor_copy(cf_t[:, i * 512:(i + 1) * 512], ps_list[i][:]), reads=[bps_list[i]],
                      **({"writes": [bcf]} if i == 0 else {"acc_writes": [bcf]}))
                fw.op("act", lambda i=i: A.activation(sq_t[:, i * 512:(i + 1) * 512], ps_list[i][:], AF.Square), reads=[bps_list[i]],
                      **({"writes": [bsq]} if i == 0 else {"acc_writes": [bsq]}))
            for i in range(4):
                fw.op("pe", lambda i=i: T.matmul(psb[:], ones, sq_t[:, i * 512:(i + 1) * 512], start=(i == 0), stop=(i == 3)),
                      reads=[bsq, b_cst], **({"writes": [bpsb]} if i == 0 else {"acc_writes": [bpsb]}))
            rs, brs = c["rsb"].next()
            fw.op("act", lambda: A.activation(rs[:], psb[:], AF.Sqrt, bias=EPS, scale=1.0 / 512), reads=[bpsb], writes=[brs])
            fw.op("dve", lambda: V.reciprocal(rs[:], rs[:]), reads=[brs], writes=[brs])
            for i in range(4):
                fw.op("dve", lambda i=i: V.tensor_tensor(outT[:, i, :], cf_t[:, i * 512:(i + 1) * 512], rs[:], ALU.mult), reads=[bcf, brs],
                      **({"writes": [b_out]} if i == 0 else {"acc_writes": [b_out]}))

        with ExitStack() as ph:
            alloc = lambda n, sh, dt: ph.enter_context(nc.sbuf_tensor(uniq(n), sh, dt))
            palloc = lambda n, sh, dt: ph.enter_context(nc.psum_tensor(uniq(n), sh, dt))
            c = {
                "xs": Ring(alloc, "xs", 2, [128, D], F32), "ss": Ring(alloc, "ss", 4, [128, 1], F32),
                "junk": Ring(alloc, "junk", 1, [128, D], BF16), "xn": Ring(alloc, "xn", 2, [128, D], BF16),
                "tp": Ring(palloc, "tp", 2, [128, 1024], BF16, excl=True),
                "posi": Ring(alloc, "posi", 2, [64, 512], I32), "ang": Ring(alloc, "ang", 4, [64, 512], F32),
                "cfm": Ring(alloc, "cfm", 1, [128, 2048], F32), "sq": Ring(alloc, "sq", 1, [128, 2048], BF16),
                "rsb": Ring(alloc, "rsb", 1, [128, 512], F32),
            }
            hTr = [(alloc(f"hT{i}", [128, 16, 512], BF16), [Buf(f"hT{i}_{s}") for s in range(4)]) for i in range(2)]
            wring = Ring(alloc, "wr", 2, [128, 16 * 512], BF16)
            psr = Ring(palloc, "ps", 6, [128, 512], F32, excl=True)
            wsm = {n: alloc("w_" + n, [128, 4 * 1024], BF16) for n in ("wkn", "wvm", "wqn", "wqr")}
            wkr_t = alloc("w_wkr", [128, 16 * 128], BF16)
            b_wsm = Buf("wsm")
            for n in wsm:
                fw.dma("sp", wsm[n][:].rearrange("p (k n) -> p k n", k=4), wb[n], reads=[b_wb[n]], acc_writes=[b_wsm])
            fw.dma("sp", wkr_t[:].rearrange("p (k n) -> p k n", k=16), wb["wkr"], reads=[b_wb["wkr"]], acc_writes=[b_wsm])
            wsmv = {n: wsm[n][:].rearrange("p (k n) -> p k n", k=4) for n in wsm}
            wkrv = wkr_t[:].rearrange("p (k n) -> p k n", k=16)
            cos2, sin2 = alloc("cos2", [64, 512], F32), alloc("sin2", [64, 512], F32)
            b_cs = Buf("cs")
            stg8 = Ring(alloc, "stg8", 2, [128, 8 * 512], BF16)
            stgv = Ring(alloc, "stgv", 2, [128, 1024], BF16)
            cnT = alloc("cnT", [128, 4, 512], BF16)
            b_cn = Buf("cnT")
            rt = Ring(alloc, "rt", 2, [64, 512], F32)
            stgr = Ring(alloc, "stgr", 2, [64, 8 * 512], BF16)
            evt = [0]

            def evac(dst, src, bsrc, kw):
                evt[0] += 1
                if evt[0] % 2:
                    fw.op("act", lambda: A.copy(dst, src), reads=[bsrc], **kw)
                else:
                    fw.op("dve", lambda: V.tensor_copy(dst, src), reads=[bsrc], **kw)

            def load_w(name, kc_n, col0, ncols):
                wt, bw = wring.next()
                v = wt[:, 0:kc_n * ncols].rearrange("p (k n) -> p k n", k=kc_n)
                fw.dma("sp", v, wb[name][:, :, col0:col0 + ncols], reads=[b_wb[name]], writes=[bw])
                return v, bw

            def proj_feat(wv, bw, wcols, KCn, rhsT, rbufs, M=128):
                pt, bp = psr.next()
                for kc in range(KCn):
                    fw.op("pe", lambda kc=kc: T.matmul(pt[0:M, :], wv[:, kc, wcols[0]:wcols[1]], rhsT[:, kc, :], start=(kc == 0), stop=(kc == KCn - 1)),
                          reads=[bw] + rbufs, **({"writes": [bp]} if kc == 0 else {"acc_writes": [bp]}))
                return pt, bp

            def proj_tok(wv, bw, wcols, KCn, lhsT, lbufs, st):
                pt, bp = psr.next()
                for kc in range(KCn):
                    fw.op("pe", lambda kc=kc: T.matmul(pt[:], lhsT[:, kc, st * 128:(st + 1) * 128], wv[:, kc, wcols[0]:wcols[1]], start=(kc == 0), stop=(kc == KCn - 1)),
                          reads=[bw] + lbufs, **({"writes": [bp]} if kc == 0 else {"acc_writes": [bp]}))
                return pt, bp

            def rope_combine(pa, bpa, pb, bpb, dst, kw):
                t1, b1 = rt.next()
                t2, b2 = rt.next()
                fw.op("dve", lambda: V.tensor_tensor(t1[:], pa[0:64, :], cos2[:], ALU.mult), reads=[bpa, b_cs], writes=[b1])
                fw.op("dve", lambda: V.tensor_tensor(t2[:], pb[0:64, :], sin2[:], ALU.mult), reads=[bpb, b_cs], writes=[b2])
                fw.op("dve", lambda: V.tensor_tensor(dst, t1[:], t2[:], ALU.add), reads=[b1, b2], **kw)

            for kt in range(cfg["nkt"] if "A" in cfg["phases"] else 0):
                hT, hb = hTr[kt % 2]
                front_end(c, lambda st, kt=kt: x_all[kt * 512 + st * 128: kt * 512 + (st + 1) * 128, :], 4, hT, hb)
                if cfg.get('astop', 99) < 2:
                    continue
                rope_tables(c, pos_all[:, kt * 512:(kt + 1) * 512], cos2, sin2, b_cs)
                if cfg.get('astop', 99) < 3:
                    continue
                sk, bsk = stg8.next()
                for cg in range(2):
                    wv, bw = load_w("wk", 16, cg * 512, 512)
                    for hh in range(4):
                        pt, bp = proj_feat(wv, bw, (hh * 128, hh * 128 + 128), 16, hT, hb)
                        h = cg * 4 + hh
                        evac(sk[:, h * 512:(h + 1) * 512], pt[:], bp, {"writes": [bsk]} if h == 0 else {"acc_writes": [bsk]})
                fw.dma("pool", Kscr[:, :, kt * 512:(kt + 1) * 512], sk[:].rearrange("p (h t) -> p h t", h=8), reads=[bsk], acc_writes=[b_scr["K"]])
                if cfg.get('astop', 99) < 4:
                    continue
                wvs = [load_w("wv", 16, cg * 512, 512) for cg in range(2)]
                for st in range(4):
                    sv, bsv = stgv.next()
                    for cg in range(2):
                        pt, bp = proj_tok(wvs[cg][0], wvs[cg][1], (0, 512), 16, hT, [hb[st]], st)
                        evac(sv[:, cg * 512:(cg + 1) * 512], pt[:], bp, {"writes": [bsv]} if cg == 0 else {"acc_writes": [bsv]})
                    fw.dma("pool", Vscr.rearrange("h p k d -> p h k d")[:, :, kt * 4 + st, :], sv[:].rearrange("p (h d) -> p h d", h=8),
                           reads=[bsv], acc_writes=[b_scr["V"]])
                if cfg.get('astop', 99) < 5:
                    continue
                wv, bw = load_w("wckv", 16, 0, 512)
                pl = [proj_feat(wv, bw, (i * 128, i * 128 + 128), 16, hT, hb) for i in range(4)]
                psb, bpsb = psr.next()
                featmajor_norm(c, [p[0] for p in pl], [p[1] for p in pl], cnT, b_cn, psb, bpsb)
                if cfg.get('astop', 99) < 6:
                    continue
                pa, bpa = proj_feat(wkrv, b_wsm, (0, 64), 16, hT, hb, M=64)
                pb, bpb = proj_feat(wkrv, b_wsm, (64, 128), 16, hT, hb, M=64)
                sr, bsr = stgr.next()
                rope_combine(pa, bpa, pb, bpb, sr[:, 0:512], {"writes": [bsr]})
                fw.dma("pool", KRscr[:, kt * 512:(kt + 1) * 512], sr[:, 0:512], reads=[bsr], acc_writes=[b_scr["KR"]])
                if cfg.get('astop', 99) < 7:
                    continue
                sk, bsk = stg8.next()
                for h in range(8):
                    pt, bp = proj_feat(wsmv["wkn"], b_wsm, (h * 128, h * 128 + 128), 4, cnT, [b_cn])
                    evac(sk[:, h * 512:(h + 1) * 512], pt[:], bp, {"writes": [bsk]} if h == 0 else {"acc_writes": [bsk]})
                fw.dma("pool", KNscr[:, :, kt * 512:(kt + 1) * 512], sk[:].rearrange("p (h t) -> p h t", h=8), reads=[bsk], acc_writes=[b_scr["KN"]])
                if cfg.get('astop', 99) < 8:
                    continue
                for st in range(4):
                    sv, bsv = stgv.next()
                    for cg in range(2):
                        pt, bp = proj_tok(wsmv["wvm"], b_wsm, (cg * 512, cg * 512 + 512), 4, cnT, [b_cn], st)
                        evac(sv[:, cg * 512:(cg + 1) * 512], pt[:], bp, {"writes": [bsv]} if cg == 0 else {"acc_writes": [bsv]})
                    fw.dma("pool", VMscr.rearrange("h p k d -> p h k d")[:, :, kt * 4 + st, :], sv[:].rearrange("p (h d) -> p h d", h=8),
                           reads=[bsv], acc_writes=[b_scr["VM"]])
            for k in range(cfg["nslotA"] if "A" in cfg["phases"] else 0):
                hT, hb = hTr[k % 2]
                front_end(c, lambda st, k=k: x_own[k * 512 + st * 128: k * 512 + (st + 1) * 128, :], 4, hT, hb)
                rope_tables(c, pos_own[:, k * 512:(k + 1) * 512], cos2, sin2, b_cs)
                sk, bsk = stg8.next()
                for cg in range(2):
                    wv, bw = load_w("wq", 16, cg * 512, 512)
                    for hh in range(4):
                        pt, bp = proj_feat(wv, bw, (hh * 128, hh * 128 + 128), 16, hT, hb)
                        h = cg * 4 + hh
                        evac(sk[:, h * 512:(h + 1) * 512], pt[:], bp, {"writes": [bsk]} if h == 0 else {"acc_writes": [bsk]})
                fw.dma("pool", Qscr[:, :, k * 512:(k + 1) * 512], sk[:].rearrange("p (h t) -> p h t", h=8), reads=[bsk], acc_writes=[b_scr["Q"]])
                wv, bw = load_w("wcq", 16, 0, 512)
                pl = [proj_feat(wv, bw, (i * 128, i * 128 + 128), 16, hT, hb) for i in range(4)]
                psb, bpsb = psr.next()
                featmajor_norm(c, [p[0] for p in pl], [p[1] for p in pl], cnT, b_cn, psb, bpsb)
                sk, bsk = stg8.next()
                for h in range(8):
                    pt, bp = proj_feat(wsmv["wqn"], b_wsm, (h * 128, h * 128 + 128), 4, cnT, [b_cn])
                    evac(sk[:, h * 512:(h + 1) * 512], pt[:], bp, {"writes": [bsk]} if h == 0 else {"acc_writes": [bsk]})
                fw.dma("pool", QNscr[:, :, k * 512:(k + 1) * 512], sk[:].rearrange("p (h t) -> p h t", h=8), reads=[bsk], acc_writes=[b_scr["QN"]])
                sr, bsr = stgr.next()
                for h in range(8):
                    pa, bpa = proj_feat(wsmv["wqr"], b_wsm, (h * 128, h * 128 + 64), 4, cnT, [b_cn], M=64)
                    pb, bpb = proj_feat(wsmv["wqr"], b_wsm, (h * 128 + 64, h * 128 + 128), 4, cnT, [b_cn], M=64)
                    rope_combine(pa, bpa, pb, bpb, sr[:, h * 512:(h + 1) * 512], {"writes": [bsr]} if h == 0 else {"acc_writes": [bsr]})
                fw.dma("pool", QRscr[:, :, k * 512:(k + 1) * 512], sr[:].rearrange("p (h t) -> p h t", h=8), reads=[bsr], acc_writes=[b_scr["QR"]])
            fw.barrier()
            fw.emit()

        def load_masks(alloc, src, dstname):
            mt = alloc(dstname, [128, 16 * 512], BF16)
            bm = Buf(dstname)
            stg = Ring(alloc, dstname + "s", 2, [128, 2048], F32)
            for i in range(4):
                t, b = stg.next()
                fw.dma("sp", t[:], src[:, i * 2048:(i + 1) * 2048], writes=[b])
                fw.op("dve", lambda t=t, i=i: V.tensor_copy(mt[:, i * 2048:(i + 1) * 2048], t[:]), reads=[b],
                      **({"writes": [bm]} if i == 0 else {"acc_writes": [bm]}))
            return mt, bm

        with ExitStack() as ph:
            alloc = lambda n, sh, dt: ph.enter_context(nc.sbuf_tensor(uniq(n), sh, dt))
            palloc = lambda n, sh, dt: ph.enter_context(nc.psum_tensor(uniq(n), sh, dt))
            msk, bmsk = load_masks(alloc, mask_sb, "msb")
            KTr = Ring(alloc, "KT", 2, [128, S], BF16)
            Vr = Ring(alloc, "Vt", 2, [128, 64 * 128], BF16)
            QTr = Ring(alloc, "QT", 2, [128, S // 2], BF16)
            zr = Ring(palloc, "zps", 2, [128, 512], F32, excl=True)
            Rp, bR = palloc("Rps", [128, 512], F32), Buf("R", True)
            Op_, bO = palloc("Ops", [128, 512], F32), Buf("O", True)
            Er = Ring(alloc, "Ef", 3, [128, 512], F32)
            Lr = Ring(alloc, "Lb", 3, [128, 512], BF16)
            Gr = Ring(alloc, "Gf", 2, [128, 512], F32)
            Wr = Ring(alloc, "wbt", 3, [128, 512], BF16)
            osr = Ring(alloc, "ost", 2, [128, 512], BF16)
            for h in range(cfg["hsb"] if "B" in cfg["phases"] else 0):
                (KT, bK), (Vt, bV), (QT, bQ) = KTr.next(), Vr.next(), QTr.next()
                fw.dma("sp", KT[:], Kscr[:, h, :], reads=[b_scr["K"]], writes=[bK])
                fw.dma("sp", QT[:], Qscr[:, h, :], reads=[b_scr["Q"]], writes=[bQ])
                for q4 in range(4):
                    fw.dma("sp", Vt[:, q4 * 2048:(q4 + 1) * 2048].rearrange("p (k d) -> p k d", k=16), Vscr[h, :, q4 * 16:(q4 + 1) * 16, :],
                           reads=[b_scr["V"]], **({"writes": [bV]} if q4 == 0 else {"acc_writes": [bV]}))
                for k in range(cfg["nslotB"]):
                    nb = 8 * (k + 1)
                    par = k % 2
                    st_ = {}

                    def stA(i, k=k, nb=nb, par=par, st_=st_, KT=KT, QT=QT, bK=bK, bQ=bQ):
                        kb = nb - 1 - i
                        zp, bz = zr.next()
                        bnd = kb >= 8 * k
                        fw.op("pe", lambda: T.matmul(zp[:], KT[:, kb * 128:(kb + 1) * 128], QT[:, k * 512:(k + 1) * 512], start=True, stop=not bnd),
                              reads=[bK, bQ], writes=[bz])
                        if bnd:
                            r = kb - 8 * k
                            m0 = (par * 8 + r) * 512
                            fw.op("pe", lambda: T.matmul(zp[:], ident, msk[:, m0:m0 + 512], start=False, stop=True),
                                  reads=[bmsk, b_cst], acc_writes=[bz])
                        st_[("z", i)] = (zp, bz)

                    def stB(i, st_=st_):
                        zp, bz = st_.pop(("z", i))
                        (Ef, bE), (Lb, bL) = Er.next(), Lr.next()
                        fw.op("act", lambda: A.activation(Ef[:], zp[:], AF.Exp, scale=SB_SCALE), reads=[bz], writes=[bE])
                        fw.op("act", lambda: A.activation(Lb[:], Ef[:], AF.Ln, bias=1.0, scale=1.0), reads=[bE], writes=[bL])
                        st_[("E", i)] = (Ef, bE)
                        st_[("L", i)] = (Lb, bL)

                    def stC(i, st_=st_):
                        Lb, bL = st_[("L", i)]
                        fw.op("pe", lambda: T.matmul(Rp[:], uincl, Lb[:], start=(i == 0), stop=True, skip_group_check=True),
                              reads=[bL, b_cst], writes=[bR])

                    def stD(i, st_=st_):
                        Gf, bG = Gr.next()
                        fw.op("act", lambda: A.activation(Gf[:], Rp[:], AF.Exp, scale=-1.0), reads=[bR], writes=[bG])
                        st_[("G", i)] = (Gf, bG)

                    def stE(i, st_=st_):
                        (Ef, bE), (Gf, bG) = st_.pop(("E", i)), st_.pop(("G", i))
                        wt_, bw_ = Wr.next()
                        fw.op("dve", lambda: V.tensor_tensor(wt_[:], Ef[:], Gf[:], ALU.mult), reads=[bE, bG], writes=[bw_])
                        st_[("w", i)] = (wt_, bw_)

                    def stF(i, nb=nb, st_=st_, Vt=Vt, bV=bV):
                        kb = nb - 1 - i
                        Lb, bL = st_.pop(("L", i))
                        wt_, bw_ = st_.pop(("w", i))
                        if i < nb - 1:
                            fw.op("pe", lambda: T.matmul(Rp[:], ubar, Lb[:], start=False, stop=True, skip_group_check=True),
                                  reads=[bL, b_cst], writes=[bR])
                        fw.op("pe", lambda: T.matmul(Op_[:], Vt[:, kb * 128:(kb + 1) * 128], wt_[:], start=(i == 0), stop=(i == nb - 1), skip_group_check=True),
                              reads=[bV, bw_], **({"writes": [bO]} if i == 0 else {"acc_writes": [bO]}))

                    for t in range(nb + 2):
                        if t < nb:
                            stA(t)
                            stB(t)
                        if 0 <= t - 2 < nb:
                            stF(t - 2)
                        if 0 <= t - 1 < nb:
                            stC(t - 1)
                            stD(t - 1)
                            stE(t - 1)
                    ot, bo = osr.next()
                    fw.op("dve", lambda ot=ot: V.tensor_copy(ot[:], Op_[:]), reads=[bO], writes=[bo])
                    fw.dma("pool", OSscr[:, h, k * 512:(k + 1) * 512], ot[:], reads=[bo], acc_writes=[b_scr["OS"]])
            fw.barrier()
            fw.emit()

        with ExitStack() as ph:
            alloc = lambda n, sh, dt: ph.enter_context(nc.sbuf_tensor(uniq(n), sh, dt))
            palloc = lambda n, sh, dt: ph.enter_context(nc.psum_tensor(uniq(n), sh, dt))
            msk, bmsk = load_masks(alloc, mask_ml, "mml")
            KNr = Ring(alloc, "KN", 2, [128, S], BF16)
            VMr = Ring(alloc, "VMt", 2, [128, 64 * 128], BF16)
            QNr = Ring(alloc, "QN", 2, [128, S // 2], BF16)
            QRr = Ring(alloc, "QR", 2, [64, S // 2], BF16)
            KRt, bKR = alloc("KRt", [64, S], BF16), Buf("KRt")
            fw.dma("sp", KRt[:], KRscr, reads=[b_scr["KR"]], writes=[bKR])
            sr_ = Ring(palloc, "sps", 3, [128, 512], F32, excl=True)
            Op_, bO = palloc("Omps", [128, 512], F32), Buf("Om", True)
            Dp, bD = palloc("Dps", [128, 512], F32), Buf("Dm", True)
            Pr = Ring(alloc, "Pb", 4, [128, 512], BF16)
            rdr = Ring(alloc, "rd", 2, [128, 512], F32)
            osr = Ring(alloc, "omt", 2, [128, 512], BF16)
            for h in range(cfg["hml"] if "M" in cfg["phases"] else 0):
                (KN, bK), (Vt, bV), (QN, bQ), (QR, bQR) = KNr.next(), VMr.next(), QNr.next(), QRr.next()
                fw.dma("sp", KN[:], KNscr[:, h, :], reads=[b_scr["KN"]], writes=[bK])
                fw.dma("sp", QN[:], QNscr[:, h, :], reads=[b_scr["QN"]], writes=[bQ])
                fw.dma("sp", QR[:], QRscr[:, h, :], reads=[b_scr["QR"]], writes=[bQR])
                for q4 in range(4):
                    fw.dma("sp", Vt[:, q4 * 2048:(q4 + 1) * 2048].rearrange("p (k d) -> p k d", k=16), VMscr[h, :, q4 * 16:(q4 + 1) * 16, :],
                           reads=[b_scr["VM"]], **({"writes": [bV]} if q4 == 0 else {"acc_writes": [bV]}))
                for k in range(cfg["nslotB"]):
                    nb = 8 * (k + 1)
                    par = k % 2
                    st_ = {}

                    def mA(i, k=k, par=par, st_=st_, KN=KN, QN=QN, QR=QR, bK=bK, bQ=bQ, bQR=bQR):
                        kb = i
                        sp_, bs = sr_.next()
                        bnd = kb >= 8 * k
                        fw.op("pe", lambda: T.matmul(sp_[:], KN[:, kb * 128:(kb + 1) * 128], QN[:, k * 512:(k + 1) * 512], start=True, stop=False),
                              reads=[bK, bQ], writes=[bs])
                        fw.op("pe", lambda: T.matmul(sp_[:], KRt[:, kb * 128:(kb + 1) * 128], QR[:, k * 512:(k + 1) * 512], start=False, stop=not bnd),
                              reads=[bKR, bQR], acc_writes=[bs])
                        if bnd:
                            r = kb - 8 * k
                            m0 = (par * 8 + r) * 512
                            fw.op("pe", lambda: T.matmul(sp_[:], ident, msk[:, m0:m0 + 512], start=False, stop=True),
                                  reads=[bmsk, b_cst], acc_writes=[bs])
                        st_[("s", i)] = (sp_, bs)

                    def mB(i, st_=st_):
                        sp_, bs = st_.pop(("s", i))
                        Pb, bP = Pr.next()
                        fw.op("act", lambda: A.activation(Pb[:], sp_[:], AF.Exp, scale=MLA_SCALE), reads=[bs], writes=[bP])
                        st_[("P", i)] = (Pb, bP)

                    def mC(i, nb=nb, st_=st_, Vt=Vt, bV=bV):
                        kb = i
                        Pb, bP = st_.pop(("P", i))
                        kw = {"writes": [bO]} if i == 0 else {"acc_writes": [bO]}
                        fw.op("pe", lambda: T.matmul(Op_[:], Vt[:, kb * 128:(kb + 1) * 128], Pb[:], start=(i == 0), stop=(i == nb - 1)),
                              reads=[bV, bP], **kw)
                        kw = {"writes": [bD]} if i == 0 else {"acc_writes": [bD]}
                        fw.op("pe", lambda: T.matmul(Dp[:], ones, Pb[:], start=(i == 0), stop=(i == nb - 1)),
                              reads=[bP, b_cst], **kw)

                    for t in range(nb + 2):
                        if t < nb:
                            mA(t)
                        if 0 <= t - 1 < nb:
                            mB(t - 1)
                        if 0 <= t - 2 < nb:
                            mC(t - 2)
                    rd, brd = rdr.next()
                    ot, bo = osr.next()
                    fw.op("dve", lambda rd=rd: V.reciprocal(rd[:], Dp[:]), reads=[bD], writes=[brd])
                    fw.op("dve", lambda rd=rd, ot=ot: V.tensor_tensor(ot[:], Op_[:], rd[:], ALU.mult), reads=[bO, brd], writes=[bo])
                    fw.dma("pool", OMscr[:, h, k * 512:(k + 1) * 512], ot[:], reads=[bo], acc_writes=[b_scr["OM"]])
            fw.barrier()
            fw.emit()

        with ExitStack() as ph:
            alloc = lambda n, sh, dt: ph.enter_context(nc.sbuf_tensor(uniq(n), sh, dt))
            palloc = lambda n, sh, dt: ph.enter_context(nc.psum_tensor(uniq(n), sh, dt))
            c = {
                "ss": Ring(alloc, "ss", 4, [128, 1], F32),
                "junk": Ring(alloc, "junk", 1, [128, D], BF16), "xn": Ring(alloc, "xn", 2, [128, D], BF16),
                "tp": Ring(palloc, "tp", 2, [128, 1024], BF16, excl=True),
            }
            xres = alloc("xres", [128, 2, D], F32)
            bxr = [Buf("xr0"), Buf("xr1")]
            ysb = alloc("ysb", [128, 2, D], F32)
            bys = [Buf("ys0"), Buf("ys1")]
            ysbf = ysb[:].rearrange("p s d -> p (s d)")
            uT = alloc("uT", [128, 64, TC], BF16)
            buT = Buf("uT")
            actT = [(alloc(f"aT{i}", [128, 16, TC], BF16), [Buf(f"aT{i}_{s}") for s in range(2)]) for i in range(2)]
            gbr = Ring(alloc, "gb", 2, [128, D], F32)
            wring = Ring(alloc, "wr", 3, [128, 16 * 512], BF16)
            osT, bosT = alloc("osT", [128, 8, TC], BF16), Buf("osT")
            omT, bomT = alloc("omT", [128, 8, TC], BF16), Buf("omT")
            psr = Ring(palloc, "ps", 6, [128, 512], F32, excl=True)
            sgr = Ring(alloc, "sg", 4, [128, 512], F32)
            pf, bpf = alloc("pf", [128, 2, 256], F32), Buf("pf")
            pbf, bpbf = alloc("pbf", [128, 2, 256], BF16), Buf("pbf")
            pT, bpT = alloc("pT", [128, 2, TC], BF16), Buf("pT")
            evt = [0]

            def evac(dst, src, bsrc, kw, reads=()):
                evt[0] += 1
                if evt[0] % 2:
                    fw.op("act", lambda: A.copy(dst, src), reads=[bsrc] + list(reads), **kw)
                else:
                    fw.op("dve", lambda: V.tensor_copy(dst, src), reads=[bsrc] + list(reads), **kw)

            def load_w(name, k0, kn, col0, ncols):
                wt, bw = wring.next()
                v = wt[:, 0:kn * ncols].rearrange("p (k n) -> p k n", k=kn)
                fw.dma("sp", v, wb[name][:, k0:k0 + kn, col0:col0 + ncols], reads=[b_wb[name]], writes=[bw])
                return v, bw

            def load_g(i):
                gt, bg = gbr.next()
                fw.dma("sp", gt[:], grow[i:i + 1, :].partition_broadcast(128), writes=[bg])
                return gt, bg

            def proj_feat(wv, bw, wcols, KCn, rhsT, rbufs):
                pt, bp = psr.next()
                for kc in range(KCn):
                    fw.op("pe", lambda kc=kc: T.matmul(pt[:, 0:TC], wv[:, kc, wcols[0]:wcols[1]], rhsT[:, kc, :], start=(kc == 0), stop=(kc == KCn - 1)),
                          reads=[bw] + rbufs, **({"writes": [bp]} if kc == 0 else {"acc_writes": [bp]}))
                return pt, bp

            def tok_mm(pt, bp, lhsT, lbufs, st, wv, bw, kcs, first, last):
                for j, (kc_l, kc_w) in enumerate(kcs):
                    fw.op("pe", lambda kc_l=kc_l, kc_w=kc_w, j=j: T.matmul(pt[:], lhsT[:, kc_l, st * 128:(st + 1) * 128], wv[:, kc_w, :],
                                                                  start=(first and j == 0), stop=(last and j == len(kcs) - 1), skip_group_check=True),
                          reads=[bw] + lbufs, **({"writes": [bp]} if (first and j == 0) else {"acc_writes": [bp]}))

            def post_norm_residual(gi):
                gt, bg = load_g(gi)
                for st in range(2):
                    ss, bss = c["ss"].next()
                    jk, bjk = c["junk"].next()
                    fw.op("dve", lambda ss=ss: V.memset(ss[:], 0.0), writes=[bss])
                    fw.op("act", lambda jk=jk, ss=ss, st=st: A.activation(jk[:], ysb[:, st, :], AF.Square, accum_out=ss[:]), reads=[bys[st], bss], writes=[bjk, bss])
                    fw.op("act", lambda ss=ss: A.activation(ss[:], ss[:], AF.Sqrt, bias=EPS, scale=1.0 / D), reads=[bss], writes=[bss])
                    fw.op("dve", lambda ss=ss: V.reciprocal(ss[:], ss[:]), reads=[bss], writes=[bss])
                    fw.op("dve", lambda ss=ss, st=st: V.scalar_tensor_tensor(ysb[:, st, :], ysb[:, st, :], ss[:, 0:1], gt[:], op0=ALU.mult, op1=ALU.mult),
                          reads=[bys[st], bss, bg], writes=[bys[st]])
                return gt, bg

            for cs in range(cfg["ncs"] if "C" in cfg["phases"] else 0):
                r0 = cs * TC
                xd = [(xres[:, st, :], bxr[st]) for st in range(2)]
                hT, hb = actT[0]
                front_end(c, lambda st, r0=r0: x_own[r0 + st * 128: r0 + (st + 1) * 128, :], 2, hT, hb, xdst=xd)
                fw.dma("sp", osT[:], OSscr[:, :, r0:r0 + TC], reads=[b_scr["OS"]], writes=[bosT])
                fw.dma("sp", omT[:], OMscr[:, :, r0:r0 + TC], reads=[b_scr["OM"]], writes=[bomT])
                mT, mb = actT[1]
                bm_all = Buf("mixedT")
                for og in range(4):
                    wo, bwo = wring.next()
                    wov = wo[:].rearrange("p (a k n) -> p a k n", a=2, k=8)
                    fw.dma("sp", wov[:, 0], wb["wsbo"][:, :, og * 512:(og + 1) * 512], reads=[b_wb["wsbo"]], writes=[bwo])
                    fw.dma("sp", wov[:, 1], wb["wmlao"][:, :, og * 512:(og + 1) * 512], reads=[b_wb["wmlao"]], acc_writes=[bwo])
                    for gsel in range(2):
                        wg, bwg = load_w("wgs" if gsel == 0 else "wgm", 0, 16, og * 512, 512)
                        for oo in range(4):
                            oc = og * 4 + oo
                            cols = (oo * 128, oo * 128 + 128)
                            pa, bpa = proj_feat(wov[:, gsel], bwo, cols, 8, osT if gsel == 0 else omT, [bosT if gsel == 0 else bomT])
                            pg, bpg = proj_feat(wg, bwg, cols, 16, hT, hb)
                            sg, bsg = sgr.next()
                            fw.op("act", lambda sg=sg, pg=pg: A.activation(sg[:, 0:TC], pg[:, 0:TC], AF.Sigmoid), reads=[bpg], writes=[bsg])
                            if gsel == 0:
                                fw.op("dve", lambda sg=sg, pa=pa: V.tensor_tensor(sg[:, 0:TC], sg[:, 0:TC], pa[:, 0:TC], ALU.mult), reads=[bsg, bpa], writes=[bsg])
                                fw.op("dve", lambda sg=sg, oc=oc: V.tensor_copy(ysbf[:, oc * TC:(oc + 1) * TC], sg[:, 0:TC]), reads=[bsg],
                                      acc_writes=[bys[0]])
                            else:
                                fw.op("dve", lambda sg=sg, pa=pa: V.tensor_tensor(sg[:, 0:TC], sg[:, 0:TC], pa[:, 0:TC], ALU.mult), reads=[bsg, bpa], writes=[bsg])
                                fw.op("dve", lambda sg=sg, oc=oc: V.tensor_tensor(mT[:, oc, :], sg[:, 0:TC], ysbf[:, oc * TC:(oc + 1) * TC], ALU.add),
                                      reads=[bsg, bys[0]], acc_writes=[bm_all])
                if dbgC and cs == 0:
                    fw.dma("pool", d_mixed, mT[:], reads=[bm_all], acc_writes=[b_dbg])
                for cg in range(4):
                    wv, bw = load_w("wout", 0, 16, cg * 512, 512)
                    for st in range(2):
                        pt, bp = psr.next()
                        tok_mm(pt, bp, mT, [bm_all], st, wv, bw, [(kc, kc) for kc in range(16)], True, True)
                        evac(ysb[:, st, cg * 512:(cg + 1) * 512], pt[:], bp, {"writes": [bys[st]]} if cg == 0 else {"acc_writes": [bys[st]]},
                             reads=[bm_all] if cg == 0 else [])
                if dbgC and cs == 0:
                    fw.dma("pool", d_y, ysb[:], reads=bys, acc_writes=[b_dbg])
                post_norm_residual(0)
                for st in range(2):
                    fw.op("dve", lambda st=st: V.tensor_tensor(xres[:, st, :], xres[:, st, :], ysb[:, st, :], ALU.add), reads=[bxr[st], bys[st]], writes=[bxr[st]])
                if dbgC and cs == 0:
                    fw.dma("pool", d_x1, xres[:], reads=bxr, acc_writes=[b_dbg])
                h2T, h2b = actT[0]
                front_end(c, None, 2, h2T, h2b, xdst=xd)
                for fg in range(16):
                    wv, bw = load_w("wup", 0, 16, fg * 512, 512)
                    for fc in range(4):
                        pt, bp = proj_feat(wv, bw, (fc * 128, fc * 128 + 128), 16, h2T, h2b)
                        sg, bsg = sgr.next()
                        fw.op("act", lambda sg=sg, pt=pt: A.activation(sg[:, 0:TC], pt[:, 0:TC], AF.Relu), reads=[bp], writes=[bsg])
                        fw.op("dve", lambda sg=sg, f=fg * 4 + fc: V.tensor_tensor(uT[:, f, :], sg[:, 0:TC], sg[:, 0:TC], ALU.mult), reads=[bsg],
                              **({"writes": [buT]} if (fg == 0 and fc == 0) else {"acc_writes": [buT]}))
                for cg in range(4):
                    accs = [psr.next() for _ in range(2)]
                    for pc in range(8):
                        wv, bw = load_w("wdown", pc * 8, 8, cg * 512, 512)
                        for st in range(2):
                            tok_mm(accs[st][0], accs[st][1], uT, [buT], st, wv, bw, [(pc * 8 + j, j) for j in range(8)], pc == 0, pc == 7)
                    for st in range(2):
                        evac(ysb[:, st, cg * 512:(cg + 1) * 512], accs[st][0][:], accs[st][1], {"writes": [bys[st]]} if cg == 0 else {"acc_writes": [bys[st]]})
                post_norm_residual(1)
                for st in range(2):
                    fw.op("dve", lambda st=st: V.tensor_tensor(xres[:, st, :], xres[:, st, :], ysb[:, st, :], ALU.add), reads=[bxr[st], bys[st]], writes=[bxr[st]])
                if dbgC and cs == 0:
                    fw.dma("pool", d_x2, xres[:], reads=bxr, acc_writes=[b_dbg])
                fw.dma("sp", pf[:], p_own[r0:r0 + TC, :].rearrange("(s p) d -> p s d", p=128), writes=[bpf])
                fw.op("dve", lambda: V.tensor_copy(pbf[:], pf[:]), reads=[bpf], writes=[bpbf])
                tp, btp = c["tp"].next()
                for st in range(2):
                    for kc in range(2):
                        j = st * 2 + kc
                        fw.op("pe", lambda st=st, kc=kc, j=j, tp=tp: T.transpose(tp[:, j * 128:(j + 1) * 128], pbf[:, st, kc * 128:(kc + 1) * 128], ident),
                              reads=[bpbf, b_cst], **({"writes": [btp]} if j == 0 else {"acc_writes": [btp]}))
                for st in range(2):
                    fw.op("act", lambda st=st, tp=tp: A.copy(pT[:, :, st * 128:(st + 1) * 128], tp[:, st * 256:(st + 1) * 256].rearrange("p (k t) -> p k t", k=2)),
                          reads=[btp], **({"writes": [bpT]} if st == 0 else {"acc_writes": [bpT]}))
                for cg in range(4):
                    wv, bw = load_w("wple", 0, 2, cg * 512, 512)
                    for st in range(2):
                        pt, bp = psr.next()
                        tok_mm(pt, bp, pT, [bpT], st, wv, bw, [(0, 0), (1, 1)], True, True)
                        evac(ysb[:, st, cg * 512:(cg + 1) * 512], pt[:], bp, {"writes": [bys[st]]} if cg == 0 else {"acc_writes": [bys[st]]})
                post_norm_residual(2)
                if dbgC and cs == 0:
                    fw.dma("pool", d_e, ysb[:], reads=bys, acc_writes=[b_dbg])
                x2T, x2b = actT[1]
                front_end(c, None, 2, x2T, x2b, xdst=xd, norm=False)
                for cg in range(4):
                    wv, bw = load_w("wpg", 0, 16, cg * 512, 512)
                    for st in range(2):
                        pt, bp = psr.next()
                        tok_mm(pt, bp, x2T, [x2b[st]], st, wv, bw, [(kc, kc) for kc in range(16)], True, True)
                        sg, bsg = sgr.next()
                        sl = slice(cg * 512, (cg + 1) * 512)
                        fw.op("act", lambda sg=sg, pt=pt: A.activation(sg[:], pt[:], AF.Sigmoid), reads=[bp], writes=[bsg])
                        fw.op("dve", lambda sg=sg, st=st, sl=sl: V.tensor_tensor(sg[:], sg[:], ysb[:, st, sl], ALU.mult), reads=[bsg, bys[st]], writes=[bsg])
                        fw.op("dve", lambda sg=sg, st=st, sl=sl: V.tensor_tensor(ysb[:, st, sl], sg[:], xres[:, st, sl], ALU.add), reads=[bsg, bxr[st]], writes=[bys[st]])
                for st in range(2):
                    fw.dma("pool", out[r0 + st * 128: r0 + (st + 1) * 128, :], ysb[:, st, :], reads=[bys[st]], acc_writes=[b_scr["out"]])
            fw.barrier()
            st = fw.emit()
            print("program stats", st, flush=True)
    return nc


def _arr(w, kc):
    k, n = w.shape
    return np.ascontiguousarray(w.reshape(kc, 128, n).transpose(1, 0, 2))


def _masks(j, strict):
    sidx = np.arange(128)[:, None]
    tq = np.arange(512)[None, :]
    out = np.zeros((128, 16, 512), np.float32)
    for q in range(2):
        is_max = (j == 1) if q == 0 else (j == 0)
        for r in range(8):
            if is_max:
                m = None if r < 4 else r - 4
                allneg = False
            else:
                m = r if r < 4 else None
                allneg = r >= 4
            if allneg:
                blk = np.full((128, 512), NEG, np.float32)
            elif m is None:
                blk = np.zeros((128, 512), np.float32)
            else:
                vis = (128 * m + sidx) < tq if strict else (128 * m + sidx) <= tq
                blk = np.where(vis, 0.0, NEG).astype(np.float32)
            out[:, q * 8 + r, :] = blk
    return out.reshape(128, 16 * 512)


_PROG = {}


def _prep(x, p, positions, g_pre_mix, w_in, g_cq, g_ckv, w_q_up, w_kv_up, w_sb_o, w_mla_o, w_out,
           g_post_mix, g_pre_mlp, w_up, w_down, g_post_mlp, w_ple, g_ple, w_ple_gate):
    f = lambda a: np.asarray(a, dtype=np.float32)
    x, p = f(x), f(p)
    positions = np.asarray(positions).astype(np.int32)
    w_in0 = f(w_in)[0]
    kr = w_in0[:, 4096:4160]
    wqu = f(w_q_up)[0].reshape(512, 8, 192)
    rope = wqu[:, :, 128:192]
    wkv = f(w_kv_up)[0].reshape(512, 8, 256)
    shared = {
        "wq_f": _arr(w_in0[:, 0:1024], 16), "wk_f": _arr(w_in0[:, 1024:2048], 16), "wv_f": _arr(w_in0[:, 2048:3072], 16),
        "wcq_f": _arr(w_in0[:, 3072:3584], 16), "wckv_f": _arr(w_in0[:, 3584:4096], 16),
        "wkr_f": _arr(np.concatenate([kr, kr[:, 32:64], kr[:, 0:32]], axis=1), 16),
        "wgs_f": _arr(w_in0[:, 4160:6208], 16), "wgm_f": _arr(w_in0[:, 6208:8256], 16),
        "wqn_f": _arr(np.ascontiguousarray(wqu[:, :, 0:128]).reshape(512, 1024), 4),
        "wqr_f": _arr(np.concatenate([rope, rope[:, :, 32:64], rope[:, :, 0:32]], axis=2).reshape(512, 1024), 4),
        "wkn_f": _arr(np.ascontiguousarray(wkv[:, :, 0:128]).reshape(512, 1024), 4),
        "wvm_f": _arr(np.ascontiguousarray(wkv[:, :, 128:256]).reshape(512, 1024), 4),
        "wsbo_f": _arr(f(w_sb_o)[0], 8), "wmlao_f": _arr(f(w_mla_o)[0], 8), "wout_f": _arr(f(w_out)[0], 16),
        "wup_f": _arr(f(w_up)[0], 16), "wdown_f": _arr(f(w_down)[0], 64), "wple_f": _arr(f(w_ple)[0], 2),
        "wpg_f": _arr(f(w_ple_gate)[0], 16),
        "gpm": np.ascontiguousarray(f(g_pre_mix)[0].reshape(16, 128).T), "gcq": np.ascontiguousarray(f(g_cq)[0].reshape(4, 128).T),
        "gckv": np.ascontiguousarray(f(g_ckv)[0].reshape(4, 128).T), "gmlp": np.ascontiguousarray(f(g_pre_mlp)[0].reshape(16, 128).T),
        "grow": np.stack([f(g_post_mix)[0], f(g_post_mlp)[0], f(g_ple)[0]], axis=0),
    }
    jj = np.arange(128)[:, None]
    ss_ = np.arange(128)[None, :]
    shared["consts"] = np.concatenate([np.eye(128), (jj >= ss_), (jj < ss_), np.ones((128, 128))], axis=1).astype(np.float32)
    inv_freq = (np.float32(10000.0) ** (-np.arange(32, dtype=np.float32) / np.float32(32))).astype(np.float32)
    shared["invf"] = np.concatenate([inv_freq, inv_freq])[:, None].astype(np.float32)
    in_maps = []
    for c in range(NCORES):
        b, j = c // 2, c % 2
        tiles = SLOT_TILES[j]
        rows = np.concatenate([np.arange(t * 512, (t + 1) * 512) for t in tiles])
        m = dict(shared)
        m["x_all"] = np.ascontiguousarray(x[b])
        m["x_own"] = np.ascontiguousarray(x[b][rows])
        m["p_own"] = np.ascontiguousarray(p[0, b][rows])
        m["pos_all"] = np.ascontiguousarray(positions[b][None, :])
        m["pos_own"] = np.ascontiguousarray(positions[b][rows][None, :])
        m["mask_sb"] = _masks(j, True)
        m["mask_ml"] = _masks(j, False)
        in_maps.append(m)
    return in_maps


def kernel(**inputs):
    in_maps = _prep(**inputs)
    if "nc" not in _PROG:
        _PROG["nc"] = build_program()
    res = run_bass_kernel_spmd(_PROG["nc"], in_maps, core_ids=list(range(NCORES)))
    outp = np.empty((4, S, D), np.float32)
    for c in range(NCORES):
        b, j = c // 2, c % 2
        o = np.asarray(res.results[c]["out"])
        for i, t in enumerate(SLOT_TILES[j]):
            outp[b, t * 512:(t + 1) * 512] = o[i * 512:(i + 1) * 512]
    return outp
```

```python
import numpy as np
from contextlib import ExitStack
import concourse.bass as bass
import concourse.mybir as mybir
from concourse.bass_utils import run_bass_kernel_spmd

F32 = mybir.dt.float32
BF16 = mybir.dt.bfloat16
I32 = mybir.dt.int32
AF = mybir.ActivationFunctionType
ALU = mybir.AluOpType
AX = mybir.AxisListType


class Buf:
    __slots__ = ("name", "writers", "readers", "war", "excl")

    def __init__(self, name="", excl=False):
        self.name = name
        self.writers = {}
        self.readers = {}
        self.war = {}
        self.excl = excl


class _Op:
    __slots__ = ("eng", "fn", "deps", "is_dma", "signal", "need_signal", "slot", "idx")


SEM_LIMIT = 30000


class FW:
    ENGS = ("pe", "act", "dve", "pool", "sp")
    NDMASEM = 24

    def __init__(self, nc, es):
        self.nc = nc
        self.es = es
        self.ops = []
        self.engobj = {"pe": nc.tensor, "act": nc.scalar, "dve": nc.vector, "pool": nc.gpsimd, "sp": nc.sync}
        self.nsem = 0
        self.barrier_idx = 0
        self.emitted = 0
        self.slot_prev = [None] * self.NDMASEM
        self.ndma = 0
        self.engsem = {}
        self.engcnt = {}
        self.waited = {e: {} for e in self.ENGS}
        self.nwait = 0
        self.nsig = 0
        self.last_on_eng = {}
        self.dma_since_barrier = []
        self.serial = False

    def new_sem(self, name):
        self.nsem += 1
        return self.es.enter_context(self.nc.semaphore(f"{name}_{self.nsem}"))

    def _record(self, eng, fn, reads, writes, is_dma, acc_writes=()):
        op = _Op()
        op.eng = eng
        op.fn = fn
        op.is_dma = is_dma
        op.signal = None
        op.need_signal = is_dma
        op.slot = None
        op.idx = idx = len(self.ops)
        key = ("dma", idx) if is_dma else eng
        deps = set()
        bi = self.barrier_idx
        for b in reads:
            for w in b.writers.values():
                if w >= bi:
                    deps.add(w)
            if b.excl:
                for r in b.readers.values():
                    if r >= bi:
                        deps.add(r)
        for b in writes:
            for r in b.readers.values():
                if r >= bi:
                    deps.add(r)
            for w in b.writers.values():
                if w >= bi:
                    deps.add(w)
            for r in b.war.values():
                if r >= bi:
                    deps.add(r)
        for b in acc_writes:
            for r in b.readers.values():
                if r >= bi:
                    deps.add(r)
            for r in b.war.values():
                if r >= bi:
                    deps.add(r)
        for b in reads:
            b.readers[key] = idx
        for b in writes:
            b.war = b.readers
            b.writers = {key: idx}
            b.readers = {}
        for b in acc_writes:
            if b.readers:
                b.war = b.readers
                b.writers = {key: idx}
                b.readers = {}
            else:
                b.writers[key] = idx
        if self.serial and idx > 0 and idx - 1 >= bi:
            deps.add(idx - 1)
        deps.discard(idx)
        op.deps = deps
        self.ops.append(op)
        if fn is not None:
            if is_dma:
                self.dma_since_barrier.append(idx)
            else:
                self.last_on_eng[eng] = idx
        return op

    def op(self, eng, fn, reads=(), writes=(), acc_writes=()):
        return self._record(eng, fn, reads, writes, False, acc_writes)

    def dma(self, eng, out, in_, reads=(), writes=(), acc_writes=()):
        e = self.engobj[eng]
        return self._record(eng, lambda: e.dma_start(out=out, in_=in_), reads, writes, True, acc_writes)

    def barrier(self, engs=None):
        deps = set(self.last_on_eng.values()) | set(self.dma_since_barrier)
        deps = {d for d in deps if d >= self.barrier_idx}
        for eng in (engs or self.ENGS):
            op = _Op()
            op.eng = eng
            op.fn = None
            op.is_dma = False
            op.signal = None
            op.need_signal = False
            op.slot = None
            op.idx = len(self.ops)
            op.deps = set(deps)
            self.ops.append(op)
        self.barrier_idx = len(self.ops)
        self.dma_since_barrier = []
        self.last_on_eng = {}

    def emit(self):
        ops = self.ops
        lo = self.emitted
        for op in ops[lo:]:
            if op.is_dma:
                s = self.ndma % self.NDMASEM
                if self.slot_prev[s] is not None:
                    op.deps.add(self.slot_prev[s])
                self.slot_prev[s] = op.idx
                op.slot = s
                self.ndma += 1
        for op in ops[lo:]:
            for d in op.deps:
                p = ops[d]
                if p.is_dma or op.is_dma or p.eng != op.eng or op.eng != "pe":
                    assert d >= lo or p.signal is not None or p.fn is None, "dep on already-emitted unsignalled op"
                    p.need_signal = True
        engsem, engcnt = self.engsem, self.engcnt
        for op in ops[lo:]:
            e = self.engobj[op.eng]
            need = {}
            for d in op.deps:
                p = ops[d]
                if p.signal is None:
                    continue
                if (not p.is_dma) and (not op.is_dma) and p.eng == op.eng and op.eng == "pe":
                    continue
                sem, val = p.signal
                k = id(sem)
                if k not in need or need[k][1] < val:
                    need[k] = (sem, val)
            wc = self.waited[op.eng]
            for k, (sem, val) in need.items():
                if wc.get(k, 0) >= val:
                    continue
                e.wait_ge(sem, val)
                wc[k] = val
                self.nwait += 1
            if op.fn is None:
                continue
            ins = op.fn()
            op.fn = True
            if op.need_signal:
                if op.is_dma:
                    key = ("dma", op.slot)
                    inc = 16
                else:
                    key = op.eng
                    inc = 1
                if key not in engsem or engcnt[key] + inc > SEM_LIMIT:
                    engsem[key] = self.new_sem("d" if op.is_dma else op.eng)
                    engcnt[key] = 0
                engcnt[key] += inc
                ins.then_inc(engsem[key], inc)
                op.signal = (engsem[key], engcnt[key])
                self.nsig += 1
        self.emitted = len(ops)
        return dict(nops=len(ops), nwait=self.nwait, nsig=self.nsig, nsem=self.nsem)


NCORES = 8
S = 8192
D = 2048
TA = 512
NT = S // TA
NSLOT = 8
TC = 512
EPS = 1e-6
SLOT_TILES = {0: [0, 3, 4, 7, 8, 11, 12, 15], 1: [1, 2, 5, 6, 9, 10, 13, 14]}
NEG = -30000.0
SB_SCALE = 128 ** -0.5
MLA_SCALE = 192 ** -0.5

WSPEC = [
    ("wq", 16, 1024, "gpm", []), ("wk", 16, 1024, "gpm", []), ("wv", 16, 1024, "gpm", []),
    ("wcq", 16, 512, "gpm", []), ("wckv", 16, 512, "gpm", []), ("wkr", 16, 128, "gpm", [(64, 96)]),
    ("wgs", 16, 2048, "gpm", []), ("wgm", 16, 2048, "gpm", []),
    ("wqn", 4, 1024, "gcq", []), ("wqr", 4, 1024, "gcq", [(h * 128 + 64, h * 128 + 96) for h in range(8)]),
    ("wkn", 4, 1024, "gckv", []), ("wvm", 4, 1024, "gckv", []),
    ("wsbo", 8, 2048, None, []), ("wmlao", 8, 2048, None, []), ("wout", 16, 2048, None, []),
    ("wup", 16, 8192, "gmlp", []), ("wdown", 64, 2048, None, []), ("wple", 2, 2048, None, []),
    ("wpg", 16, 2048, None, []),
]
GSPEC = {"gpm": 16, "gcq": 4, "gckv": 4, "gmlp": 16}


class Ring:
    def __init__(self, alloc, name, n, shape, dt, excl=False):
        self.items = [(alloc(f"{name}{i}", shape, dt), Buf(f"{name}{i}", excl)) for i in range(n)]
        self.i = 0

    def next(self):
        it = self.items[self.i % len(self.items)]
        self.i += 1
        return it


def build_program(cfg=None):
    cfg = dict(dict(nkt=NT, nslotA=NSLOT, hsb=8, hml=8, nslotB=NSLOT, ncs=(S // 2) // TC, phases="0ABMC", dbg=()), **(cfg or {}))
    nc = bass.Bass("TRN2", target_bir_lowering=False)

    def din(name, shape, dt=F32):
        return nc.dram_tensor(name, shape, dt, kind="ExternalInput").ap()

    def dscr(name, shape, dt=BF16):
        if name in cfg["dbg"]:
            return nc.dram_tensor(name, shape, dt, kind="ExternalOutput").ap()
        return nc.dram_tensor(name, shape, dt).ap()

    x_all = din("x_all", [S, D])
    x_own = din("x_own", [S // 2, D])
    p_own = din("p_own", [S // 2, 256])
    pos_all = din("pos_all", [1, S], I32)
    pos_own = din("pos_own", [1, S // 2], I32)
    consts = din("consts", [128, 4 * 128])
    invf = din("invf", [64, 1])
    mask_sb = din("mask_sb", [128, 16 * 512])
    mask_ml = din("mask_ml", [128, 16 * 512])
    grow = din("grow", [3, D])
    gin = {g: din(g, [128, kc]) for g, kc in GSPEC.items()}
    win = {n: din(n + "_f", [128, kc, nn]) for n, kc, nn, _, _ in WSPEC}
    out = nc.dram_tensor("out", [S // 2, D], F32, kind="ExternalOutput").ap()

    wb = {n: dscr(n + "_b", [128, kc, nn]) for n, kc, nn, _, _ in WSPEC}
    Kscr = dscr("Kscr", [128, 8, S])
    Vscr = dscr("Vscr", [8, 128, 64, 128])
    KNscr = dscr("KNscr", [128, 8, S])
    VMscr = dscr("VMscr", [8, 128, 64, 128])
    KRscr = dscr("KRscr", [64, S])
    Qscr = dscr("Qscr", [128, 8, S // 2])
    QNscr = dscr("QNscr", [128, 8, S // 2])
    QRscr = dscr("QRscr", [64, 8, S // 2])
    OSscr = dscr("OSscr", [128, 8, S // 2])
    OMscr = dscr("OMscr", [128, 8, S // 2])
    dbgC = cfg.get("dbgC", False)
    if dbgC:
        d_mixed = nc.dram_tensor("d_mixed", [128, 16, TC], BF16, kind="ExternalOutput").ap()
        d_x1 = nc.dram_tensor("d_x1", [128, TC // 128, D], F32, kind="ExternalOutput").ap()
        d_x2 = nc.dram_tensor("d_x2", [128, TC // 128, D], F32, kind="ExternalOutput").ap()
        d_e = nc.dram_tensor("d_e", [128, TC // 128, D], F32, kind="ExternalOutput").ap()
        d_y = nc.dram_tensor("d_y", [128, TC // 128, D], F32, kind="ExternalOutput").ap()
        b_dbg = Buf("dbg")
    b_wb = {n: Buf(n) for n in wb}
    b_scr = {n: Buf(n) for n in ["K", "V", "KN", "VM", "KR", "Q", "QN", "QR", "OS", "OM", "out"]}

    _uid = [0]

    def uniq(n):
        _uid[0] += 1
        return f"{n}_{_uid[0]}"

    with ExitStack() as es:
        fw = FW(nc, es)
        fw.serial = bool(cfg.get('serial', False))
        V, A, P, T = nc.vector, nc.scalar, nc.gpsimd, nc.tensor

        galloc = lambda n, sh, dt: es.enter_context(nc.sbuf_tensor(n, sh, dt))
        cst = galloc("cst", [128, 512], BF16)
        ident, uincl, ubar, ones = cst[:, 0:128], cst[:, 128:256], cst[:, 256:384], cst[:, 384:512]
        gT = {g: galloc(g + "_t", [128, kc], F32) for g, kc in GSPEC.items()}
        gTn = {g: galloc(g + "_n", [128, GSPEC[g]], F32) for g in ("gpm", "gcq")}
        invf_t = galloc("invf_t", [64, 1], F32)
        b_cst = Buf("cst")
        with ExitStack() as ph:
            alloc = lambda n, sh, dt: ph.enter_context(nc.sbuf_tensor(uniq(n), sh, dt))
            cf = alloc("cf", [128, 512], F32)
            b_cf = Buf("cf")
            fw.dma("sp", cf[:], consts, writes=[b_cf])
            fw.op("dve", lambda: V.tensor_copy(cst[:], cf[:]), reads=[b_cf], writes=[b_cst])
            for g in GSPEC:
                fw.dma("sp", gT[g][:], gin[g], writes=[b_cst])
            fw.dma("sp", invf_t[:], invf, writes=[b_cst])
            fw.barrier()
            for g in gTn:
                fw.op("dve", lambda g=g: V.tensor_scalar(gTn[g][:], gT[g][:], -1.0, None, op0=ALU.mult),
                      reads=[b_cst], writes=[b_cst])

            stf = Ring(alloc, "stf", 2, [128, 4096], F32)
            stb = Ring(alloc, "stb", 2, [128, 4096], BF16)
            tog = 0
            for name, KC, N, gname, negs in (WSPEC if "0" in cfg["phases"] else []):
                if N >= 4096:
                    chunks = [(kc, 1, n0, min(4096, N - n0)) for kc in range(KC) for n0 in range(0, N, 4096)]
                else:
                    kcn = max(1, 4096 // N)
                    chunks = [(k0, min(kcn, KC - k0), 0, N) for k0 in range(0, KC, kcn)]
                for k0, kn, n0, nn in chunks:
                    (tf, bf), (tb, bb) = stf.next(), stb.next()
                    fv = tf[:, 0:kn * nn].rearrange("p (k n) -> p k n", k=kn)
                    bv = tb[:, 0:kn * nn].rearrange("p (k n) -> p k n", k=kn)
                    fw.dma("sp", fv, win[name][:, k0:k0 + kn, n0:n0 + nn], writes=[bf])
                    if gname is None:
                        if tog % 2 == 0:
                            fw.op("act", lambda tb=tb, tf=tf, m=kn * nn: A.copy(tb[:, 0:m], tf[:, 0:m]), reads=[bf], writes=[bb])
                        else:
                            fw.op("dve", lambda tb=tb, tf=tf, m=kn * nn: V.tensor_copy(tb[:, 0:m], tf[:, 0:m]), reads=[bf], writes=[bb])
                        tog += 1
                    else:
                        first = True
                        for kk in range(kn):
                            kc = k0 + kk
                            segs, c = [], 0
                            for lo, hi in negs:
                                if lo > c:
                                    segs.append((c, lo, 1))
                                segs.append((lo, hi, -1))
                                c = hi
                            if c < nn:
                                segs.append((c, nn, 1))
                            for lo, hi, sg in segs:
                                sc = (gT if sg > 0 else gTn)[gname][:, kc:kc + 1]
                                o_ap, i_ap = tb[:, kk * nn + lo:kk * nn + hi], tf[:, kk * nn + lo:kk * nn + hi]
                                kw = {"writes": [bb]} if first else {"acc_writes": [bb]}
                                first = False
                                if tog % 2 == 0:
                                    fw.op("act", lambda o=o_ap, i=i_ap, sc=sc: A.activation(o, i, AF.Copy, scale=sc), reads=[bf, b_cst], **kw)
                                else:
                                    fw.op("dve", lambda o=o_ap, i=i_ap, sc=sc: V.tensor_scalar(o, i, sc, None, op0=ALU.mult), reads=[bf, b_cst], **kw)
                                tog += 1
                    fw.dma("pool", wb[name][:, k0:k0 + kn, n0:n0 + nn], bv, reads=[bb], acc_writes=[b_wb[name]])
            fw.barrier()
            fw.emit()

        def front_end(alloc_ctx, xsrc, nsub, hT, hbufs, xdst=None, norm=True):
            c = alloc_ctx
            for st in range(nsub):
                if xdst is None:
                    xt, bx = c["xs"].next()
                    xap = xt[:]
                else:
                    xap, bx = xdst[st]
                if xsrc is not None:
                    fw.dma("sp", xap, xsrc(st), writes=[bx])
                xn, bxn = c["xn"].next()
                if not norm:
                    fw.op("dve", lambda xn=xn, xap=xap: V.tensor_copy(xn[:], xap), reads=[bx], writes=[bxn])
                ss, bss = c["ss"].next()
                jk, bjk = c["junk"].next()
                if norm:
                  fw.op("dve", lambda ss=ss: V.memset(ss[:], 0.0), writes=[bss])
                  fw.op("act", lambda jk=jk, xap=xap, ss=ss: A.activation(jk[:], xap, AF.Square, accum_out=ss[:]),
                      reads=[bx, bss], writes=[bjk, bss])
                  fw.op("act", lambda ss=ss: A.activation(ss[:], ss[:], AF.Sqrt, bias=EPS, scale=1.0 / D),
                      reads=[bss], writes=[bss])
                  fw.op("dve", lambda ss=ss: V.reciprocal(ss[:], ss[:]),
                      reads=[bss], writes=[bss])
                  fw.op("dve", lambda xn=xn, xap=xap, ss=ss: V.tensor_scalar(xn[:], xap, ss[:, 0:1], None, op0=ALU.mult),
                      reads=[bx, bss], writes=[bxn])
                for half in range(2):
                    tp, btp = c["tp"].next()
                    for q in range(8):
                        kc = half * 8 + q
                        fw.op("pe", lambda tp=tp, xn=xn, q=q, kc=kc: T.transpose(tp[:, q * 128:(q + 1) * 128], xn[:, kc * 128:(kc + 1) * 128], ident),
                              reads=[bxn, b_cst], **({"writes": [btp]} if q == 0 else {"acc_writes": [btp]}))
                    dst = hT[:, half * 8:half * 8 + 8, st * 128:(st + 1) * 128]
                    src = tp[:].rearrange("p (k t) -> p k t", k=8)
                    kw = {"writes": [hbufs[st]]} if half == 0 else {"acc_writes": [hbufs[st]]}
                    if half == 0:
                        fw.op("act", lambda dst=dst, src=src: A.copy(dst, src), reads=[btp], **kw)
                    else:
                        fw.op("dve", lambda dst=dst, src=src: V.tensor_copy(dst, src), reads=[btp], **kw)

        def rope_tables(c, pos_src, cos2, sin2, b_cs):
            TWO_PI = float(2 * np.pi)
            C1 = 6.28125
            C2 = TWO_PI - C1
            pi_t, bpi = c["posi"].next()
            fw.dma("sp", pi_t[:], pos_src.partition_broadcast(64), writes=[bpi])
            ang, bang = c["ang"].next()
            y, by = c["ang"].next()
            r, br = c["ang"].next()
            m, bm = c["ang"].next()
            ni, bni = c["posi"].next()
            fw.op("dve", lambda: V.tensor_copy(ang[:], pi_t[:]), reads=[bpi], writes=[bang])
            fw.op("dve", lambda: V.tensor_scalar(ang[:], ang[:], invf_t[:, 0:1], None, op0=ALU.mult), reads=[bang, b_cst], writes=[bang])
            fw.op("dve", lambda: V.tensor_scalar(y[:], ang[:], 1.0 / TWO_PI, 0.5, op0=ALU.mult, op1=ALU.add), reads=[bang], writes=[by])
            fw.op("dve", lambda: V.tensor_copy(ni[:], y[:]), reads=[by], writes=[bni])
            fw.op("dve", lambda: V.tensor_copy(y[:], ni[:]), reads=[bni], writes=[by])
            fw.op("dve", lambda: V.scalar_tensor_tensor(r[:], y[:], -C1, ang[:], op0=ALU.mult, op1=ALU.add), reads=[by, bang], writes=[br])
            fw.op("dve", lambda: V.scalar_tensor_tensor(r[:], y[:], -C2, r[:], op0=ALU.mult, op1=ALU.add), reads=[by, br], writes=[br])

            def wrap(t, bt):
                fw.op("dve", lambda: V.tensor_scalar(m[:], t[:], float(-np.pi), None, op0=ALU.is_lt), reads=[bt], writes=[bm])
                fw.op("dve", lambda: V.scalar_tensor_tensor(t[:], m[:], TWO_PI, t[:], op0=ALU.mult, op1=ALU.add), reads=[bm, bt], writes=[bt])
                fw.op("dve", lambda: V.tensor_scalar(m[:], t[:], float(np.pi), None, op0=ALU.is_gt), reads=[bt], writes=[bm])
                fw.op("dve", lambda: V.scalar_tensor_tensor(t[:], m[:], -TWO_PI, t[:], op0=ALU.mult, op1=ALU.add), reads=[bm, bt], writes=[bt])
                fw.op("dve", lambda: V.tensor_scalar(t[:], t[:], float(-np.pi), float(np.pi), op0=ALU.max, op1=ALU.min), reads=[bt], writes=[bt])

            wrap(r, br)
            fw.op("act", lambda: A.activation(sin2[:], r[:], AF.Sin), reads=[br], writes=[b_cs])
            fw.op("dve", lambda: V.tensor_scalar(y[:], r[:], float(np.pi / 2), None, op0=ALU.add), reads=[br], writes=[by])
            wrap(y, by)
            fw.op("act", lambda: A.activation(cos2[:], y[:], AF.Sin), reads=[by], writes=[b_cs])

        def featmajor_norm(c, ps_list, bps_list, outT, b_out, psb, bpsb):
            cf_t, bcf = c["cfm"].next()
            sq_t, bsq = c["sq"].next()
            for i in range(4):
                fw.op("dve", lambda i=i: V.tensor_copy(cf_t[:, i * 512:(i + 1) * 512], ps_list[i][:]), reads=[bps_list[i]],
                      **({"writes": [bcf]} if i == 0 else {"acc_writes": [bcf]}))
                fw.op("act", lambda i=i: A.activation(sq_t[:, i * 512:(i + 1) * 512], ps_list[i][:], AF.Square), reads=[bps_list[i]],
                      **({"writes": [bsq]} if i == 0 else {"acc_writes": [bsq]}))
            for i in range(4):
                fw.op("pe", lambda i=i: T.matmul(psb[:], ones, sq_t[:, i * 512:(i + 1) * 512], start=(i == 0), stop=(i == 3)),
                      reads=[bsq, b_cst], **({"writes": [bpsb]} if i == 0 else {"acc_writes": [bpsb]}))
            rs, brs = c["rsb"].next()
            fw.op("act", lambda: A.activation(rs[:], psb[:], AF.Sqrt, bias=EPS, scale=1.0 / 512), reads=[bpsb], writes=[brs])
            fw.op("dve", lambda: V.reciprocal(rs[:], rs[:]), reads=[brs], writes=[brs])
            for i in range(4):
                fw.op("dve", lambda i=i: V.tensor_tensor(outT[:, i, :], cf_t[:, i * 512:(i + 1) * 512], rs[:], ALU.mult), reads=[bcf, brs],
                      **({"writes": [b_out]} if i == 0 else {"acc_writes": [b_out]}))

        with ExitStack() as ph:
            alloc = lambda n, sh, dt: ph.enter_context(nc.sbuf_tensor(uniq(n), sh, dt))
            palloc = lambda n, sh, dt: ph.enter_context(nc.psum_tensor(uniq(n), sh, dt))
            c = {
                "xs": Ring(alloc, "xs", 2, [128, D], F32), "ss": Ring(alloc, "ss", 4, [128, 1], F32),
                "junk": Ring(alloc, "junk", 1, [128, D], BF16), "xn": Ring(alloc, "xn", 2, [128, D], BF16),
                "tp": Ring(palloc, "tp", 2, [128, 1024], BF16, excl=True),
                "posi": Ring(alloc, "posi", 2, [64, 512], I32), "ang": Ring(alloc, "ang", 4, [64, 512], F32),
                "cfm": Ring(alloc, "cfm", 1, [128, 2048], F32), "sq": Ring(alloc, "sq", 1, [128, 2048], BF16),
                "rsb": Ring(alloc, "rsb", 1, [128, 512], F32),
            }
            hTr = [(alloc(f"hT{i}", [128, 16, 512], BF16), [Buf(f"hT{i}_{s}") for s in range(4)]) for i in range(2)]
            wring = Ring(alloc, "wr", 2, [128, 16 * 512], BF16)
            psr = Ring(palloc, "ps", 6, [128, 512], F32, excl=True)
            wsm = {n: alloc("w_" + n, [128, 4 * 1024], BF16) for n in ("wkn", "wvm", "wqn", "wqr")}
            wkr_t = alloc("w_wkr", [128, 16 * 128], BF16)
            b_wsm = Buf("wsm")
            for n in wsm:
                fw.dma("sp", wsm[n][:].rearrange("p (k n) -> p k n", k=4), wb[n], reads=[b_wb[n]], acc_writes=[b_wsm])
            fw.dma("sp", wkr_t[:].rearrange("p (k n) -> p k n", k=16), wb["wkr"], reads=[b_wb["wkr"]], acc_writes=[b_wsm])
            wsmv = {n: wsm[n][:].rearrange("p (k n) -> p k n", k=4) for n in wsm}
            wkrv = wkr_t[:].rearrange("p (k n) -> p k n", k=16)
            cos2, sin2 = alloc("cos2", [64, 512], F32), alloc("sin2", [64, 512], F32)
            b_cs = Buf("cs")
            stg8 = Ring(alloc, "stg8", 2, [128, 8 * 512], BF16)
            stgv = Ring(alloc, "stgv", 2, [128, 1024], BF16)
            cnT = alloc("cnT", [128, 4, 512], BF16)
            b_cn = Buf("cnT")
            rt = Ring(alloc, "rt", 2, [64, 512], F32)
            stgr = Ring(alloc, "stgr", 2, [64, 8 * 512], BF16)
            evt = [0]

            def evac(dst, src, bsrc, kw):
                evt[0] += 1
                if evt[0] % 2:
                    fw.op("act", lambda: A.copy(dst, src), reads=[bsrc], **kw)
                else:
                    fw.op("dve", lambda: V.tensor_copy(dst, src), reads=[bsrc], **kw)

            def load_w(name, kc_n, col0, ncols):
                wt, bw = wring.next()
                v = wt[:, 0:kc_n * ncols].rearrange("p (k n) -> p k n", k=kc_n)
                fw.dma("sp", v, wb[name][:, :, col0:col0 + ncols], reads=[b_wb[name]], writes=[bw])
                return v, bw

            def proj_feat(wv, bw, wcols, KCn, rhsT, rbufs, M=128):
                pt, bp = psr.next()
                for kc in range(KCn):
                    fw.op("pe", lambda kc=kc: T.matmul(pt[0:M, :], wv[:, kc, wcols[0]:wcols[1]], rhsT[:, kc, :], start=(kc == 0), stop=(kc == KCn - 1)),
                          reads=[bw] + rbufs, **({"writes": [bp]} if kc == 0 else {"acc_writes": [bp]}))
                return pt, bp

            def proj_tok(wv, bw, wcols, KCn, lhsT, lbufs, st):
                pt, bp = psr.next()
                for kc in range(KCn):
                    fw.op("pe", lambda kc=kc: T.matmul(pt[:], lhsT[:, kc, st * 128:(st + 1) * 128], wv[:, kc, wcols[0]:wcols[1]], start=(kc == 0), stop=(kc == KCn - 1)),
                          reads=[bw] + lbufs, **({"writes": [bp]} if kc == 0 else {"acc_writes": [bp]}))
                return pt, bp

            def rope_combine(pa, bpa, pb, bpb, dst, kw):
                t1, b1 = rt.next()
                t2, b2 = rt.next()
                fw.op("dve", lambda: V.tensor_tensor(t1[:], pa[0:64, :], cos2[:], ALU.mult), reads=[bpa, b_cs], writes=[b1])
                fw.op("dve", lambda: V.tensor_tensor(t2[:], pb[0:64, :], sin2[:], ALU.mult), reads=[bpb, b_cs], writes=[b2])
                fw.op("dve", lambda: V.tensor_tensor(dst, t1[:], t2[:], ALU.add), reads=[b1, b2], **kw)

            for kt in range(cfg["nkt"] if "A" in cfg["phases"] else 0):
                hT, hb = hTr[kt % 2]
                front_end(c, lambda st, kt=kt: x_all[kt * 512 + st * 128: kt * 512 + (st + 1) * 128, :], 4, hT, hb)
                if cfg.get('astop', 99) < 2:
                    continue
                rope_tables(c, pos_all[:, kt * 512:(kt + 1) * 512], cos2, sin2, b_cs)
                if cfg.get('astop', 99) < 3:
                    continue
                sk, bsk = stg8.next()
                for cg in range(2):
                    wv, bw = load_w("wk", 16, cg * 512, 512)
                    for hh in range(4):
                        pt, bp = proj_feat(wv, bw, (hh * 128, hh * 128 + 128), 16, hT, hb)
                        h = cg * 4 + hh
                        evac(sk[:, h * 512:(h + 1) * 512], pt[:], bp, {"writes": [bsk]} if h == 0 else {"acc_writes": [bsk]})
                fw.dma("pool", Kscr[:, :, kt * 512:(kt + 1) * 512], sk[:].rearrange("p (h t) -> p h t", h=8), reads=[bsk], acc_writes=[b_scr["K"]])
                if cfg.get('astop', 99) < 4:
                    continue
                wvs = [load_w("wv", 16, cg * 512, 512) for cg in range(2)]
                for st in range(4):
                    sv, bsv = stgv.next()
                    for cg in range(2):
                        pt, bp = proj_tok(wvs[cg][0], wvs[cg][1], (0, 512), 16, hT, [hb[st]], st)
                        evac(sv[:, cg * 512:(cg + 1) * 512], pt[:], bp, {"writes": [bsv]} if cg == 0 else {"acc_writes": [bsv]})
                    fw.dma("pool", Vscr.rearrange("h p k d -> p h k d")[:, :, kt * 4 + st, :], sv[:].rearrange("p (h d) -> p h d", h=8),
                           reads=[bsv], acc_writes=[b_scr["V"]])
                if cfg.get('astop', 99) < 5:
                    continue
                wv, bw = load_w("wckv", 16, 0, 512)
                pl = [proj_feat(wv, bw, (i * 128, i * 128 + 128), 16, hT, hb) for i in range(4)]
                psb, bpsb = psr.next()
                featmajor_norm(c, [p[0] for p in pl], [p[1] for p in pl], cnT, b_cn, psb, bpsb)
                if cfg.get('astop', 99) < 6:
                    continue
                pa, bpa = proj_feat(wkrv, b_wsm, (0, 64), 16, hT, hb, M=64)
                pb, bpb = proj_feat(wkrv, b_wsm, (64, 128), 16, hT, hb, M=64)
                sr, bsr = stgr.next()
                rope_combine(pa, bpa, pb, bpb, sr[:, 0:512], {"writes": [bsr]})
                fw.dma("pool", KRscr[:, kt * 512:(kt + 1) * 512], sr[:, 0:512], reads=[bsr], acc_writes=[b_scr["KR"]])
                if cfg.get('astop', 99) < 7:
                    continue
                sk, bsk = stg8.next()
                for h in range(8):
                    pt, bp = proj_feat(wsmv["wkn"], b_wsm, (h * 128, h * 128 + 128), 4, cnT, [b_cn])
                    evac(sk[:, h * 512:(h + 1) * 512], pt[:], bp, {"writes": [bsk]} if h == 0 else {"acc_writes": [bsk]})
                fw.dma("pool", KNscr[:, :, kt * 512:(kt + 1) * 512], sk[:].rearrange("p (h t) -> p h t", h=8), reads=[bsk], acc_writes=[b_scr["KN"]])
                if cfg.get('astop', 99) < 8:
                    continue
                for st in range(4):
                    sv, bsv = stgv.next()
                    for cg in range(2):
                        pt, bp = proj_tok(wsmv["wvm"], b_wsm, (cg * 512, cg * 512 + 512), 4, cnT, [b_cn], st)
                        evac(sv[:, cg * 512:(cg + 1) * 512], pt[:], bp, {"writes": [bsv]} if cg == 0 else {"acc_writes": [bsv]})
                    fw.dma("pool", VMscr.rearrange("h p k d -> p h k d")[:, :, kt * 4 + st, :], sv[:].rearrange("p (h d) -> p h d", h=8),
                           reads=[bsv], acc_writes=[b_scr["VM"]])
            for k in range(cfg["nslotA"] if "A" in cfg["phases"] else 0):
                hT, hb = hTr[k % 2]
                front_end(c, lambda st, k=k: x_own[k * 512 + st * 128: k * 512 + (st + 1) * 128, :], 4, hT, hb)
                rope_tables(c, pos_own[:, k * 512:(k + 1) * 512], cos2, sin2, b_cs)
                sk, bsk = stg8.next()
                for cg in range(2):
                    wv, bw = load_w("wq", 16, cg * 512, 512)
                    for hh in range(4):
                        pt, bp = proj_feat(wv, bw, (hh * 128, hh * 128 + 128), 16, hT, hb)
                        h = cg * 4 + hh
                        evac(sk[:, h * 512:(h + 1) * 512], pt[:], bp, {"writes": [bsk]} if h == 0 else {"acc_writes": [bsk]})
                fw.dma("pool", Qscr[:, :, k * 512:(k + 1) * 512], sk[:].rearrange("p (h t) -> p h t", h=8), reads=[bsk], acc_writes=[b_scr["Q"]])
                wv, bw = load_w("wcq", 16, 0, 512)
                pl = [proj_feat(wv, bw, (i * 128, i * 128 + 128), 16, hT, hb) for i in range(4)]
                psb, bpsb = psr.next()
                featmajor_norm(c, [p[0] for p in pl], [p[1] for p in pl], cnT, b_cn, psb, bpsb)
                sk, bsk = stg8.next()
                for h in range(8):
                    pt, bp = proj_feat(wsmv["wqn"], b_wsm, (h * 128, h * 128 + 128), 4, cnT, [b_cn])
                    evac(sk[:, h * 512:(h + 1) * 512], pt[:], bp, {"writes": [bsk]} if h == 0 else {"acc_writes": [bsk]})
                fw.dma("pool", QNscr[:, :, k * 512:(k + 1) * 512], sk[:].rearrange("p (h t) -> p h t", h=8), reads=[bsk], acc_writes=[b_scr["QN"]])
                sr, bsr = stgr.next()
                for h in range(8):
                    pa, bpa = proj_feat(wsmv["wqr"], b_wsm, (h * 128, h * 128 + 64), 4, cnT, [b_cn], M=64)
                    pb, bpb = proj_feat(wsmv["wqr"], b_wsm, (h * 128 + 64, h * 128 + 128), 4, cnT, [b_cn], M=64)
                    rope_combine(pa, bpa, pb, bpb, sr[:, h * 512:(h + 1) * 512], {"writes": [bsr]} if h == 0 else {"acc_writes": [bsr]})
                fw.dma("pool", QRscr[:, :, k * 512:(k + 1) * 512], sr[:].rearrange("p (h t) -> p h t", h=8), reads=[bsr], acc_writes=[b_scr["QR"]])
            fw.barrier()
            fw.emit()

        def load_masks(alloc, src, dstname):
            mt = alloc(dstname, [128, 16 * 512], BF16)
            bm = Buf(dstname)
            stg = Ring(alloc, dstname + "s", 2, [128, 2048], F32)
            for i in range(4):
                t, b = stg.next()
                fw.dma("sp", t[:], src[:, i * 2048:(i + 1) * 2048], writes=[b])
                fw.op("dve", lambda t=t, i=i: V.tensor_copy(mt[:, i * 2048:(i + 1) * 2048], t[:]), reads=[b],
                      **({"writes": [bm]} if i == 0 else {"acc_writes": [bm]}))
            return mt, bm

        with ExitStack() as ph:
            alloc = lambda n, sh, dt: ph.enter_context(nc.sbuf_tensor(uniq(n), sh, dt))
            palloc = lambda n, sh, dt: ph.enter_context(nc.psum_tensor(uniq(n), sh, dt))
            msk, bmsk = load_masks(alloc, mask_sb, "msb")
            KTr = Ring(alloc, "KT", 2, [128, S], BF16)
            Vr = Ring(alloc, "Vt", 2, [128, 64 * 128], BF16)
            QTr = Ring(alloc, "QT", 2, [128, S // 2], BF16)
            zr = Ring(palloc, "zps", 2, [128, 512], F32, excl=True)
            Rp, bR = palloc("Rps", [128, 512], F32), Buf("R", True)
            Op_, bO = palloc("Ops", [128, 512], F32), Buf("O", True)
            Er = Ring(alloc, "Ef", 3, [128, 512], F32)
            Lr = Ring(alloc, "Lb", 3, [128, 512], BF16)
            Gr = Ring(alloc, "Gf", 2, [128, 512], F32)
            Wr = Ring(alloc, "wbt", 3, [128, 512], BF16)
            osr = Ring(alloc, "ost", 2, [128, 512], BF16)
            for h in range(cfg["hsb"] if "B" in cfg["phases"] else 0):
                (KT, bK), (Vt, bV), (QT, bQ) = KTr.next(), Vr.next(), QTr.next()
                fw.dma("sp", KT[:], Kscr[:, h, :], reads=[b_scr["K"]], writes=[bK])
                fw.dma("sp", QT[:], Qscr[:, h, :], reads=[b_scr["Q"]], writes=[bQ])
                for q4 in range(4):
                    fw.dma("sp", Vt[:, q4 * 2048:(q4 + 1) * 2048].rearrange("p (k d) -> p k d", k=16), Vscr[h, :, q4 * 16:(q4 + 1) * 16, :],
                           reads=[b_scr["V"]], **({"writes": [bV]} if q4 == 0 else {"acc_writes": [bV]}))
                for k in range(cfg["nslotB"]):
                    nb = 8 * (k + 1)
                    par = k % 2
                    st_ = {}

                    def stA(i, k=k, nb=nb, par=par, st_=st_, KT=KT, QT=QT, bK=bK, bQ=bQ):
                        kb = nb - 1 - i
                        zp, bz = zr.next()
                        bnd = kb >= 8 * k
                        fw.op("pe", lambda: T.matmul(zp[:], KT[:, kb * 128:(kb + 1) * 128], QT[:, k * 512:(k + 1) * 512], start=True, stop=not bnd),
                              reads=[bK, bQ], writes=[bz])
                        if bnd:
                            r = kb - 8 * k
                            m0 = (par * 8 + r) * 512
                            fw.op("pe", lambda: T.matmul(zp[:], ident, msk[:, m0:m0 + 512], start=False, stop=True),
                                  reads=[bmsk, b_cst], acc_writes=[bz])
                        st_[("z", i)] = (zp, bz)

                    def stB(i, st_=st_):
                        zp, bz = st_.pop(("z", i))
                        (Ef, bE), (Lb, bL) = Er.next(), Lr.next()
                        fw.op("act", lambda: A.activation(Ef[:], zp[:], AF.Exp, scale=SB_SCALE), reads=[bz], writes=[bE])
                        fw.op("act", lambda: A.activation(Lb[:], Ef[:], AF.Ln, bias=1.0, scale=1.0), reads=[bE], writes=[bL])
                        st_[("E", i)] = (Ef, bE)
                        st_[("L", i)] = (Lb, bL)

                    def stC(i, st_=st_):
                        Lb, bL = st_[("L", i)]
                        fw.op("pe", lambda: T.matmul(Rp[:], uincl, Lb[:], start=(i == 0), stop=True, skip_group_check=True),
                              reads=[bL, b_cst], writes=[bR])

                    def stD(i, st_=st_):
                        Gf, bG = Gr.next()
                        fw.op("act", lambda: A.activation(Gf[:], Rp[:], AF.Exp, scale=-1.0), reads=[bR], writes=[bG])
                        st_[("G", i)] = (Gf, bG)

                    def stE(i, st_=st_):
                        (Ef, bE), (Gf, bG) = st_.pop(("E", i)), st_.pop(("G", i))
                        wt_, bw_ = Wr.next()
                        fw.op("dve", lambda: V.tensor_tensor(wt_[:], Ef[:], Gf[:], ALU.mult), reads=[bE, bG], writes=[bw_])
                        st_[("w", i)] = (wt_, bw_)

                    def stF(i, nb=nb, st_=st_, Vt=Vt, bV=bV):
                        kb = nb - 1 - i
                        Lb, bL = st_.pop(("L", i))
                        wt_, bw_ = st_.pop(("w", i))
                        if i < nb - 1:
                            fw.op("pe", lambda: T.matmul(Rp[:], ubar, Lb[:], start=False, stop=True, skip_group_check=True),
                                  reads=[bL, b_cst], writes=[bR])
                        fw.op("pe", lambda: T.matmul(Op_[:], Vt[:, kb * 128:(kb + 1) * 128], wt_[:], start=(i == 0), stop=(i == nb - 1), skip_group_check=True),
                              reads=[bV, bw_], **({"writes": [bO]} if i == 0 else {"acc_writes": [bO]}))

                    for t in range(nb + 2):
                        if t < nb:
                            stA(t)
                            stB(t)
                        if 0 <= t - 2 < nb:
                            stF(t - 2)
                        if 0 <= t - 1 < nb:
                            stC(t - 1)
                            stD(t - 1)
                            stE(t - 1)
                    ot, bo = osr.next()
                    fw.op("dve", lambda ot=ot: V.tensor_copy(ot[:], Op_[:]), reads=[bO], writes=[bo])
                    fw.dma("pool", OSscr[:, h, k * 512:(k + 1) * 512], ot[:], reads=[bo], acc_writes=[b_scr["OS"]])
            fw.barrier()
            fw.emit()

        with ExitStack() as ph:
            alloc = lambda n, sh, dt: ph.enter_context(nc.sbuf_tensor(uniq(n), sh, dt))
            palloc = lambda n, sh, dt: ph.enter_context(nc.psum_tensor(uniq(n), sh, dt))
            msk, bmsk = load_masks(alloc, mask_ml, "mml")
            KNr = Ring(alloc, "KN", 2, [128, S], BF16)
            VMr = Ring(alloc, "VMt", 2, [128, 64 * 128], BF16)
            QNr = Ring(alloc, "QN", 2, [128, S // 2], BF16)
            QRr = Ring(alloc, "QR", 2, [64, S // 2], BF16)
            KRt, bKR = alloc("KRt", [64, S], BF16), Buf("KRt")
            fw.dma("sp", KRt[:], KRscr, reads=[b_scr["KR"]], writes=[bKR])
            sr_ = Ring(palloc, "sps", 3, [128, 512], F32, excl=True)
            Op_, bO = palloc("Omps", [128, 512], F32), Buf("Om", True)
            Dp, bD = palloc("Dps", [128, 512], F32), Buf("Dm", True)
            Pr = Ring(alloc, "Pb", 4, [128, 512], BF16)
            rdr = Ring(alloc, "rd", 2, [128, 512], F32)
            osr = Ring(alloc, "omt", 2, [128, 512], BF16)
            for h in range(cfg["hml"] if "M" in cfg["phases"] else 0):
                (KN, bK), (Vt, bV), (QN, bQ), (QR, bQR) = KNr.next(), VMr.next(), QNr.next(), QRr.next()
                fw.dma("sp", KN[:], KNscr[:, h, :], reads=[b_scr["KN"]], writes=[bK])
                fw.dma("sp", QN[:], QNscr[:, h, :], reads=[b_scr["QN"]], writes=[bQ])
                fw.dma("sp", QR[:], QRscr[:, h, :], reads=[b_scr["QR"]], writes=[bQR])
                for q4 in range(4):
                    fw.dma("sp", Vt[:, q4 * 2048:(q4 + 1) * 2048].rearrange("p (k d) -> p k d", k=16), VMscr[h, :, q4 * 16:(q4 + 1) * 16, :],
                           reads=[b_scr["VM"]], **({"writes": [bV]} if q4 == 0 else {"acc_writes": [bV]}))
                for k in range(cfg["nslotB"]):
                    nb = 8 * (k + 1)
                    par = k % 2
                    st_ = {}

                    def mA(i, k=k, par=par, st_=st_, KN=KN, QN=QN, QR=QR, bK=bK, bQ=bQ, bQR=bQR):
                        kb = i
                        sp_, bs = sr_.next()
                        bnd = kb >= 8 * k
                        fw.op("pe", lambda: T.matmul(sp_[:], KN[:, kb * 128:(kb + 1) * 128], QN[:, k * 512:(k + 1) * 512], start=True, stop=False),
                              reads=[bK, bQ], writes=[bs])
                        fw.op("pe", lambda: T.matmul(sp_[:], KRt[:, kb * 128:(kb + 1) * 128], QR[:, k * 512:(k + 1) * 512], start=False, stop=not bnd),
                              reads=[bKR, bQR], acc_writes=[bs])
                        if bnd:
                            r = kb - 8 * k
                            m0 = (par * 8 + r) * 512
                            fw.op("pe", lambda: T.matmul(sp_[:], ident, msk[:, m0:m0 + 512], start=False, stop=True),
                                  reads=[bmsk, b_cst], acc_writes=[bs])
                        st_[("s", i)] = (sp_, bs)

                    def mB(i, st_=st_):
                        sp_, bs = st_.pop(("s", i))
                        Pb, bP = Pr.next()
                        fw.op("act", lambda: A.activation(Pb[:], sp_[:], AF.Exp, scale=MLA_SCALE), reads=[bs], writes=[bP])
                        st_[("P", i)] = (Pb, bP)

                    def mC(i, nb=nb, st_=st_, Vt=Vt, bV=bV):
                        kb = i
                        Pb, bP = st_.pop(("P", i))
                        kw = {"writes": [bO]} if i == 0 else {"acc_writes": [bO]}
                        fw.op("pe", lambda: T.matmul(Op_[:], Vt[:, kb * 128:(kb + 1) * 128], Pb[:], start=(i == 0), stop=(i == nb - 1)),
                              reads=[bV, bP], **kw)
                        kw = {"writes": [bD]} if i == 0 else {"acc_writes": [bD]}
                        fw.op("pe", lambda: T.matmul(Dp[:], ones, Pb[:], start=(i == 0), stop=(i == nb - 1)),
                              reads=[bP, b_cst], **kw)

                    for t in range(nb + 2):
                        if t < nb:
                            mA(t)
                        if 0 <= t - 1 < nb:
                            mB(t - 1)
                        if 0 <= t - 2 < nb:
                            mC(t - 2)
                    rd, brd = rdr.next()
                    ot, bo = osr.next()
                    fw.op("dve", lambda rd=rd: V.reciprocal(rd[:], Dp[:]), reads=[bD], writes=[brd])
                    fw.op("dve", lambda rd=rd, ot=ot: V.tensor_tensor(ot[:], Op_[:], rd[:], ALU.mult), reads=[bO, brd], writes=[bo])
                    fw.dma("pool", OMscr[:, h, k * 512:(k + 1) * 512], ot[:], reads=[bo], acc_writes=[b_scr["OM"]])
            fw.barrier()
            fw.emit()

        NS = TC // 128
        with ExitStack() as ph:
            alloc = lambda n, sh, dt: ph.enter_context(nc.sbuf_tensor(uniq(n), sh, dt))
            palloc = lambda n, sh, dt: ph.enter_context(nc.psum_tensor(uniq(n), sh, dt))
            xnr = Ring(alloc, "xn", 2, [128, D], BF16)
            c = {
                "ss": Ring(alloc, "ss", 4, [128, 1], F32),
                "junk": xnr, "xn": xnr,
                "tp": Ring(palloc, "tp", 2, [128, 1024], BF16, excl=True),
            }
            xres = alloc("xres", [128, NS, D], F32)
            bxr = [Buf(f"xr{i}") for i in range(NS)]
            ysb = alloc("ysb", [128, NS, D], F32)
            bys = [Buf(f"ys{i}") for i in range(NS)]
            ysbf = ysb[:].rearrange("p s d -> p (s d)")
            big = alloc("big", [128, 32 * TC], BF16)
            b_big = Buf("big")
            osT = big[:, 0:8 * TC].rearrange("p (k t) -> p k t", k=8)
            omT = big[:, 8 * TC:16 * TC].rearrange("p (k t) -> p k t", k=8)
            uT = big[:].rearrange("p (k t) -> p k t", k=32)
            actT = [(alloc(f"aT{i}", [128, 16, TC], BF16), [Buf(f"aT{i}_{s}") for s in range(NS)]) for i in range(2)]
            gbr = Ring(alloc, "gb", 1, [128, D], F32)
            wring = Ring(alloc, "wr", 3, [128, 16 * 512], BF16)
            psr = Ring(palloc, "ps", 6, [128, 512], F32, excl=True)
            sgr = Ring(alloc, "sg", 3, [128, 512], F32)
            pf, bpf = alloc("pf", [128, NS, 256], F32), Buf("pf")
            pbf, bpbf = alloc("pbf", [128, NS, 256], BF16), Buf("pbf")
            pT, bpT = alloc("pT", [128, 2, TC], BF16), Buf("pT")
            evt = [0]

            def evac(dst, src, bsrc, kw, reads=()):
                evt[0] += 1
                if evt[0] % 2:
                    fw.op("act", lambda: A.copy(dst, src), reads=[bsrc] + list(reads), **kw)
                else:
                    fw.op("dve", lambda: V.tensor_copy(dst, src), reads=[bsrc] + list(reads), **kw)

            def load_w(name, k0, kn, col0, ncols):
                wt, bw = wring.next()
                v = wt[:, 0:kn * ncols].rearrange("p (k n) -> p k n", k=kn)
                fw.dma("sp", v, wb[name][:, k0:k0 + kn, col0:col0 + ncols], reads=[b_wb[name]], writes=[bw])
                return v, bw

            def load_g(i):
                gt, bg = gbr.next()
                fw.dma("sp", gt[:], grow[i:i + 1, :].partition_broadcast(128), writes=[bg])
                return gt, bg

            def proj_feat(wv, bw, wcols, KCn, rhsT, rbufs):
                pt, bp = psr.next()
                for kc in range(KCn):
                    fw.op("pe", lambda kc=kc: T.matmul(pt[:, 0:TC], wv[:, kc, wcols[0]:wcols[1]], rhsT[:, kc, :], start=(kc == 0), stop=(kc == KCn - 1)),
                          reads=[bw] + rbufs, **({"writes": [bp]} if kc == 0 else {"acc_writes": [bp]}))
                return pt, bp

            def tok_mm(pt, bp, lhsT, lbufs, st, wv, bw, kcs, first, last):
                for j, (kc_l, kc_w) in enumerate(kcs):
                    fw.op("pe", lambda kc_l=kc_l, kc_w=kc_w, j=j: T.matmul(pt[:], lhsT[:, kc_l, st * 128:(st + 1) * 128], wv[:, kc_w, :],
                                                                  start=(first and j == 0), stop=(last and j == len(kcs) - 1), skip_group_check=True),
                          reads=[bw] + lbufs, **({"writes": [bp]} if (first and j == 0) else {"acc_writes": [bp]}))

            def post_norm(gi):
                gt, bg = load_g(gi)
                for st in range(NS):
                    ss, bss = c["ss"].next()
                    jk, bjk = c["junk"].next()
                    fw.op("dve", lambda ss=ss: V.memset(ss[:], 0.0), writes=[bss])
                    fw.op("act", lambda jk=jk, ss=ss, st=st: A.activation(jk[:], ysb[:, st, :], AF.Square, accum_out=ss[:]), reads=[bys[st], bss], writes=[bjk, bss])
                    fw.op("act", lambda ss=ss: A.activation(ss[:], ss[:], AF.Sqrt, bias=EPS, scale=1.0 / D), reads=[bss], writes=[bss])
                    fw.op("dve", lambda ss=ss: V.reciprocal(ss[:], ss[:]), reads=[bss], writes=[bss])
                    fw.op("dve", lambda ss=ss, st=st: V.scalar_tensor_tensor(ysb[:, st, :], ysb[:, st, :], ss[:, 0:1], gt[:], op0=ALU.mult, op1=ALU.mult),
                          reads=[bys[st], bss, bg], writes=[bys[st]])

            def residual_add():
                for st in range(NS):
                    fw.op("dve", lambda st=st: V.tensor_tensor(xres[:, st, :], xres[:, st, :], ysb[:, st, :], ALU.add), reads=[bxr[st], bys[st]], writes=[bxr[st]])

            for cs in range(cfg["ncs"] if "C" in cfg["phases"] else 0):
                r0 = cs * TC
                xd = [(xres[:, st, :], bxr[st]) for st in range(NS)]
                hT, hb = actT[0]
                front_end(c, lambda st, r0=r0: x_own[r0 + st * 128: r0 + (st + 1) * 128, :], NS, hT, hb, xdst=xd)
                fw.dma("sp", osT, OSscr[:, :, r0:r0 + TC], reads=[b_scr["OS"]], writes=[b_big])
                fw.dma("sp", omT, OMscr[:, :, r0:r0 + TC], reads=[b_scr["OM"]], acc_writes=[b_big])
                mT, mb = actT[1]
                bm_all = Buf("mixedT")
                for og in range(4):
                    wo, bwo = wring.next()
                    wov = wo[:].rearrange("p (a k n) -> p a k n", a=2, k=8)
                    fw.dma("sp", wov[:, 0], wb["wsbo"][:, :, og * 512:(og + 1) * 512], reads=[b_wb["wsbo"]], writes=[bwo])
                    fw.dma("sp", wov[:, 1], wb["wmlao"][:, :, og * 512:(og + 1) * 512], reads=[b_wb["wmlao"]], acc_writes=[bwo])
                    for gsel in range(2):
                        wg, bwg = load_w("wgs" if gsel == 0 else "wgm", 0, 16, og * 512, 512)
                        for oo in range(4):
                            oc = og * 4 + oo
                            cols = (oo * 128, oo * 128 + 128)
                            pa, bpa = proj_feat(wov[:, gsel], bwo, cols, 8, osT if gsel == 0 else omT, [b_big])
                            pg, bpg = proj_feat(wg, bwg, cols, 16, hT, hb)
                            sg, bsg = sgr.next()
                            fw.op("act", lambda sg=sg, pg=pg: A.activation(sg[:, 0:TC], pg[:, 0:TC], AF.Sigmoid), reads=[bpg], writes=[bsg])
                            fw.op("dve", lambda sg=sg, pa=pa: V.tensor_tensor(sg[:, 0:TC], sg[:, 0:TC], pa[:, 0:TC], ALU.mult), reads=[bsg, bpa], writes=[bsg])
                            if gsel == 0:
                                fw.op("dve", lambda sg=sg, oc=oc: V.tensor_copy(ysbf[:, oc * TC:(oc + 1) * TC], sg[:, 0:TC]), reads=[bsg],
                                      acc_writes=[bys[0]])
                            else:
                                fw.op("dve", lambda sg=sg, oc=oc: V.tensor_tensor(mT[:, oc, :], sg[:, 0:TC], ysbf[:, oc * TC:(oc + 1) * TC], ALU.add),
                                      reads=[bsg, bys[0]], acc_writes=[bm_all])
                if dbgC and cs == 0:
                    fw.dma("pool", d_mixed, mT[:], reads=[bm_all], acc_writes=[b_dbg])
                for cg in range(4):
                    wv, bw = load_w("wout", 0, 16, cg * 512, 512)
                    for st in range(NS):
                        pt, bp = psr.next()
                        tok_mm(pt, bp, mT, [bm_all], st, wv, bw, [(kc, kc) for kc in range(16)], True, True)
                        evac(ysb[:, st, cg * 512:(cg + 1) * 512], pt[:], bp, {"writes": [bys[st]]} if cg == 0 else {"acc_writes": [bys[st]]},
                             reads=[bm_all] if cg == 0 else [])
                if dbgC and cs == 0:
                    fw.dma("pool", d_y, ysb[:], reads=bys, acc_writes=[b_dbg])
                post_norm(0)
                residual_add()
                if dbgC and cs == 0:
                    fw.dma("pool", d_x1, xres[:], reads=bxr, acc_writes=[b_dbg])
                h2T, h2b = actT[0]
                front_end(c, None, NS, h2T, h2b, xdst=xd)
                for fh in range(2):
                    for fg in range(8):
                        wv, bw = load_w("wup", 0, 16, (fh * 8 + fg) * 512, 512)
                        for fc in range(4):
                            pt, bp = proj_feat(wv, bw, (fc * 128, fc * 128 + 128), 16, h2T, h2b)
                            sg, bsg = sgr.next()
                            fw.op("act", lambda sg=sg, pt=pt: A.activation(sg[:, 0:TC], pt[:, 0:TC], AF.Relu), reads=[bp], writes=[bsg])
                            fw.op("dve", lambda sg=sg, f=fg * 4 + fc: V.tensor_tensor(uT[:, f, :], sg[:, 0:TC], sg[:, 0:TC], ALU.mult), reads=[bsg],
                                  **({"writes": [b_big]} if (fg == 0 and fc == 0) else {"acc_writes": [b_big]}))
                    for cg in range(4):
                        accs = [psr.next() for _ in range(NS)]
                        for pc in range(4):
                            wv, bw = load_w("wdown", fh * 32 + pc * 8, 8, cg * 512, 512)
                            for st in range(NS):
                                tok_mm(accs[st][0], accs[st][1], uT, [b_big], st, wv, bw, [(pc * 8 + j, j) for j in range(8)], pc == 0, pc == 3)
                        for st in range(NS):
                            dst = ysb[:, st, cg * 512:(cg + 1) * 512]
                            if fh == 0:
                                evac(dst, accs[st][0][:], accs[st][1], {"writes": [bys[st]]} if cg == 0 else {"acc_writes": [bys[st]]})
                            else:
                                fw.op("dve", lambda dst=dst, ps_=accs[st][0]: V.tensor_tensor(dst, ps_[:], dst, ALU.add), reads=[accs[st][1], bys[st]],
                                      **({"writes": [bys[st]]} if cg == 0 else {"acc_writes": [bys[st]]}))
                post_norm(1)
                residual_add()
                if dbgC and cs == 0:
                    fw.dma("pool", d_x2, xres[:], reads=bxr, acc_writes=[b_dbg])
                fw.dma("sp", pf[:], p_own[r0:r0 + TC, :].rearrange("(s p) d -> p s d", p=128), writes=[bpf])
                fw.op("dve", lambda: V.tensor_copy(pbf[:], pf[:]), reads=[bpf], writes=[bpbf])
                tp, btp = c["tp"].next()
                for st in range(NS):
                    for kc in range(2):
                        j = st * 2 + kc
                        fw.op("pe", lambda st=st, kc=kc, j=j, tp=tp: T.transpose(tp[:, j * 128:(j + 1) * 128], pbf[:, st, kc * 128:(kc + 1) * 128], ident),
                              reads=[bpbf, b_cst], **({"writes": [btp]} if j == 0 else {"acc_writes": [btp]}))
                for st in range(NS):
                    fw.op("act", lambda st=st, tp=tp: A.copy(pT[:, :, st * 128:(st + 1) * 128], tp[:, st * 256:(st + 1) * 256].rearrange("p (k t) -> p k t", k=2)),
                          reads=[btp], **({"writes": [bpT]} if st == 0 else {"acc_writes": [bpT]}))
                for cg in range(4):
                    wv, bw = load_w("wple", 0, 2, cg * 512, 512)
                    for st in range(NS):
                        pt, bp = psr.next()
                        tok_mm(pt, bp, pT, [bpT], st, wv, bw, [(0, 0), (1, 1)], True, True)
                        evac(ysb[:, st, cg * 512:(cg + 1) * 512], pt[:], bp, {"writes": [bys[st]]} if cg == 0 else {"acc_writes": [bys[st]]})
                post_norm(2)
                if dbgC and cs == 0:
                    fw.dma("pool", d_e, ysb[:], reads=bys, acc_writes=[b_dbg])
                x2T, x2b = actT[1]
                front_end(c, None, NS, x2T, x2b, xdst=xd, norm=False)
                for cg in range(4):
                    wv, bw = load_w("wpg", 0, 16, cg * 512, 512)
                    for st in range(NS):
                        pt, bp = psr.next()
                        tok_mm(pt, bp, x2T, [x2b[st]], st, wv, bw, [(kc, kc) for kc in range(16)], True, True)
                        sg, bsg = sgr.next()
                        sl = slice(cg * 512, (cg + 1) * 512)
                        fw.op("act", lambda sg=sg, pt=pt: A.activation(sg[:], pt[:], AF.Sigmoid), reads=[bp], writes=[bsg])
                        fw.op("dve", lambda sg=sg, st=st, sl=sl: V.tensor_tensor(sg[:], sg[:], ysb[:, st, sl], ALU.mult), reads=[bsg, bys[st]], writes=[bsg])
                        fw.op("dve", lambda sg=sg, st=st, sl=sl: V.tensor_tensor(ysb[:, st, sl], sg[:], xres[:, st, sl], ALU.add), reads=[bsg, bxr[st]], writes=[bys[st]])
                for st in range(NS):
                    fw.dma("pool", out[r0 + st * 128: r0 + (st + 1) * 128, :], ysb[:, st, :], reads=[bys[st]], acc_writes=[b_scr["out"]])
            fw.barrier()
            st = fw.emit()
            print("program stats", st, flush=True)
    return nc


def _arr(w, kc):
    k, n = w.shape
    return np.ascontiguousarray(w.reshape(kc, 128, n).transpose(1, 0, 2))


def _masks(j, strict):
    sidx = np.arange(128)[:, None]
    tq = np.arange(512)[None, :]
    out = np.zeros((128, 16, 512), np.float32)
    for q in range(2):
        is_max = (j == 1) if q == 0 else (j == 0)
        for r in range(8):
            if is_max:
                m = None if r < 4 else r - 4
                allneg = False
            else:
                m = r if r < 4 else None
                allneg = r >= 4
            if allneg:
                blk = np.full((128, 512), NEG, np.float32)
            elif m is None:
                blk = np.zeros((128, 512), np.float32)
            else:
                vis = (128 * m + sidx) < tq if strict else (128 * m + sidx) <= tq
                blk = np.where(vis, 0.0, NEG).astype(np.float32)
            out[:, q * 8 + r, :] = blk
    return out.reshape(128, 16 * 512)


_PROG = {}


def _prep(x, p, positions, g_pre_mix, w_in, g_cq, g_ckv, w_q_up, w_kv_up, w_sb_o, w_mla_o, w_out,
           g_post_mix, g_pre_mlp, w_up, w_down, g_post_mlp, w_ple, g_ple, w_ple_gate):
    f = lambda a: np.asarray(a, dtype=np.float32)
    x, p = f(x), f(p)
    positions = np.asarray(positions).astype(np.int32)
    w_in0 = f(w_in)[0]
    kr = w_in0[:, 4096:4160]
    wqu = f(w_q_up)[0].reshape(512, 8, 192)
    rope = wqu[:, :, 128:192]
    wkv = f(w_kv_up)[0].reshape(512, 8, 256)
    shared = {
        "wq_f": _arr(w_in0[:, 0:1024], 16), "wk_f": _arr(w_in0[:, 1024:2048], 16), "wv_f": _arr(w_in0[:, 2048:3072], 16),
        "wcq_f": _arr(w_in0[:, 3072:3584], 16), "wckv_f": _arr(w_in0[:, 3584:4096], 16),
        "wkr_f": _arr(np.concatenate([kr, kr[:, 32:64], kr[:, 0:32]], axis=1), 16),
        "wgs_f": _arr(w_in0[:, 4160:6208], 16), "wgm_f": _arr(w_in0[:, 6208:8256], 16),
        "wqn_f": _arr(np.ascontiguousarray(wqu[:, :, 0:128]).reshape(512, 1024), 4),
        "wqr_f": _arr(np.concatenate([rope, rope[:, :, 32:64], rope[:, :, 0:32]], axis=2).reshape(512, 1024), 4),
        "wkn_f": _arr(np.ascontiguousarray(wkv[:, :, 0:128]).reshape(512, 1024), 4),
        "wvm_f": _arr(np.ascontiguousarray(wkv[:, :, 128:256]).reshape(512, 1024), 4),
        "wsbo_f": _arr(f(w_sb_o)[0], 8), "wmlao_f": _arr(f(w_mla_o)[0], 8), "wout_f": _arr(f(w_out)[0], 16),
        "wup_f": _arr(f(w_up)[0], 16), "wdown_f": _arr(f(w_down)[0], 64), "wple_f": _arr(f(w_ple)[0], 2),
        "wpg_f": _arr(f(w_ple_gate)[0], 16),
        "gpm": np.ascontiguousarray(f(g_pre_mix)[0].reshape(16, 128).T), "gcq": np.ascontiguousarray(f(g_cq)[0].reshape(4, 128).T),
        "gckv": np.ascontiguousarray(f(g_ckv)[0].reshape(4, 128).T), "gmlp": np.ascontiguousarray(f(g_pre_mlp)[0].reshape(16, 128).T),
        "grow": np.stack([f(g_post_mix)[0], f(g_post_mlp)[0], f(g_ple)[0]], axis=0),
    }
    jj = np.arange(128)[:, None]
    ss_ = np.arange(128)[None, :]
    shared["consts"] = np.concatenate([np.eye(128), (jj >= ss_), (jj < ss_), np.ones((128, 128))], axis=1).astype(np.float32)
    inv_freq = (np.float32(10000.0) ** (-np.arange(32, dtype=np.float32) / np.float32(32))).astype(np.float32)
    shared["invf"] = np.concatenate([inv_freq, inv_freq])[:, None].astype(np.float32)
    in_maps = []
    for c in range(NCORES):
        b, j = c // 2, c % 2
        tiles = SLOT_TILES[j]
        rows = np.concatenate([np.arange(t * 512, (t + 1) * 512) for t in tiles])
        m = dict(shared)
        m["x_all"] = np.ascontiguousarray(x[b])
        m["x_own"] = np.ascontiguousarray(x[b][rows])
        m["p_own"] = np.ascontiguousarray(p[0, b][rows])
        m["pos_all"] = np.ascontiguousarray(positions[b][None, :])
        m["pos_own"] = np.ascontiguousarray(positions[b][rows][None, :])
        m["mask_sb"] = _masks(j, True)
        m["mask_ml"] = _masks(j, False)
        in_maps.append(m)
    return in_maps


def kernel(**inputs):
    in_maps = _prep(**inputs)
    if "nc" not in _PROG:
        _PROG["nc"] = build_program()
    res = run_bass_kernel_spmd(_PROG["nc"], in_maps, core_ids=list(range(NCORES)))
    outp = np.empty((4, S, D), np.float32)
    for c in range(NCORES):
        b, j = c // 2, c % 2
        o = np.asarray(res.results[c]["out"])
        for i, t in enumerate(SLOT_TILES[j]):
            outp[b, t * 512:(t + 1) * 512] = o[i * 512:(i + 1) * 512]
    return outp
```

```python
import numpy as np
from contextlib import ExitStack
import concourse.bass as bass
import concourse.mybir as mybir
from concourse.bass_utils import run_bass_kernel_spmd

F32 = mybir.dt.float32
BF16 = mybir.dt.bfloat16
I32 = mybir.dt.int32
AF = mybir.ActivationFunctionType
ALU = mybir.AluOpType
AX = mybir.AxisListType


class Buf:
    __slots__ = ("name", "writers", "readers", "war", "excl")

    def __init__(self, name="", excl=False):
        self.name = name
        self.writers = {}
        self.readers = {}
        self.war = {}
        self.excl = excl


class _Op:
    __slots__ = ("eng", "fn", "deps", "is_dma", "signal", "need_signal", "slot", "idx")


SEM_LIMIT = 30000


class FW:
    ENGS = ("pe", "act", "dve", "pool", "sp")
    NDMASEM = 24

    def __init__(self, nc, es):
        self.nc = nc
        self.es = es
        self.ops = []
        self.engobj = {"pe": nc.tensor, "act": nc.scalar, "dve": nc.vector, "pool": nc.gpsimd, "sp": nc.sync}
        self.nsem = 0
        self.barrier_idx = 0
        self.emitted = 0
        self.slot_prev = [None] * self.NDMASEM
        self.ndma = 0
        self.engsem = {}
        self.engcnt = {}
        self.waited = {e: {} for e in self.ENGS}
        self.nwait = 0
        self.nsig = 0
        self.last_on_eng = {}
        self.dma_since_barrier = []
        self.serial = False

    def new_sem(self, name):
        self.nsem += 1
        return self.es.enter_context(self.nc.semaphore(f"{name}_{self.nsem}"))

    def _record(self, eng, fn, reads, writes, is_dma, acc_writes=()):
        op = _Op()
        op.eng = eng
        op.fn = fn
        op.is_dma = is_dma
        op.signal = None
        op.need_signal = is_dma
        op.slot = None
        op.idx = idx = len(self.ops)
        key = ("dma", idx) if is_dma else eng
        deps = set()
        bi = self.barrier_idx
        for b in reads:
            for w in b.writers.values():
                if w >= bi:
                    deps.add(w)
            if b.excl:
                for r in b.readers.values():
                    if r >= bi:
                        deps.add(r)
        for b in writes:
            for r in b.readers.values():
                if r >= bi:
                    deps.add(r)
            for w in b.writers.values():
                if w >= bi:
                    deps.add(w)
            for r in b.war.values():
                if r >= bi:
                    deps.add(r)
        for b in acc_writes:
            for r in b.readers.values():
                if r >= bi:
                    deps.add(r)
            for r in b.war.values():
                if r >= bi:
                    deps.add(r)
        for b in reads:
            b.readers[key] = idx
        for b in writes:
            b.war = b.readers
            b.writers = {key: idx}
            b.readers = {}
        for b in acc_writes:
            if b.readers:
                b.war = b.readers
                b.writers = {key: idx}
                b.readers = {}
            else:
                b.writers[key] = idx
        if self.serial and idx > 0 and idx - 1 >= bi:
            deps.add(idx - 1)
        deps.discard(idx)
        op.deps = deps
        self.ops.append(op)
        if fn is not None:
            if is_dma:
                self.dma_since_barrier.append(idx)
            else:
                self.last_on_eng[eng] = idx
        return op

    def op(self, eng, fn, reads=(), writes=(), acc_writes=()):
        return self._record(eng, fn, reads, writes, False, acc_writes)

    def dma(self, eng, out, in_, reads=(), writes=(), acc_writes=()):
        e = self.engobj[eng]
        return self._record(eng, lambda: e.dma_start(out=out, in_=in_), reads, writes, True, acc_writes)

    def barrier(self, engs=None):
        deps = set(self.last_on_eng.values()) | set(self.dma_since_barrier)
        deps = {d for d in deps if d >= self.barrier_idx}
        for eng in (engs or self.ENGS):
            op = _Op()
            op.eng = eng
            op.fn = None
            op.is_dma = False
            op.signal = None
            op.need_signal = False
            op.slot = None
            op.idx = len(self.ops)
            op.deps = set(deps)
            self.ops.append(op)
        self.barrier_idx = len(self.ops)
        self.dma_since_barrier = []
        self.last_on_eng = {}

    def emit(self):
        ops = self.ops
        lo = self.emitted
        for op in ops[lo:]:
            if op.is_dma:
                s = self.ndma % self.NDMASEM
                if self.slot_prev[s] is not None:
                    op.deps.add(self.slot_prev[s])
                self.slot_prev[s] = op.idx
                op.slot = s
                self.ndma += 1
        for op in ops[lo:]:
            for d in op.deps:
                p = ops[d]
                if p.is_dma or op.is_dma or p.eng != op.eng or op.eng != "pe":
                    assert d >= lo or p.signal is not None or p.fn is None, "dep on already-emitted unsignalled op"
                    p.need_signal = True
        engsem, engcnt = self.engsem, self.engcnt
        for op in ops[lo:]:
            e = self.engobj[op.eng]
            need = {}
            for d in op.deps:
                p = ops[d]
                if p.signal is None:
                    continue
                if (not p.is_dma) and (not op.is_dma) and p.eng == op.eng and op.eng == "pe":
                    continue
                sem, val = p.signal
                k = id(sem)
                if k not in need or need[k][1] < val:
                    need[k] = (sem, val)
            wc = self.waited[op.eng]
            for k, (sem, val) in need.items():
                if wc.get(k, 0) >= val:
                    continue
                e.wait_ge(sem, val)
                wc[k] = val
                self.nwait += 1
            if op.fn is None:
                continue
            ins = op.fn()
            op.fn = True
            if op.need_signal:
                if op.is_dma:
                    key = ("dma", op.slot)
                    inc = 16
                else:
                    key = op.eng
                    inc = 1
                if key not in engsem or engcnt[key] + inc > SEM_LIMIT:
                    engsem[key] = self.new_sem("d" if op.is_dma else op.eng)
                    engcnt[key] = 0
                engcnt[key] += inc
                ins.then_inc(engsem[key], inc)
                op.signal = (engsem[key], engcnt[key])
                self.nsig += 1
        self.emitted = len(ops)
        return dict(nops=len(ops), nwait=self.nwait, nsig=self.nsig, nsem=self.nsem)


NCORES = 8
S = 8192
D = 2048
TA = 512
NT = S // TA
NSLOT = 8
TC = 512
EPS = 1e-6
SLOT_TILES = {0: [0, 3, 4, 7, 8, 11, 12, 15], 1: [1, 2, 5, 6, 9, 10, 13, 14]}
NEG = -30000.0
SB_SCALE = 128 ** -0.5
MLA_SCALE = 192 ** -0.5

WSPEC = [
    ("wq", 16, 1024, "gpm", []), ("wk", 16, 1024, "gpm", []), ("wv", 16, 1024, "gpm", []),
    ("wcq", 16, 512, "gpm", []), ("wckv", 16, 512, "gpm", []), ("wkr", 16, 128, "gpm", [(64, 96)]),
    ("wgs", 16, 2048, "gpm", []), ("wgm", 16, 2048, "gpm", []),
    ("wqn", 4, 1024, "gcq", []), ("wqr", 4, 1024, "gcq", [(h * 128 + 64, h * 128 + 96) for h in range(8)]),
    ("wkn", 4, 1024, "gckv", []), ("wvm", 4, 1024, "gckv", []),
    ("wsbo", 8, 2048, None, []), ("wmlao", 8, 2048, None, []), ("wout", 16, 2048, None, []),
    ("wup", 16, 8192, "gmlp", []), ("wdown", 64, 2048, None, []), ("wple", 2, 2048, None, []),
    ("wpg", 16, 2048, None, []),
]
GSPEC = {"gpm": 16, "gcq": 4, "gckv": 4, "gmlp": 16}


class Ring:
    def __init__(self, alloc, name, n, shape, dt, excl=False):
        self.items = [(alloc(f"{name}{i}", shape, dt), Buf(f"{name}{i}", excl)) for i in range(n)]
        self.i = 0

    def next(self):
        it = self.items[self.i % len(self.items)]
        self.i += 1
        return it


def build_program(cfg=None):
    cfg = dict(dict(nkt=NT, nslotA=NSLOT, hsb=8, hml=8, nslotB=NSLOT, ncs=(S // 2) // TC, phases="0ABMC", dbg=()), **(cfg or {}))
    nc = bass.Bass("TRN2", target_bir_lowering=False)

    def din(name, shape, dt=F32):
        return nc.dram_tensor(name, shape, dt, kind="ExternalInput").ap()

    def dscr(name, shape, dt=BF16):
        if name in cfg["dbg"]:
            return nc.dram_tensor(name, shape, dt, kind="ExternalOutput").ap()
        return nc.dram_tensor(name, shape, dt).ap()

    x_all = din("x_all", [S, D])
    x_own = din("x_own", [S // 2, D])
    p_own = din("p_own", [S // 2, 256])
    pos_all = din("pos_all", [1, S], I32)
    pos_own = din("pos_own", [1, S // 2], I32)
    consts = din("consts", [128, 4 * 128])
    invf = din("invf", [64, 1])
    mask_sb = din("mask_sb", [128, 16 * 512])
    mask_ml = din("mask_ml", [128, 16 * 512])
    grow = din("grow", [3, D])
    gin = {g: din(g, [128, kc]) for g, kc in GSPEC.items()}
    win = {n: din(n + "_f", [128, kc, nn]) for n, kc, nn, _, _ in WSPEC}
    out = nc.dram_tensor("out", [S // 2, D], F32, kind="ExternalOutput").ap()

    wb = {n: dscr(n + "_b", [128, kc, nn]) for n, kc, nn, _, _ in WSPEC}
    Kscr = dscr("Kscr", [128, 8, S])
    Vscr = dscr("Vscr", [8, 128, 64, 128])
    KNscr = dscr("KNscr", [128, 8, S])
    VMscr = dscr("VMscr", [8, 128, 64, 128])
    KRscr = dscr("KRscr", [64, S])
    Qscr = dscr("Qscr", [128, 8, S // 2])
    QNscr = dscr("QNscr", [128, 8, S // 2])
    QRscr = dscr("QRscr", [64, 8, S // 2])
    OSscr = dscr("OSscr", [128, 8, S // 2])
    OMscr = dscr("OMscr", [128, 8, S // 2])
    dbgC = cfg.get("dbgC", False)
    if dbgC:
        d_mixed = nc.dram_tensor("d_mixed", [128, 16, TC], BF16, kind="ExternalOutput").ap()
        d_x1 = nc.dram_tensor("d_x1", [128, TC // 128, D], F32, kind="ExternalOutput").ap()
        d_x2 = nc.dram_tensor("d_x2", [128, TC // 128, D], F32, kind="ExternalOutput").ap()
        d_e = nc.dram_tensor("d_e", [128, TC // 128, D], F32, kind="ExternalOutput").ap()
        d_y = nc.dram_tensor("d_y", [128, TC // 128, D], F32, kind="ExternalOutput").ap()
        b_dbg = Buf("dbg")
    b_wb = {n: Buf(n) for n in wb}
    b_scr = {n: Buf(n) for n in ["K", "V", "KN", "VM", "KR", "Q", "QN", "QR", "OS", "OM", "out"]}

    _uid = [0]

    def uniq(n):
        _uid[0] += 1
        return f"{n}_{_uid[0]}"

    with ExitStack() as es:
        fw = FW(nc, es)
        fw.serial = bool(cfg.get('serial', False))
        V, A, P, T = nc.vector, nc.scalar, nc.gpsimd, nc.tensor

        galloc = lambda n, sh, dt: es.enter_context(nc.sbuf_tensor(n, sh, dt))
        cst = galloc("cst", [128, 512], BF16)
        ident, uincl, ubar, ones = cst[:, 0:128], cst[:, 128:256], cst[:, 256:384], cst[:, 384:512]
        gT = {g: galloc(g + "_t", [128, kc], F32) for g, kc in GSPEC.items()}
        gTn = {g: galloc(g + "_n", [128, GSPEC[g]], F32) for g in ("gpm", "gcq")}
        invf_t = galloc("invf_t", [64, 1], F32)
        b_cst = Buf("cst")
        with ExitStack() as ph:
            alloc = lambda n, sh, dt: ph.enter_context(nc.sbuf_tensor(uniq(n), sh, dt))
            cf = alloc("cf", [128, 512], F32)
            b_cf = Buf("cf")
            fw.dma("sp", cf[:], consts, writes=[b_cf])
            fw.op("dve", lambda: V.tensor_copy(cst[:], cf[:]), reads=[b_cf], writes=[b_cst])
            for g in GSPEC:
                fw.dma("sp", gT[g][:], gin[g], writes=[b_cst])
            fw.dma("sp", invf_t[:], invf, writes=[b_cst])
            fw.barrier()
            for g in gTn:
                fw.op("dve", lambda g=g: V.tensor_scalar(gTn[g][:], gT[g][:], -1.0, None, op0=ALU.mult),
                      reads=[b_cst], writes=[b_cst])

            stf = Ring(alloc, "stf", 2, [128, 4096], F32)
            stb = Ring(alloc, "stb", 2, [128, 4096], BF16)
            tog = 0
            for name, KC, N, gname, negs in (WSPEC if "0" in cfg["phases"] else []):
                if N >= 4096:
                    chunks = [(kc, 1, n0, min(4096, N - n0)) for kc in range(KC) for n0 in range(0, N, 4096)]
                else:
                    kcn = max(1, 4096 // N)
                    chunks = [(k0, min(kcn, KC - k0), 0, N) for k0 in range(0, KC, kcn)]
                for k0, kn, n0, nn in chunks:
                    (tf, bf), (tb, bb) = stf.next(), stb.next()
                    fv = tf[:, 0:kn * nn].rearrange("p (k n) -> p k n", k=kn)
                    bv = tb[:, 0:kn * nn].rearrange("p (k n) -> p k n", k=kn)
                    fw.dma("sp", fv, win[name][:, k0:k0 + kn, n0:n0 + nn], writes=[bf])
                    if gname is None:
                        if tog % 2 == 0:
                            fw.op("act", lambda tb=tb, tf=tf, m=kn * nn: A.copy(tb[:, 0:m], tf[:, 0:m]), reads=[bf], writes=[bb])
                        else:
                            fw.op("dve", lambda tb=tb, tf=tf, m=kn * nn: V.tensor_copy(tb[:, 0:m], tf[:, 0:m]), reads=[bf], writes=[bb])
                        tog += 1
                    else:
                        first = True
                        for kk in range(kn):
                            kc = k0 + kk
                            segs, c = [], 0
                            for lo, hi in negs:
                                if lo > c:
                                    segs.append((c, lo, 1))
                                segs.append((lo, hi, -1))
                                c = hi
                            if c < nn:
                                segs.append((c, nn, 1))
                            for lo, hi, sg in segs:
                                sc = (gT if sg > 0 else gTn)[gname][:, kc:kc + 1]
                                o_ap, i_ap = tb[:, kk * nn + lo:kk * nn + hi], tf[:, kk * nn + lo:kk * nn + hi]
                                kw = {"writes": [bb]} if first else {"acc_writes": [bb]}
                                first = False
                                if tog % 2 == 0:
                                    fw.op("act", lambda o=o_ap, i=i_ap, sc=sc: A.activation(o, i, AF.Copy, scale=sc), reads=[bf, b_cst], **kw)
                                else:
                                    fw.op("dve", lambda o=o_ap, i=i_ap, sc=sc: V.tensor_scalar(o, i, sc, None, op0=ALU.mult), reads=[bf, b_cst], **kw)
                                tog += 1
                    fw.dma("pool", wb[name][:, k0:k0 + kn, n0:n0 + nn], bv, reads=[bb], acc_writes=[b_wb[name]])
            fw.barrier()
            fw.emit()

        def front_end(alloc_ctx, xsrc, nsub, hT, hbufs, xdst=None, norm=True):
            c = alloc_ctx
            for st in range(nsub):
                if xdst is None:
                    xt, bx = c["xs"].next()
                    xap = xt[:]
                else:
                    xap, bx = xdst[st]
                if xsrc is not None:
                    fw.dma("sp", xap, xsrc(st), writes=[bx])
                xn, bxn = c["xn"].next()
                if not norm:
                    fw.op("dve", lambda xn=xn, xap=xap: V.tensor_copy(xn[:], xap), reads=[bx], writes=[bxn])
                ss, bss = c["ss"].next()
                jk, bjk = c["junk"].next()
                if norm:
                  fw.op("dve", lambda ss=ss: V.memset(ss[:], 0.0), writes=[bss])
                  fw.op("act", lambda jk=jk, xap=xap, ss=ss: A.activation(jk[:], xap, AF.Square, accum_out=ss[:]),
                      reads=[bx, bss], writes=[bjk, bss])
                  fw.op("act", lambda ss=ss: A.activation(ss[:], ss[:], AF.Sqrt, bias=EPS, scale=1.0 / D),
                      reads=[bss], writes=[bss])
                  fw.op("dve", lambda ss=ss: V.reciprocal(ss[:], ss[:]),
                      reads=[bss], writes=[bss])
                  fw.op("dve", lambda xn=xn, xap=xap, ss=ss: V.tensor_scalar(xn[:], xap, ss[:, 0:1], None, op0=ALU.mult),
                      reads=[bx, bss], writes=[bxn])
                for half in range(2):
                    tp, btp = c["tp"].next()
                    for q in range(8):
                        kc = half * 8 + q
                        fw.op("pe", lambda tp=tp, xn=xn, q=q, kc=kc: T.transpose(tp[:, q * 128:(q + 1) * 128], xn[:, kc * 128:(kc + 1) * 128], ident),
                              reads=[bxn, b_cst], **({"writes": [btp]} if q == 0 else {"acc_writes": [btp]}))
                    dst = hT[:, half * 8:half * 8 + 8, st * 128:(st + 1) * 128]
                    src = tp[:].rearrange("p (k t) -> p k t", k=8)
                    kw = {"writes": [hbufs[st]]} if half == 0 else {"acc_writes": [hbufs[st]]}
                    if half == 0:
                        fw.op("act", lambda dst=dst, src=src: A.copy(dst, src), reads=[btp], **kw)
                    else:
                        fw.op("dve", lambda dst=dst, src=src: V.tensor_copy(dst, src), reads=[btp], **kw)

        def rope_tables(c, pos_src, cos2, sin2, b_cs):
            TWO_PI = float(2 * np.pi)
            C1 = 6.28125
            C2 = TWO_PI - C1
            pi_t, bpi = c["posi"].next()
            fw.dma("sp", pi_t[:], pos_src.partition_broadcast(64), writes=[bpi])
            ang, bang = c["ang"].next()
            y, by = c["ang"].next()
            r, br = c["ang"].next()
            m, bm = c["ang"].next()
            ni, bni = c["posi"].next()
            fw.op("dve", lambda: V.tensor_copy(ang[:], pi_t[:]), reads=[bpi], writes=[bang])
            fw.op("dve", lambda: V.tensor_scalar(ang[:], ang[:], invf_t[:, 0:1], None, op0=ALU.mult), reads=[bang, b_cst], writes=[bang])
            fw.op("dve", lambda: V.tensor_scalar(y[:], ang[:], 1.0 / TWO_PI, 0.5, op0=ALU.mult, op1=ALU.add), reads=[bang], writes=[by])
            fw.op("dve", lambda: V.tensor_copy(ni[:], y[:]), reads=[by], writes=[bni])
            fw.op("dve", lambda: V.tensor_copy(y[:], ni[:]), reads=[bni], writes=[by])
            fw.op("dve", lambda: V.scalar_tensor_tensor(r[:], y[:], -C1, ang[:], op0=ALU.mult, op1=ALU.add), reads=[by, bang], writes=[br])
            fw.op("dve", lambda: V.scalar_tensor_tensor(r[:], y[:], -C2, r[:], op0=ALU.mult, op1=ALU.add), reads=[by, br], writes=[br])

            def wrap(t, bt):
                fw.op("dve", lambda: V.tensor_scalar(m[:], t[:], float(-np.pi), None, op0=ALU.is_lt), reads=[bt], writes=[bm])
                fw.op("dve", lambda: V.scalar_tensor_tensor(t[:], m[:], TWO_PI, t[:], op0=ALU.mult, op1=ALU.add), reads=[bm, bt], writes=[bt])
                fw.op("dve", lambda: V.tensor_scalar(m[:], t[:], float(np.pi), None, op0=ALU.is_gt), reads=[bt], writes=[bm])
                fw.op("dve", lambda: V.scalar_tensor_tensor(t[:], m[:], -TWO_PI, t[:], op0=ALU.mult, op1=ALU.add), reads=[bm, bt], writes=[bt])
                fw.op("dve", lambda: V.tensor_scalar(t[:], t[:], float(-np.pi), float(np.pi), op0=ALU.max, op1=ALU.min), reads=[bt], writes=[bt])

            wrap(r, br)
            fw.op("act", lambda: A.activation(sin2[:], r[:], AF.Sin), reads=[br], writes=[b_cs])
            fw.op("dve", lambda: V.tensor_scalar(y[:], r[:], float(np.pi / 2), None, op0=ALU.add), reads=[br], writes=[by])
            wrap(y, by)
            fw.op("act", lambda: A.activation(cos2[:], y[:], AF.Sin), reads=[by], writes=[b_cs])

        def featmajor_norm(c, ps_list, bps_list, outT, b_out, psb, bpsb):
            cf_t, bcf = c["cfm"].next()
            sq_t, bsq = c["sq"].next()
            for i in range(4):
                fw.op("dve", lambda i=i: V.tensor_copy(cf_t[:, i * 512:(i + 1) * 512], ps_list[i][:]), reads=[bps_list[i]],
                      **({"writes": [bcf]} if i == 0 else {"acc_writes": [bcf]}))
                fw.op("act", lambda i=i: A.activation(sq_t[:, i * 512:(i + 1) * 512], ps_list[i][:], AF.Square), reads=[bps_list[i]],
                      **({"writes": [bsq]} if i == 0 else {"acc_writes": [bsq]}))
            for i in range(4):
                fw.op("pe", lambda i=i: T.matmul(psb[:], ones, sq_t[:, i * 512:(i + 1) * 512], start=(i == 0), stop=(i == 3)),
                      reads=[bsq, b_cst], **({"writes": [bpsb]} if i == 0 else {"acc_writes": [bpsb]}))
            rs, brs = c["rsb"].next()
            fw.op("act", lambda: A.activation(rs[:], psb[:], AF.Sqrt, bias=EPS, scale=1.0 / 512), reads=[bpsb], writes=[brs])
            fw.op("dve", lambda: V.reciprocal(rs[:], rs[:]), reads=[brs], writes=[brs])
            for i in range(4):
                fw.op("dve", lambda i=i: V.tensor_tensor(outT[:, i, :], cf_t[:, i * 512:(i + 1) * 512], rs[:], ALU.mult), reads=[bcf, brs],
                      **({"writes": [b_out]} if i == 0 else {"acc_writes": [b_out]}))

        with ExitStack() as ph:
            alloc = lambda n, sh, dt: ph.enter_context(nc.sbuf_tensor(uniq(n), sh, dt))
            palloc = lambda n, sh, dt: ph.enter_context(nc.psum_tensor(uniq(n), sh, dt))
            c = {
                "xs": Ring(alloc, "xs", 2, [128, D], F32), "ss": Ring(alloc, "ss", 4, [128, 1], F32),
                "junk": Ring(alloc, "junk", 1, [128, D], BF16), "xn": Ring(alloc, "xn", 2, [128, D], BF16),
                "tp": Ring(palloc, "tp", 2, [128, 1024], BF16, excl=True),
                "posi": Ring(alloc, "posi", 2, [64, 512], I32), "ang": Ring(alloc, "ang", 4, [64, 512], F32),
                "cfm": Ring(alloc, "cfm", 1, [128, 2048], F32), "sq": Ring(alloc, "sq", 1, [128, 2048], BF16),
                "rsb": Ring(alloc, "rsb", 1, [128, 512], F32),
            }
            hTr = [(alloc(f"hT{i}", [128, 16, 512], BF16), [Buf(f"hT{i}_{s}") for s in range(4)]) for i in range(2)]
            wring = Ring(alloc, "wr", 2, [128, 16 * 512], BF16)
            psr = Ring(palloc, "ps", 6, [128, 512], F32, excl=True)
            wsm = {n: alloc("w_" + n, [128, 4 * 1024], BF16) for n in ("wkn", "wvm", "wqn", "wqr")}
            wkr_t = alloc("w_wkr", [128, 16 * 128], BF16)
            b_wsm = Buf("wsm")
            for n in wsm:
                fw.dma("sp", wsm[n][:].rearrange("p (k n) -> p k n", k=4), wb[n], reads=[b_wb[n]], acc_writes=[b_wsm])
            fw.dma("sp", wkr_t[:].rearrange("p (k n) -> p k n", k=16), wb["wkr"], reads=[b_wb["wkr"]], acc_writes=[b_wsm])
            wsmv = {n: wsm[n][:].rearrange("p (k n) -> p k n", k=4) for n in wsm}
            wkrv = wkr_t[:].rearrange("p (k n) -> p k n", k=16)
            cos2, sin2 = alloc("cos2", [64, 512], F32), alloc("sin2", [64, 512], F32)
            b_cs = Buf("cs")
            stg8 = Ring(alloc, "stg8", 2, [128, 8 * 512], BF16)
            stgv = Ring(alloc, "stgv", 2, [128, 1024], BF16)
            cnT = alloc("cnT", [128, 4, 512], BF16)
            b_cn = Buf("cnT")
            rt = Ring(alloc, "rt", 2, [64, 512], F32)
            stgr = Ring(alloc, "stgr", 2, [64, 8 * 512], BF16)
            evt = [0]

            def evac(dst, src, bsrc, kw):
                evt[0] += 1
                if evt[0] % 2:
                    fw.op("act", lambda: A.copy(dst, src), reads=[bsrc], **kw)
                else:
                    fw.op("dve", lambda: V.tensor_copy(dst, src), reads=[bsrc], **kw)

            def load_w(name, kc_n, col0, ncols):
                wt, bw = wring.next()
                v = wt[:, 0:kc_n * ncols].rearrange("p (k n) -> p k n", k=kc_n)
                fw.dma("sp", v, wb[name][:, :, col0:col0 + ncols], reads=[b_wb[name]], writes=[bw])
                return v, bw

            def proj_feat(wv, bw, wcols, KCn, rhsT, rbufs, M=128):
                pt, bp = psr.next()
                for kc in range(KCn):
                    fw.op("pe", lambda kc=kc: T.matmul(pt[0:M, :], wv[:, kc, wcols[0]:wcols[1]], rhsT[:, kc, :], start=(kc == 0), stop=(kc == KCn - 1)),
                          reads=[bw] + rbufs, **({"writes": [bp]} if kc == 0 else {"acc_writes": [bp]}))
                return pt, bp

            def proj_tok(wv, bw, wcols, KCn, lhsT, lbufs, st):
                pt, bp = psr.next()
                for kc in range(KCn):
                    fw.op("pe", lambda kc=kc: T.matmul(pt[:], lhsT[:, kc, st * 128:(st + 1) * 128], wv[:, kc, wcols[0]:wcols[1]], start=(kc == 0), stop=(kc == KCn - 1)),
                          reads=[bw] + lbufs, **({"writes": [bp]} if kc == 0 else {"acc_writes": [bp]}))
                return pt, bp

            def rope_combine(pa, bpa, pb, bpb, dst, kw):
                t1, b1 = rt.next()
                t2, b2 = rt.next()
                fw.op("dve", lambda: V.tensor_tensor(t1[:], pa[0:64, :], cos2[:], ALU.mult), reads=[bpa, b_cs], writes=[b1])
                fw.op("dve", lambda: V.tensor_tensor(t2[:], pb[0:64, :], sin2[:], ALU.mult), reads=[bpb, b_cs], writes=[b2])
                fw.op("dve", lambda: V.tensor_tensor(dst, t1[:], t2[:], ALU.add), reads=[b1, b2], **kw)

            for kt in range(cfg["nkt"] if "A" in cfg["phases"] else 0):
                hT, hb = hTr[kt % 2]
                front_end(c, lambda st, kt=kt: x_all[kt * 512 + st * 128: kt * 512 + (st + 1) * 128, :], 4, hT, hb)
                if cfg.get('astop', 99) < 2:
                    continue
                rope_tables(c, pos_all[:, kt * 512:(kt + 1) * 512], cos2, sin2, b_cs)
                if cfg.get('astop', 99) < 3:
                    continue
                sk, bsk = stg8.next()
                for cg in range(2):
                    wv, bw = load_w("wk", 16, cg * 512, 512)
                    for hh in range(4):
                        pt, bp = proj_feat(wv, bw, (hh * 128, hh * 128 + 128), 16, hT, hb)
                        h = cg * 4 + hh
                        evac(sk[:, h * 512:(h + 1) * 512], pt[:], bp, {"writes": [bsk]} if h == 0 else {"acc_writes": [bsk]})
                fw.dma("pool", Kscr[:, :, kt * 512:(kt + 1) * 512], sk[:].rearrange("p (h t) -> p h t", h=8), reads=[bsk], acc_writes=[b_scr["K"]])
                if cfg.get('astop', 99) < 4:
                    continue
                wvs = [load_w("wv", 16, cg * 512, 512) for cg in range(2)]
                for st in range(4):
                    sv, bsv = stgv.next()
                    for cg in range(2):
                        pt, bp = proj_tok(wvs[cg][0], wvs[cg][1], (0, 512), 16, hT, [hb[st]], st)
                        evac(sv[:, cg * 512:(cg + 1) * 512], pt[:], bp, {"writes": [bsv]} if cg == 0 else {"acc_writes": [bsv]})
                    fw.dma("pool", Vscr.rearrange("h p k d -> p h k d")[:, :, kt * 4 + st, :], sv[:].rearrange("p (h d) -> p h d", h=8),
                           reads=[bsv], acc_writes=[b_scr["V"]])
                if cfg.get('astop', 99) < 5:
                    continue
                wv, bw = load_w("wckv", 16, 0, 512)
                pl = [proj_feat(wv, bw, (i * 128, i * 128 + 128), 16, hT, hb) for i in range(4)]
                psb, bpsb = psr.next()
                featmajor_norm(c, [p[0] for p in pl], [p[1] for p in pl], cnT, b_cn, psb, bpsb)
                if cfg.get('astop', 99) < 6:
                    continue
                pa, bpa = proj_feat(wkrv, b_wsm, (0, 64), 16, hT, hb, M=64)
                pb, bpb = proj_feat(wkrv, b_wsm, (64, 128), 16, hT, hb, M=64)
                sr, bsr = stgr.next()
                rope_combine(pa, bpa, pb, bpb, sr[:, 0:512], {"writes": [bsr]})
                fw.dma("pool", KRscr[:, kt * 512:(kt + 1) * 512], sr[:, 0:512], reads=[bsr], acc_writes=[b_scr["KR"]])
                if cfg.get('astop', 99) < 7:
                    continue
                sk, bsk = stg8.next()
                for h in range(8):
                    pt, bp = proj_feat(wsmv["wkn"], b_wsm, (h * 128, h * 128 + 128), 4, cnT, [b_cn])
                    evac(sk[:, h * 512:(h + 1) * 512], pt[:], bp, {"writes": [bsk]} if h == 0 else {"acc_writes": [bsk]})
                fw.dma("pool", KNscr[:, :, kt * 512:(kt + 1) * 512], sk[:].rearrange("p (h t) -> p h t", h=8), reads=[bsk], acc_writes=[b_scr["KN"]])
                if cfg.get('astop', 99) < 8:
                    continue
                for st in range(4):
                    sv, bsv = stgv.next()
                    for cg in range(2):
                        pt, bp = proj_tok(wsmv["wvm"], b_wsm, (cg * 512, cg * 512 + 512), 4, cnT, [b_cn], st)
                        evac(sv[:, cg * 512:(cg + 1) * 512], pt[:], bp, {"writes": [bsv]} if cg == 0 else {"acc_writes": [bsv]})
                    fw.dma("pool", VMscr.rearrange("h p k d -> p h k d")[:, :, kt * 4 + st, :], sv[:].rearrange("p (h d) -> p h d", h=8),
                           reads=[bsv], acc_writes=[b_scr["VM"]])
            for k in range(cfg["nslotA"] if "A" in cfg["phases"] else 0):
                hT, hb = hTr[k % 2]
                front_end(c, lambda st, k=k: x_own[k * 512 + st * 128: k * 512 + (st + 1) * 128, :], 4, hT, hb)
                rope_tables(c, pos_own[:, k * 512:(k + 1) * 512], cos2, sin2, b_cs)
                sk, bsk = stg8.next()
                for cg in range(2):
                    wv, bw = load_w("wq", 16, cg * 512, 512)
                    for hh in range(4):
                        pt, bp = proj_feat(wv, bw, (hh * 128, hh * 128 + 128), 16, hT, hb)
                        h = cg * 4 + hh
                        evac(sk[:, h * 512:(h + 1) * 512], pt[:], bp, {"writes": [bsk]} if h == 0 else {"acc_writes": [bsk]})
                fw.dma("pool", Qscr[:, :, k * 512:(k + 1) * 512], sk[:].rearrange("p (h t) -> p h t", h=8), reads=[bsk], acc_writes=[b_scr["Q"]])
                wv, bw = load_w("wcq", 16, 0, 512)
                pl = [proj_feat(wv, bw, (i * 128, i * 128 + 128), 16, hT, hb) for i in range(4)]
                psb, bpsb = psr.next()
                featmajor_norm(c, [p[0] for p in pl], [p[1] for p in pl], cnT, b_cn, psb, bpsb)
                sk, bsk = stg8.next()
                for h in range(8):
                    pt, bp = proj_feat(wsmv["wqn"], b_wsm, (h * 128, h * 128 + 128), 4, cnT, [b_cn])
                    evac(sk[:, h * 512:(h + 1) * 512], pt[:], bp, {"writes": [bsk]} if h == 0 else {"acc_writes": [bsk]})
                fw.dma("pool", QNscr[:, :, k * 512:(k + 1) * 512], sk[:].rearrange("p (h t) -> p h t", h=8), reads=[bsk], acc_writes=[b_scr["QN"]])
                sr, bsr = stgr.next()
                for h in range(8):
                    pa, bpa = proj_feat(wsmv["wqr"], b_wsm, (h * 128, h * 128 + 64), 4, cnT, [b_cn], M=64)
                    pb, bpb = proj_feat(wsmv["wqr"], b_wsm, (h * 128 + 64, h * 128 + 128), 4, cnT, [b_cn], M=64)
                    rope_combine(pa, bpa, pb, bpb, sr[:, h * 512:(h + 1) * 512], {"writes": [bsr]} if h == 0 else {"acc_writes": [bsr]})
                fw.dma("pool", QRscr[:, :, k * 512:(k + 1) * 512], sr[:].rearrange("p (h t) -> p h t", h=8), reads=[bsr], acc_writes=[b_scr["QR"]])
            fw.barrier()
            fw.emit()

        def load_masks(alloc, src, dstname):
            mt = alloc(dstname, [128, 16 * 512], BF16)
            bm = Buf(dstname)
            stg = Ring(alloc, dstname + "s", 2, [128, 2048], F32)
            for i in range(4):
                t, b = stg.next()
                fw.dma("sp", t[:], src[:, i * 2048:(i + 1) * 2048], writes=[b])
                fw.op("dve", lambda t=t, i=i: V.tensor_copy(mt[:, i * 2048:(i + 1) * 2048], t[:]), reads=[b],
                      **({"writes": [bm]} if i == 0 else {"acc_writes": [bm]}))
            return mt, bm

        with ExitStack() as ph:
            alloc = lambda n, sh, dt: ph.enter_context(nc.sbuf_tensor(uniq(n), sh, dt))
            palloc = lambda n, sh, dt: ph.enter_context(nc.psum_tensor(uniq(n), sh, dt))
            do_sb = "B" in cfg["phases"]
            do_ml = "M" in cfg["phases"]
            msk, bmsk = load_masks(alloc, mask_sb, "msb")
            mskm, bmskm = load_masks(alloc, mask_ml, "mml")
            KT, bK = alloc("KT", [128, S], BF16), Buf("KT")
            Vt, bV = alloc("Vt", [128, 64 * 128], BF16), Buf("Vt")
            QT, bQ = alloc("QT", [128, S // 2], BF16), Buf("QT")
            KN, bKN = alloc("KN", [128, S], BF16), Buf("KN")
            VMt, bVM = alloc("VMt", [128, 64 * 128], BF16), Buf("VMt")
            QN, bQN = alloc("QN", [128, S // 2], BF16), Buf("QN")
            QR, bQR = alloc("QR", [64, S // 2], BF16), Buf("QR")
            KRt, bKR = alloc("KRt", [64, S], BF16), Buf("KRt")
            if do_ml:
                fw.dma("sp", KRt[:], KRscr, reads=[b_scr["KR"]], writes=[bKR])
            zr = Ring(palloc, "zps", 2, [128, 512], F32, excl=True)
            Rp, bR = palloc("Rps", [128, 512], F32), Buf("R", True)
            Os, bOs = palloc("Ops", [128, 512], F32), Buf("O", True)
            sr_ = Ring(palloc, "sps", 2, [128, 512], F32, excl=True)
            Om, bOm = palloc("Omps", [128, 512], F32), Buf("Om", True)
            Dp, bD = palloc("Dps", [128, 512], F32), Buf("Dm", True)
            Er = Ring(alloc, "Ef", 3, [128, 512], F32)
            Lr = Ring(alloc, "Lb", 3, [128, 512], BF16)
            Gr = Ring(alloc, "Gf", 2, [128, 512], F32)
            Wr = Ring(alloc, "wbt", 3, [128, 512], BF16)
            Pr = Ring(alloc, "Pb", 4, [128, 512], BF16)
            rdr = Ring(alloc, "rd", 2, [128, 512], F32)
            osr = Ring(alloc, "ost", 2, [128, 512], BF16)
            omr = Ring(alloc, "omt", 2, [128, 512], BF16)
            for h in range(max(cfg["hsb"] if do_sb else 0, cfg["hml"] if do_ml else 0)):
                sb_on = do_sb and h < cfg["hsb"]
                ml_on = do_ml and h < cfg["hml"]
                if sb_on:
                    fw.dma("sp", KT[:], Kscr[:, h, :], reads=[b_scr["K"]], writes=[bK])
                    fw.dma("sp", QT[:], Qscr[:, h, :], reads=[b_scr["Q"]], writes=[bQ])
                    for q4 in range(4):
                        fw.dma("sp", Vt[:, q4 * 2048:(q4 + 1) * 2048].rearrange("p (k d) -> p k d", k=16), Vscr[h, :, q4 * 16:(q4 + 1) * 16, :],
                               reads=[b_scr["V"]], **({"writes": [bV]} if q4 == 0 else {"acc_writes": [bV]}))
                if ml_on:
                    fw.dma("sp", KN[:], KNscr[:, h, :], reads=[b_scr["KN"]], writes=[bKN])
                    fw.dma("sp", QN[:], QNscr[:, h, :], reads=[b_scr["QN"]], writes=[bQN])
                    fw.dma("sp", QR[:], QRscr[:, h, :], reads=[b_scr["QR"]], writes=[bQR])
                    for q4 in range(4):
                        fw.dma("sp", VMt[:, q4 * 2048:(q4 + 1) * 2048].rearrange("p (k d) -> p k d", k=16), VMscr[h, :, q4 * 16:(q4 + 1) * 16, :],
                               reads=[b_scr["VM"]], **({"writes": [bVM]} if q4 == 0 else {"acc_writes": [bVM]}))
                for k in range(cfg["nslotB"]):
                    nb = 8 * (k + 1)
                    par = k % 2
                    st_ = {}

                    def stA(i, k=k, nb=nb, par=par, st_=st_):
                        kb = nb - 1 - i
                        zp, bz = zr.next()
                        bnd = kb >= 8 * k
                        fw.op("pe", lambda: T.matmul(zp[:], KT[:, kb * 128:(kb + 1) * 128], QT[:, k * 512:(k + 1) * 512], start=True, stop=not bnd),
                              reads=[bK, bQ], writes=[bz])
                        if bnd:
                            m0 = (par * 8 + kb - 8 * k) * 512
                            fw.op("pe", lambda: T.matmul(zp[:], ident, msk[:, m0:m0 + 512], start=False, stop=True),
                                  reads=[bmsk, b_cst], acc_writes=[bz])
                        st_[("z", i)] = (zp, bz)

                    def stB(i, st_=st_):
                        zp, bz = st_.pop(("z", i))
                        (Ef, bE), (Lb, bL) = Er.next(), Lr.next()
                        fw.op("act", lambda: A.activation(Ef[:], zp[:], AF.Exp, scale=SB_SCALE), reads=[bz], writes=[bE])
                        fw.op("act", lambda: A.activation(Lb[:], Ef[:], AF.Ln, bias=1.0, scale=1.0), reads=[bE], writes=[bL])
                        st_[("E", i)] = (Ef, bE)
                        st_[("L", i)] = (Lb, bL)

                    def stC(i, st_=st_):
                        Lb, bL = st_[("L", i)]
                        fw.op("pe", lambda: T.matmul(Rp[:], uincl, Lb[:], start=(i == 0), stop=True, skip_group_check=True),
                              reads=[bL, b_cst], writes=[bR])

                    def stD(i, st_=st_):
                        Gf, bG = Gr.next()
                        fw.op("act", lambda: A.activation(Gf[:], Rp[:], AF.Exp, scale=-1.0), reads=[bR], writes=[bG])
                        st_[("G", i)] = (Gf, bG)

                    def stE(i, st_=st_):
                        (Ef, bE), (Gf, bG) = st_.pop(("E", i)), st_.pop(("G", i))
                        wt_, bw_ = Wr.next()
                        fw.op("dve", lambda: V.tensor_tensor(wt_[:], Ef[:], Gf[:], ALU.mult), reads=[bE, bG], writes=[bw_])
                        st_[("w", i)] = (wt_, bw_)

                    def stF1(i, nb=nb, st_=st_):
                        Lb, bL = st_.pop(("L", i))
                        if i < nb - 1:
                            fw.op("pe", lambda: T.matmul(Rp[:], ubar, Lb[:], start=False, stop=True, skip_group_check=True),
                                  reads=[bL, b_cst], writes=[bR])

                    def stF2(i, nb=nb, st_=st_):
                        kb = nb - 1 - i
                        wt_, bw_ = st_.pop(("w", i))
                        fw.op("pe", lambda: T.matmul(Os[:], Vt[:, kb * 128:(kb + 1) * 128], wt_[:], start=(i == 0), stop=(i == nb - 1), skip_group_check=True),
                              reads=[bV, bw_], **({"writes": [bOs]} if i == 0 else {"acc_writes": [bOs]}))

                    def mA(i, k=k, par=par, st_=st_):
                        kb = i
                        sp_, bs = sr_.next()
                        bnd = kb >= 8 * k
                        fw.op("pe", lambda: T.matmul(sp_[:], KN[:, kb * 128:(kb + 1) * 128], QN[:, k * 512:(k + 1) * 512], start=True, stop=False),
                              reads=[bKN, bQN], writes=[bs])
                        fw.op("pe", lambda: T.matmul(sp_[:], KRt[:, kb * 128:(kb + 1) * 128], QR[:, k * 512:(k + 1) * 512], start=False, stop=not bnd),
                              reads=[bKR, bQR], acc_writes=[bs])
                        if bnd:
                            m0 = (par * 8 + kb - 8 * k) * 512
                            fw.op("pe", lambda: T.matmul(sp_[:], ident, mskm[:, m0:m0 + 512], start=False, stop=True),
                                  reads=[bmskm, b_cst], acc_writes=[bs])
                        st_[("s", i)] = (sp_, bs)

                    def mB(i, st_=st_):
                        sp_, bs = st_.pop(("s", i))
                        Pb, bP = Pr.next()
                        fw.op("act", lambda: A.activation(Pb[:], sp_[:], AF.Exp, scale=MLA_SCALE), reads=[bs], writes=[bP])
                        st_[("P", i)] = (Pb, bP)

                    def mC(i, nb=nb, st_=st_):
                        kb = i
                        Pb, bP = st_.pop(("P", i))
                        fw.op("pe", lambda: T.matmul(Om[:], VMt[:, kb * 128:(kb + 1) * 128], Pb[:], start=(i == 0), stop=(i == nb - 1)),
                              reads=[bVM, bP], **({"writes": [bOm]} if i == 0 else {"acc_writes": [bOm]}))
                        fw.op("pe", lambda: T.matmul(Dp[:], ones, Pb[:], start=(i == 0), stop=(i == nb - 1)),
                              reads=[bP, b_cst], **({"writes": [bD]} if i == 0 else {"acc_writes": [bD]}))

                    for t in range(nb + 2):
                        if t < nb:
                            if sb_on:
                                stA(t)
                            if ml_on:
                                mA(t)
                            if sb_on:
                                stB(t)
                        if ml_on and 0 <= t - 1 < nb:
                            mB(t - 1)
                        if sb_on and 0 <= t - 2 < nb:
                            stF1(t - 2)
                        if sb_on and 0 <= t - 1 < nb:
                            stC(t - 1)
                            stD(t - 1)
                            stE(t - 1)
                        if sb_on and 0 <= t - 2 < nb:
                            stF2(t - 2)
                        if ml_on and 0 <= t - 2 < nb:
                            mC(t - 2)
                    if sb_on:
                        ot, bo = osr.next()
                        fw.op("dve", lambda ot=ot: V.tensor_copy(ot[:], Os[:]), reads=[bOs], writes=[bo])
                        fw.dma("pool", OSscr[:, h, k * 512:(k + 1) * 512], ot[:], reads=[bo], acc_writes=[b_scr["OS"]])
                    if ml_on:
                        rd, brd = rdr.next()
                        ot, bo = omr.next()
                        fw.op("dve", lambda rd=rd: V.reciprocal(rd[:], Dp[:]), reads=[bD], writes=[brd])
                        fw.op("dve", lambda rd=rd, ot=ot: V.tensor_tensor(ot[:], Om[:], rd[:], ALU.mult), reads=[bOm, brd], writes=[bo])
                        fw.dma("pool", OMscr[:, h, k * 512:(k + 1) * 512], ot[:], reads=[bo], acc_writes=[b_scr["OM"]])
            fw.barrier()
            fw.emit()

        NS = TC // 128
        with ExitStack() as ph:
            alloc = lambda n, sh, dt: ph.enter_context(nc.sbuf_tensor(uniq(n), sh, dt))
            palloc = lambda n, sh, dt: ph.enter_context(nc.psum_tensor(uniq(n), sh, dt))
            xnr = Ring(alloc, "xn", 2, [128, D], BF16)
            c = {
                "ss": Ring(alloc, "ss", 4, [128, 1], F32),
                "junk": xnr, "xn": xnr,
                "tp": Ring(palloc, "tp", 2, [128, 1024], BF16, excl=True),
            }
            xres = alloc("xres", [128, NS, D], F32)
            bxr = [Buf(f"xr{i}") for i in range(NS)]
            ysb = alloc("ysb", [128, NS, D], F32)
            bys = [Buf(f"ys{i}") for i in range(NS)]
            ysbf = ysb[:].rearrange("p s d -> p (s d)")
            big = alloc("big", [128, 32 * TC], BF16)
            b_big = Buf("big")
            osT = big[:, 0:8 * TC].rearrange("p (k t) -> p k t", k=8)
            omT = big[:, 8 * TC:16 * TC].rearrange("p (k t) -> p k t", k=8)
            uT = big[:].rearrange("p (k t) -> p k t", k=32)
            actT = [(alloc(f"aT{i}", [128, 16, TC], BF16), [Buf(f"aT{i}_{s}") for s in range(NS)]) for i in range(2)]
            gbr = Ring(alloc, "gb", 1, [128, D], F32)
            wring = Ring(alloc, "wr", 3, [128, 16 * 512], BF16)
            psr = Ring(palloc, "ps", 6, [128, 512], F32, excl=True)
            sgr = Ring(alloc, "sg", 3, [128, 512], F32)
            pf, bpf = alloc("pf", [128, NS, 256], F32), Buf("pf")
            pbf, bpbf = alloc("pbf", [128, NS, 256], BF16), Buf("pbf")
            pT, bpT = alloc("pT", [128, 2, TC], BF16), Buf("pT")
            evt = [0]

            def evac(dst, src, bsrc, kw, reads=()):
                evt[0] += 1
                if evt[0] % 2:
                    fw.op("act", lambda: A.copy(dst, src), reads=[bsrc] + list(reads), **kw)
                else:
                    fw.op("dve", lambda: V.tensor_copy(dst, src), reads=[bsrc] + list(reads), **kw)

            def load_w(name, k0, kn, col0, ncols):
                wt, bw = wring.next()
                v = wt[:, 0:kn * ncols].rearrange("p (k n) -> p k n", k=kn)
                fw.dma("sp", v, wb[name][:, k0:k0 + kn, col0:col0 + ncols], reads=[b_wb[name]], writes=[bw])
                return v, bw

            def load_g(i):
                gt, bg = gbr.next()
                fw.dma("sp", gt[:], grow[i:i + 1, :].partition_broadcast(128), writes=[bg])
                return gt, bg

            def proj_feat(wv, bw, wcols, KCn, rhsT, rbufs):
                pt, bp = psr.next()
                for kc in range(KCn):
                    fw.op("pe", lambda kc=kc: T.matmul(pt[:, 0:TC], wv[:, kc, wcols[0]:wcols[1]], rhsT[:, kc, :], start=(kc == 0), stop=(kc == KCn - 1)),
                          reads=[bw] + rbufs, **({"writes": [bp]} if kc == 0 else {"acc_writes": [bp]}))
                return pt, bp

            def tok_mm(pt, bp, lhsT, lbufs, st, wv, bw, kcs, first, last):
                for j, (kc_l, kc_w) in enumerate(kcs):
                    fw.op("pe", lambda kc_l=kc_l, kc_w=kc_w, j=j: T.matmul(pt[:], lhsT[:, kc_l, st * 128:(st + 1) * 128], wv[:, kc_w, :],
                                                                  start=(first and j == 0), stop=(last and j == len(kcs) - 1), skip_group_check=True),
                          reads=[bw] + lbufs, **({"writes": [bp]} if (first and j == 0) else {"acc_writes": [bp]}))

            def post_norm(gi):
                gt, bg = load_g(gi)
                for st in range(NS):
                    ss, bss = c["ss"].next()
                    jk, bjk = c["junk"].next()
                    fw.op("dve", lambda ss=ss: V.memset(ss[:], 0.0), writes=[bss])
                    fw.op("act", lambda jk=jk, ss=ss, st=st: A.activation(jk[:], ysb[:, st, :], AF.Square, accum_out=ss[:]), reads=[bys[st], bss], writes=[bjk, bss])
                    fw.op("act", lambda ss=ss: A.activation(ss[:], ss[:], AF.Sqrt, bias=EPS, scale=1.0 / D), reads=[bss], writes=[bss])
                    fw.op("dve", lambda ss=ss: V.reciprocal(ss[:], ss[:]), reads=[bss], writes=[bss])
                    fw.op("dve", lambda ss=ss, st=st: V.scalar_tensor_tensor(ysb[:, st, :], ysb[:, st, :], ss[:, 0:1], gt[:], op0=ALU.mult, op1=ALU.mult),
                          reads=[bys[st], bss, bg], writes=[bys[st]])

            def residual_add():
                for st in range(NS):
                    fw.op("dve", lambda st=st: V.tensor_tensor(xres[:, st, :], xres[:, st, :], ysb[:, st, :], ALU.add), reads=[bxr[st], bys[st]], writes=[bxr[st]])

            for cs in range(cfg["ncs"] if "C" in cfg["phases"] else 0):
                r0 = cs * TC
                xd = [(xres[:, st, :], bxr[st]) for st in range(NS)]
                hT, hb = actT[0]
                front_end(c, lambda st, r0=r0: x_own[r0 + st * 128: r0 + (st + 1) * 128, :], NS, hT, hb, xdst=xd)
                fw.dma("sp", osT, OSscr[:, :, r0:r0 + TC], reads=[b_scr["OS"]], writes=[b_big])
                fw.dma("sp", omT, OMscr[:, :, r0:r0 + TC], reads=[b_scr["OM"]], acc_writes=[b_big])
                mT, mb = actT[1]
                bm_all = Buf("mixedT")
                for og in range(4):
                    wo, bwo = wring.next()
                    wov = wo[:].rearrange("p (a k n) -> p a k n", a=2, k=8)
                    fw.dma("sp", wov[:, 0], wb["wsbo"][:, :, og * 512:(og + 1) * 512], reads=[b_wb["wsbo"]], writes=[bwo])
                    fw.dma("sp", wov[:, 1], wb["wmlao"][:, :, og * 512:(og + 1) * 512], reads=[b_wb["wmlao"]], acc_writes=[bwo])
                    for gsel in range(2):
                        wg, bwg = load_w("wgs" if gsel == 0 else "wgm", 0, 16, og * 512, 512)
                        for oo in range(4):
                            oc = og * 4 + oo
                            cols = (oo * 128, oo * 128 + 128)
                            pa, bpa = proj_feat(wov[:, gsel], bwo, cols, 8, osT if gsel == 0 else omT, [b_big])
                            pg, bpg = proj_feat(wg, bwg, cols, 16, hT, hb)
                            sg, bsg = sgr.next()
                            fw.op("act", lambda sg=sg, pg=pg: A.activation(sg[:, 0:TC], pg[:, 0:TC], AF.Sigmoid), reads=[bpg], writes=[bsg])
                            fw.op("dve", lambda sg=sg, pa=pa: V.tensor_tensor(sg[:, 0:TC], sg[:, 0:TC], pa[:, 0:TC], ALU.mult), reads=[bsg, bpa], writes=[bsg])
                            if gsel == 0:
                                fw.op("dve", lambda sg=sg, oc=oc: V.tensor_copy(ysbf[:, oc * TC:(oc + 1) * TC], sg[:, 0:TC]), reads=[bsg],
                                      acc_writes=[bys[0]])
                            else:
                                fw.op("dve", lambda sg=sg, oc=oc: V.tensor_tensor(mT[:, oc, :], sg[:, 0:TC], ysbf[:, oc * TC:(oc + 1) * TC], ALU.add),
                                      reads=[bsg, bys[0]], acc_writes=[bm_all])
                if dbgC and cs == 0:
                    fw.dma("pool", d_mixed, mT[:], reads=[bm_all], acc_writes=[b_dbg])
                for cg in range(4):
                    wv, bw = load_w("wout", 0, 16, cg * 512, 512)
                    for st in range(NS):
                        pt, bp = psr.next()
                        tok_mm(pt, bp, mT, [bm_all], st, wv, bw, [(kc, kc) for kc in range(16)], True, True)
                        evac(ysb[:, st, cg * 512:(cg + 1) * 512], pt[:], bp, {"writes": [bys[st]]} if cg == 0 else {"acc_writes": [bys[st]]},
                             reads=[bm_all] if cg == 0 else [])
                if dbgC and cs == 0:
                    fw.dma("pool", d_y, ysb[:], reads=bys, acc_writes=[b_dbg])
                post_norm(0)
                residual_add()
                if dbgC and cs == 0:
                    fw.dma("pool", d_x1, xres[:], reads=bxr, acc_writes=[b_dbg])
                h2T, h2b = actT[0]
                front_end(c, None, NS, h2T, h2b, xdst=xd)
                for fh in range(2):
                    for fg in range(8):
                        wv, bw = load_w("wup", 0, 16, (fh * 8 + fg) * 512, 512)
                        for fc in range(4):
                            pt, bp = proj_feat(wv, bw, (fc * 128, fc * 128 + 128), 16, h2T, h2b)
                            sg, bsg = sgr.next()
                            fw.op("act", lambda sg=sg, pt=pt: A.activation(sg[:, 0:TC], pt[:, 0:TC], AF.Relu), reads=[bp], writes=[bsg])
                            fw.op("dve", lambda sg=sg, f=fg * 4 + fc: V.tensor_tensor(uT[:, f, :], sg[:, 0:TC], sg[:, 0:TC], ALU.mult), reads=[bsg],
                                  **({"writes": [b_big]} if (fg == 0 and fc == 0) else {"acc_writes": [b_big]}))
                    for cg in range(4):
                        accs = [psr.next() for _ in range(NS)]
                        for pc in range(4):
                            wv, bw = load_w("wdown", fh * 32 + pc * 8, 8, cg * 512, 512)
                            for st in range(NS):
                                tok_mm(accs[st][0], accs[st][1], uT, [b_big], st, wv, bw, [(pc * 8 + j, j) for j in range(8)], pc == 0, pc == 3)
                        for st in range(NS):
                            dst = ysb[:, st, cg * 512:(cg + 1) * 512]
                            if fh == 0:
                                evac(dst, accs[st][0][:], accs[st][1], {"writes": [bys[st]]} if cg == 0 else {"acc_writes": [bys[st]]})
                            else:
                                fw.op("dve", lambda dst=dst, ps_=accs[st][0]: V.tensor_tensor(dst, ps_[:], dst, ALU.add), reads=[accs[st][1], bys[st]],
                                      **({"writes": [bys[st]]} if cg == 0 else {"acc_writes": [bys[st]]}))
                post_norm(1)
                residual_add()
                if dbgC and cs == 0:
                    fw.dma("pool", d_x2, xres[:], reads=bxr, acc_writes=[b_dbg])
                fw.dma("sp", pf[:], p_own[r0:r0 + TC, :].rearrange("(s p) d -> p s d", p=128), writes=[bpf])
                fw.op("dve", lambda: V.tensor_copy(pbf[:], pf[:]), reads=[bpf], writes=[bpbf])
                tp, btp = c["tp"].next()
                for st in range(NS):
                    for kc in range(2):
                        j = st * 2 + kc
                        fw.op("pe", lambda st=st, kc=kc, j=j, tp=tp: T.transpose(tp[:, j * 128:(j + 1) * 128], pbf[:, st, kc * 128:(kc + 1) * 128], ident),
                              reads=[bpbf, b_cst], **({"writes": [btp]} if j == 0 else {"acc_writes": [btp]}))
                for st in range(NS):
                    fw.op("act", lambda st=st, tp=tp: A.copy(pT[:, :, st * 128:(st + 1) * 128], tp[:, st * 256:(st + 1) * 256].rearrange("p (k t) -> p k t", k=2)),
                          reads=[btp], **({"writes": [bpT]} if st == 0 else {"acc_writes": [bpT]}))
                for cg in range(4):
                    wv, bw = load_w("wple", 0, 2, cg * 512, 512)
                    for st in range(NS):
                        pt, bp = psr.next()
                        tok_mm(pt, bp, pT, [bpT], st, wv, bw, [(0, 0), (1, 1)], True, True)
                        evac(ysb[:, st, cg * 512:(cg + 1) * 512], pt[:], bp, {"writes": [bys[st]]} if cg == 0 else {"acc_writes": [bys[st]]})
                post_norm(2)
                if dbgC and cs == 0:
                    fw.dma("pool", d_e, ysb[:], reads=bys, acc_writes=[b_dbg])
                x2T, x2b = actT[1]
                front_end(c, None, NS, x2T, x2b, xdst=xd, norm=False)
                for cg in range(4):
                    wv, bw = load_w("wpg", 0, 16, cg * 512, 512)
                    for st in range(NS):
                        pt, bp = psr.next()
                        tok_mm(pt, bp, x2T, [x2b[st]], st, wv, bw, [(kc, kc) for kc in range(16)], True, True)
                        sg, bsg = sgr.next()
                        sl = slice(cg * 512, (cg + 1) * 512)
                        fw.op("act", lambda sg=sg, pt=pt: A.activation(sg[:], pt[:], AF.Sigmoid), reads=[bp], writes=[bsg])
                        fw.op("dve", lambda sg=sg, st=st, sl=sl: V.tensor_tensor(sg[:], sg[:], ysb[:, st, sl], ALU.mult), reads=[bsg, bys[st]], writes=[bsg])
                        fw.op("dve", lambda sg=sg, st=st, sl=sl: V.tensor_tensor(ysb[:, st, sl], sg[:], xres[:, st, sl], ALU.add), reads=[bsg, bxr[st]], writes=[bys[st]])
                for st in range(NS):
                    fw.dma("pool", out[r0 + st * 128: r0 + (st + 1) * 128, :], ysb[:, st, :], reads=[bys[st]], acc_writes=[b_scr["out"]])
            fw.barrier()
            st = fw.emit()
            print("program stats", st, flush=True)
    return nc


def _arr(w, kc):
    k, n = w.shape
    return np.ascontiguousarray(w.reshape(kc, 128, n).transpose(1, 0, 2))


def _masks(j, strict):
    sidx = np.arange(128)[:, None]
    tq = np.arange(512)[None, :]
    out = np.zeros((128, 16, 512), np.float32)
    for q in range(2):
        is_max = (j == 1) if q == 0 else (j == 0)
        for r in range(8):
            if is_max:
                m = None if r < 4 else r - 4
                allneg = False
            else:
                m = r if r < 4 else None
                allneg = r >= 4
            if allneg:
                blk = np.full((128, 512), NEG, np.float32)
            elif m is None:
                blk = np.zeros((128, 512), np.float32)
            else:
                vis = (128 * m + sidx) < tq if strict else (128 * m + sidx) <= tq
                blk = np.where(vis, 0.0, NEG).astype(np.float32)
            out[:, q * 8 + r, :] = blk
    return out.reshape(128, 16 * 512)


_PROG = {}


def _prep(x, p, positions, g_pre_mix, w_in, g_cq, g_ckv, w_q_up, w_kv_up, w_sb_o, w_mla_o, w_out,
           g_post_mix, g_pre_mlp, w_up, w_down, g_post_mlp, w_ple, g_ple, w_ple_gate):
    f = lambda a: np.asarray(a, dtype=np.float32)
    x, p = f(x), f(p)
    positions = np.asarray(positions).astype(np.int32)
    w_in0 = f(w_in)[0]
    kr = w_in0[:, 4096:4160]
    wqu = f(w_q_up)[0].reshape(512, 8, 192)
    rope = wqu[:, :, 128:192]
    wkv = f(w_kv_up)[0].reshape(512, 8, 256)
    shared = {
        "wq_f": _arr(w_in0[:, 0:1024], 16), "wk_f": _arr(w_in0[:, 1024:2048], 16), "wv_f": _arr(w_in0[:, 2048:3072], 16),
        "wcq_f": _arr(w_in0[:, 3072:3584], 16), "wckv_f": _arr(w_in0[:, 3584:4096], 16),
        "wkr_f": _arr(np.concatenate([kr, kr[:, 32:64], kr[:, 0:32]], axis=1), 16),
        "wgs_f": _arr(w_in0[:, 4160:6208], 16), "wgm_f": _arr(w_in0[:, 6208:8256], 16),
        "wqn_f": _arr(np.ascontiguousarray(wqu[:, :, 0:128]).reshape(512, 1024), 4),
        "wqr_f": _arr(np.concatenate([rope, rope[:, :, 32:64], rope[:, :, 0:32]], axis=2).reshape(512, 1024), 4),
        "wkn_f": _arr(np.ascontiguousarray(wkv[:, :, 0:128]).reshape(512, 1024), 4),
        "wvm_f": _arr(np.ascontiguousarray(wkv[:, :, 128:256]).reshape(512, 1024), 4),
        "wsbo_f": _arr(f(w_sb_o)[0], 8), "wmlao_f": _arr(f(w_mla_o)[0], 8), "wout_f": _arr(f(w_out)[0], 16),
        "wup_f": _arr(f(w_up)[0], 16), "wdown_f": _arr(f(w_down)[0], 64), "wple_f": _arr(f(w_ple)[0], 2),
        "wpg_f": _arr(f(w_ple_gate)[0], 16),
        "gpm": np.ascontiguousarray(f(g_pre_mix)[0].reshape(16, 128).T), "gcq": np.ascontiguousarray(f(g_cq)[0].reshape(4, 128).T),
        "gckv": np.ascontiguousarray(f(g_ckv)[0].reshape(4, 128).T), "gmlp": np.ascontiguousarray(f(g_pre_mlp)[0].reshape(16, 128).T),
        "grow": np.stack([f(g_post_mix)[0], f(g_post_mlp)[0], f(g_ple)[0]], axis=0),
    }
    jj = np.arange(128)[:, None]
    ss_ = np.arange(128)[None, :]
    shared["consts"] = np.concatenate([np.eye(128), (jj >= ss_), (jj < ss_), np.ones((128, 128))], axis=1).astype(np.float32)
    inv_freq = (np.float32(10000.0) ** (-np.arange(32, dtype=np.float32) / np.float32(32))).astype(np.float32)
    shared["invf"] = np.concatenate([inv_freq, inv_freq])[:, None].astype(np.float32)
    in_maps = []
    for c in range(NCORES):
        b, j = c // 2, c % 2
        tiles = SLOT_TILES[j]
        rows = np.concatenate([np.arange(t * 512, (t + 1) * 512) for t in tiles])
        m = dict(shared)
        m["x_all"] = np.ascontiguousarray(x[b])
        m["x_own"] = np.ascontiguousarray(x[b][rows])
        m["p_own"] = np.ascontiguousarray(p[0, b][rows])
        m["pos_all"] = np.ascontiguousarray(positions[b][None, :])
        m["pos_own"] = np.ascontiguousarray(positions[b][rows][None, :])
        m["mask_sb"] = _masks(j, True)
        m["mask_ml"] = _masks(j, False)
        in_maps.append(m)
    return in_maps


def kernel(**inputs):
    in_maps = _prep(**inputs)
    if "nc" not in _PROG:
        _PROG["nc"] = build_program()
    res = run_bass_kernel_spmd(_PROG["nc"], in_maps, core_ids=list(range(NCORES)))
    outp = np.empty((4, S, D), np.float32)
    for c in range(NCORES):
        b, j = c // 2, c % 2
        o = np.asarray(res.results[c]["out"])
        for i, t in enumerate(SLOT_TILES[j]):
            outp[b, t * 512:(t + 1) * 512] = o[i * 512:(i + 1) * 512]
    return outp
```

```python
import numpy as np
from contextlib import ExitStack
import concourse.bass as bass
import concourse.mybir as mybir
from concourse.bass_utils import run_bass_kernel_spmd

F32 = mybir.dt.float32
BF16 = mybir.dt.bfloat16
I32 = mybir.dt.int32
AF = mybir.ActivationFunctionType
ALU = mybir.AluOpType
AX = mybir.AxisListType


class Buf:
    __slots__ = ("name", "writers", "readers", "war", "excl")

    def __init__(self, name="", excl=False):
        self.name = name
        self.writers = {}
        self.readers = {}
        self.war = {}
        self.excl = excl


class _Op:
    __slots__ = ("eng", "fn", "deps", "is_dma", "signal", "need_signal", "slot", "idx")


SEM_LIMIT = 30000


class FW:
    ENGS = ("pe", "act", "dve", "pool", "sp")
    NDMASEM = 24

    def __init__(self, nc, es):
        self.nc = nc
        self.es = es
        self.ops = []
        self.engobj = {"pe": nc.tensor, "act": nc.scalar, "dve": nc.vector, "pool": nc.gpsimd, "sp": nc.sync}
        self.nsem = 0
        self.barrier_idx = 0
        self.emitted = 0
        self.slot_prev = [None] * self.NDMASEM
        self.ndma = 0
        self.engsem = {}
        self.engcnt = {}
        self.waited = {e: {} for e in self.ENGS}
        self.nwait = 0
        self.nsig = 0
        self.last_on_eng = {}
        self.dma_since_barrier = []
        self.serial = False

    def new_sem(self, name):
        self.nsem += 1
        return self.es.enter_context(self.nc.semaphore(f"{name}_{self.nsem}"))

    def _record(self, eng, fn, reads, writes, is_dma, acc_writes=()):
        op = _Op()
        op.eng = eng
        op.fn = fn
        op.is_dma = is_dma
        op.signal = None
        op.need_signal = is_dma
        op.slot = None
        op.idx = idx = len(self.ops)
        key = ("dma", idx) if is_dma else eng
        deps = set()
        bi = self.barrier_idx
        for b in reads:
            for w in b.writers.values():
                if w >= bi:
                    deps.add(w)
            if b.excl:
                for r in b.readers.values():
                    if r >= bi:
                        deps.add(r)
        for b in writes:
            for r in b.readers.values():
                if r >= bi:
                    deps.add(r)
            for w in b.writers.values():
                if w >= bi:
                    deps.add(w)
            for r in b.war.values():
                if r >= bi:
                    deps.add(r)
        for b in acc_writes:
            for r in b.readers.values():
                if r >= bi:
                    deps.add(r)
            for r in b.war.values():
                if r >= bi:
                    deps.add(r)
        for b in reads:
            b.readers[key] = idx
        for b in writes:
            b.war = b.readers
            b.writers = {key: idx}
            b.readers = {}
        for b in acc_writes:
            if b.readers:
                b.war = b.readers
                b.writers = {key: idx}
                b.readers = {}
            else:
                b.writers[key] = idx
        if self.serial and idx > 0 and idx - 1 >= bi:
            deps.add(idx - 1)
        deps.discard(idx)
        op.deps = deps
        self.ops.append(op)
        if fn is not None:
            if is_dma:
                self.dma_since_barrier.append(idx)
            else:
                self.last_on_eng[eng] = idx
        return op

    def op(self, eng, fn, reads=(), writes=(), acc_writes=()):
        return self._record(eng, fn, reads, writes, False, acc_writes)

    def dma(self, eng, out, in_, reads=(), writes=(), acc_writes=()):
        e = self.engobj[eng]
        return self._record(eng, lambda: e.dma_start(out=out, in_=in_), reads, writes, True, acc_writes)

    def barrier(self, engs=None):
        deps = set(self.last_on_eng.values()) | set(self.dma_since_barrier)
        deps = {d for d in deps if d >= self.barrier_idx}
        for eng in (engs or self.ENGS):
            op = _Op()
            op.eng = eng
            op.fn = None
            op.is_dma = False
            op.signal = None
            op.need_signal = False
            op.slot = None
            op.idx = len(self.ops)
            op.deps = set(deps)
            self.ops.append(op)
        self.barrier_idx = len(self.ops)
        self.dma_since_barrier = []
        self.last_on_eng = {}

    def emit(self):
        ops = self.ops
        lo = self.emitted
        for op in ops[lo:]:
            if op.is_dma:
                s = self.ndma % self.NDMASEM
                if self.slot_prev[s] is not None:
                    op.deps.add(self.slot_prev[s])
                self.slot_prev[s] = op.idx
                op.slot = s
                self.ndma += 1
        for op in ops[lo:]:
            for d in op.deps:
                p = ops[d]
                if p.is_dma or op.is_dma or p.eng != op.eng or op.eng != "pe":
                    assert d >= lo or p.signal is not None or p.fn is None, "dep on already-emitted unsignalled op"
                    p.need_signal = True
        engsem, engcnt = self.engsem, self.engcnt
        for op in ops[lo:]:
            e = self.engobj[op.eng]
            need = {}
            for d in op.deps:
                p = ops[d]
                if p.signal is None:
                    continue
                if (not p.is_dma) and (not op.is_dma) and p.eng == op.eng and op.eng == "pe":
                    continue
                sem, val = p.signal
                k = id(sem)
                if k not in need or need[k][1] < val:
                    need[k] = (sem, val)
            wc = self.waited[op.eng]
            for k, (sem, val) in need.items():
                if wc.get(k, 0) >= val:
                    continue
                e.wait_ge(sem, val)
                wc[k] = val
                self.nwait += 1
            if op.fn is None:
                continue
            ins = op.fn()
            op.fn = True
            if op.need_signal:
                if op.is_dma:
                    key = ("dma", op.slot)
                    inc = 16
                else:
                    key = op.eng
                    inc = 1
                if key not in engsem or engcnt[key] + inc > SEM_LIMIT:
                    engsem[key] = self.new_sem("d" if op.is_dma else op.eng)
                    engcnt[key] = 0
                engcnt[key] += inc
                ins.then_inc(engsem[key], inc)
                op.signal = (engsem[key], engcnt[key])
                self.nsig += 1
        self.emitted = len(ops)
        return dict(nops=len(ops), nwait=self.nwait, nsig=self.nsig, nsem=self.nsem)


NCORES = 8
S = 8192
D = 2048
TA = 512
NT = S // TA
NSLOT = 8
TC = 512
EPS = 1e-6
SLOT_TILES = {0: [0, 3, 4, 7, 8, 11, 12, 15], 1: [1, 2, 5, 6, 9, 10, 13, 14]}
NEG = -30000.0
SB_SCALE = 128 ** -0.5
MLA_SCALE = 192 ** -0.5

WSPEC = [
    ("wq", 16, 1024, "gpm", []), ("wk", 16, 1024, "gpm", []), ("wv", 16, 1024, "gpm", []),
    ("wcq", 16, 512, "gpm", []), ("wckv", 16, 512, "gpm", []), ("wkr", 16, 128, "gpm", [(64, 96)]),
    ("wgs", 16, 2048, "gpm", []), ("wgm", 16, 2048, "gpm", []),
    ("wqn", 4, 1024, "gcq", []), ("wqr", 4, 1024, "gcq", [(h * 128 + 64, h * 128 + 96) for h in range(8)]),
    ("wkn", 4, 1024, "gckv", []), ("wvm", 4, 1024, "gckv", []),
    ("wsbo", 8, 2048, None, []), ("wmlao", 8, 2048, None, []), ("wout", 16, 2048, None, []),
    ("wup", 16, 8192, "gmlp", []), ("wdown", 64, 2048, None, []), ("wple", 2, 2048, None, []),
    ("wpg", 16, 2048, None, []),
]
GSPEC = {"gpm": 16, "gcq": 4, "gckv": 4, "gmlp": 16}


class Ring:
    def __init__(self, alloc, name, n, shape, dt, excl=False):
        self.items = [(alloc(f"{name}{i}", shape, dt), Buf(f"{name}{i}", excl)) for i in range(n)]
        self.i = 0

    def next(self):
        it = self.items[self.i % len(self.items)]
        self.i += 1
        return it


def build_program(cfg=None):
    cfg = dict(dict(nkt=NT, nslotA=NSLOT, hsb=8, hml=8, nslotB=NSLOT, ncs=(S // 2) // TC, phases="0ABMC", dbg=()), **(cfg or {}))
    nc = bass.Bass("TRN2", target_bir_lowering=False)

    def din(name, shape, dt=F32):
        return nc.dram_tensor(name, shape, dt, kind="ExternalInput").ap()

    def dscr(name, shape, dt=BF16):
        if name in cfg["dbg"]:
            return nc.dram_tensor(name, shape, dt, kind="ExternalOutput").ap()
        return nc.dram_tensor(name, shape, dt).ap()

    x_all = din("x_all", [S, D])
    x_own = din("x_own", [S // 2, D])
    p_own = din("p_own", [S // 2, 256])
    pos_all = din("pos_all", [1, S], I32)
    pos_own = din("pos_own", [1, S // 2], I32)
    consts = din("consts", [128, 4 * 128])
    invf = din("invf", [64, 1])
    mask_sb = din("mask_sb", [128, 16 * 512])
    mask_ml = din("mask_ml", [128, 16 * 512])
    grow = din("grow", [3, D])
    gin = {g: din(g, [128, kc]) for g, kc in GSPEC.items()}
    win = {n: din(n + "_f", [128, kc, nn]) for n, kc, nn, _, _ in WSPEC}
    out = nc.dram_tensor("out", [S // 2, D], F32, kind="ExternalOutput").ap()

    wb = {n: dscr(n + "_b", [128, kc, nn]) for n, kc, nn, _, _ in WSPEC}
    Kscr = dscr("Kscr", [128, 8, S])
    Vscr = dscr("Vscr", [8, 128, 64, 128])
    KNscr = dscr("KNscr", [128, 8, S])
    VMscr = dscr("VMscr", [8, 128, 64, 128])
    KRscr = dscr("KRscr", [64, S])
    Qscr = dscr("Qscr", [128, 8, S // 2])
    QNscr = dscr("QNscr", [128, 8, S // 2])
    QRscr = dscr("QRscr", [64, 8, S // 2])
    OSscr = dscr("OSscr", [128, 8, S // 2])
    OMscr = dscr("OMscr", [128, 8, S // 2])
    dbgC = cfg.get("dbgC", False)
    if dbgC:
        d_mixed = nc.dram_tensor("d_mixed", [128, 16, TC], BF16, kind="ExternalOutput").ap()
        d_x1 = nc.dram_tensor("d_x1", [128, TC // 128, D], F32, kind="ExternalOutput").ap()
        d_x2 = nc.dram_tensor("d_x2", [128, TC // 128, D], F32, kind="ExternalOutput").ap()
        d_e = nc.dram_tensor("d_e", [128, TC // 128, D], F32, kind="ExternalOutput").ap()
        d_y = nc.dram_tensor("d_y", [128, TC // 128, D], F32, kind="ExternalOutput").ap()
        b_dbg = Buf("dbg")
    b_wb = {n: Buf(n) for n in wb}
    b_scr = {n: Buf(n) for n in ["K", "V", "KN", "VM", "KR", "Q", "QN", "QR", "OS", "OM", "out"]}

    _uid = [0]

    def uniq(n):
        _uid[0] += 1
        return f"{n}_{_uid[0]}"

    with ExitStack() as es:
        fw = FW(nc, es)
        fw.serial = bool(cfg.get('serial', False))
        V, A, P, T = nc.vector, nc.scalar, nc.gpsimd, nc.tensor

        galloc = lambda n, sh, dt: es.enter_context(nc.sbuf_tensor(n, sh, dt))
        cst = galloc("cst", [128, 512], BF16)
        ident, uincl, ubar, ones = cst[:, 0:128], cst[:, 128:256], cst[:, 256:384], cst[:, 384:512]
        gT = {g: galloc(g + "_t", [128, kc], F32) for g, kc in GSPEC.items()}
        gTn = {g: galloc(g + "_n", [128, GSPEC[g]], F32) for g in ("gpm", "gcq")}
        invf_t = galloc("invf_t", [64, 1], F32)
        b_cst = Buf("cst")
        with ExitStack() as ph:
            alloc = lambda n, sh, dt: ph.enter_context(nc.sbuf_tensor(uniq(n), sh, dt))
            cf = alloc("cf", [128, 512], F32)
            b_cf = Buf("cf")
            fw.dma("sp", cf[:], consts, writes=[b_cf])
            fw.op("dve", lambda: V.tensor_copy(cst[:], cf[:]), reads=[b_cf], writes=[b_cst])
            for g in GSPEC:
                fw.dma("sp", gT[g][:], gin[g], writes=[b_cst])
            fw.dma("sp", invf_t[:], invf, writes=[b_cst])
            fw.barrier()
            for g in gTn:
                fw.op("dve", lambda g=g: V.tensor_scalar(gTn[g][:], gT[g][:], -1.0, None, op0=ALU.mult),
                      reads=[b_cst], writes=[b_cst])

            stf = Ring(alloc, "stf", 2, [128, 4096], F32)
            stb = Ring(alloc, "stb", 2, [128, 4096], BF16)
            tog = 0
            for name, KC, N, gname, negs in (WSPEC if "0" in cfg["phases"] else []):
                if N >= 4096:
                    chunks = [(kc, 1, n0, min(4096, N - n0)) for kc in range(KC) for n0 in range(0, N, 4096)]
                else:
                    kcn = max(1, 4096 // N)
                    chunks = [(k0, min(kcn, KC - k0), 0, N) for k0 in range(0, KC, kcn)]
                for k0, kn, n0, nn in chunks:
                    (tf, bf), (tb, bb) = stf.next(), stb.next()
                    fv = tf[:, 0:kn * nn].rearrange("p (k n) -> p k n", k=kn)
                    bv = tb[:, 0:kn * nn].rearrange("p (k n) -> p k n", k=kn)
                    fw.dma("sp", fv, win[name][:, k0:k0 + kn, n0:n0 + nn], writes=[bf])
                    if gname is None:
                        if tog % 2 == 0:
                            fw.op("act", lambda tb=tb, tf=tf, m=kn * nn: A.copy(tb[:, 0:m], tf[:, 0:m]), reads=[bf], writes=[bb])
                        else:
                            fw.op("dve", lambda tb=tb, tf=tf, m=kn * nn: V.tensor_copy(tb[:, 0:m], tf[:, 0:m]), reads=[bf], writes=[bb])
                        tog += 1
                    else:
                        first = True
                        for kk in range(kn):
                            kc = k0 + kk
                            segs, c = [], 0
                            for lo, hi in negs:
                                if lo > c:
                                    segs.append((c, lo, 1))
                                segs.append((lo, hi, -1))
                                c = hi
                            if c < nn:
                                segs.append((c, nn, 1))
                            for lo, hi, sg in segs:
                                sc = (gT if sg > 0 else gTn)[gname][:, kc:kc + 1]
                                o_ap, i_ap = tb[:, kk * nn + lo:kk * nn + hi], tf[:, kk * nn + lo:kk * nn + hi]
                                kw = {"writes": [bb]} if first else {"acc_writes": [bb]}
                                first = False
                                if tog % 2 == 0:
                                    fw.op("act", lambda o=o_ap, i=i_ap, sc=sc: A.activation(o, i, AF.Copy, scale=sc), reads=[bf, b_cst], **kw)
                                else:
                                    fw.op("dve", lambda o=o_ap, i=i_ap, sc=sc: V.tensor_scalar(o, i, sc, None, op0=ALU.mult), reads=[bf, b_cst], **kw)
                                tog += 1
                    fw.dma("pool", wb[name][:, k0:k0 + kn, n0:n0 + nn], bv, reads=[bb], acc_writes=[b_wb[name]])
            fw.barrier()
            fw.emit()

        def front_end(alloc_ctx, xsrc, nsub, hT, hbufs, xdst=None, norm=True):
            for st in range(nsub):
                front_end_sub(alloc_ctx, xsrc, st, hT, hbufs, xdst, norm)

        def front_end_sub(alloc_ctx, xsrc, st, hT, hbufs, xdst=None, norm=True):
            c = alloc_ctx
            if True:
                if xdst is None:
                    xt, bx = c["xs"].next()
                    xap = xt[:]
                else:
                    xap, bx = xdst[st]
                if xsrc is not None:
                    fw.dma("sp", xap, xsrc(st), writes=[bx])
                xn, bxn = c["xn"].next()
                if not norm:
                    fw.op("dve", lambda xn=xn, xap=xap: V.tensor_copy(xn[:], xap), reads=[bx], writes=[bxn])
                ss, bss = c["ss"].next()
                jk, bjk = c["junk"].next()
                if norm:
                  fw.op("dve", lambda ss=ss: V.memset(ss[:], 0.0), writes=[bss])
                  fw.op("act", lambda jk=jk, xap=xap, ss=ss: A.activation(jk[:], xap, AF.Square, accum_out=ss[:]),
                      reads=[bx, bss], writes=[bjk, bss])
                  fw.op("act", lambda ss=ss: A.activation(ss[:], ss[:], AF.Sqrt, bias=EPS, scale=1.0 / D),
                      reads=[bss], writes=[bss])
                  fw.op("dve", lambda ss=ss: V.reciprocal(ss[:], ss[:]),
                      reads=[bss], writes=[bss])
                  fw.op("dve", lambda xn=xn, xap=xap, ss=ss: V.tensor_scalar(xn[:], xap, ss[:, 0:1], None, op0=ALU.mult),
                      reads=[bx, bss], writes=[bxn])
                for half in range(2):
                    tp, btp = c["tp"].next()
                    for q in range(8):
                        kc = half * 8 + q
                        fw.op("pe", lambda tp=tp, xn=xn, q=q, kc=kc: T.transpose(tp[:, q * 128:(q + 1) * 128], xn[:, kc * 128:(kc + 1) * 128], ident),
                              reads=[bxn, b_cst], **({"writes": [btp]} if q == 0 else {"acc_writes": [btp]}))
                    dst = hT[:, half * 8:half * 8 + 8, st * 128:(st + 1) * 128]
                    src = tp[:].rearrange("p (k t) -> p k t", k=8)
                    kw = {"writes": [hbufs[st]]} if half == 0 else {"acc_writes": [hbufs[st]]}
                    if half == 0:
                        fw.op("act", lambda dst=dst, src=src: A.copy(dst, src), reads=[btp], **kw)
                    else:
                        fw.op("dve", lambda dst=dst, src=src: V.tensor_copy(dst, src), reads=[btp], **kw)

        def rope_tables(c, pos_src, cos2, sin2, b_cs):
            TWO_PI = float(2 * np.pi)
            C1 = 6.28125
            C2 = TWO_PI - C1
            pi_t, bpi = c["posi"].next()
            fw.dma("sp", pi_t[:], pos_src.partition_broadcast(64), writes=[bpi])
            ang, bang = c["ang"].next()
            y, by = c["ang"].next()
            r, br = c["ang"].next()
            m, bm = c["ang"].next()
            ni, bni = c["posi"].next()
            fw.op("dve", lambda: V.tensor_copy(ang[:], pi_t[:]), reads=[bpi], writes=[bang])
            fw.op("dve", lambda: V.tensor_scalar(ang[:], ang[:], invf_t[:, 0:1], None, op0=ALU.mult), reads=[bang, b_cst], writes=[bang])
            fw.op("dve", lambda: V.tensor_scalar(y[:], ang[:], 1.0 / TWO_PI, 0.5, op0=ALU.mult, op1=ALU.add), reads=[bang], writes=[by])
            fw.op("dve", lambda: V.tensor_copy(ni[:], y[:]), reads=[by], writes=[bni])
            fw.op("dve", lambda: V.tensor_copy(y[:], ni[:]), reads=[bni], writes=[by])
            fw.op("dve", lambda: V.scalar_tensor_tensor(r[:], y[:], -C1, ang[:], op0=ALU.mult, op1=ALU.add), reads=[by, bang], writes=[br])
            fw.op("dve", lambda: V.scalar_tensor_tensor(r[:], y[:], -C2, r[:], op0=ALU.mult, op1=ALU.add), reads=[by, br], writes=[br])

            def wrap(t, bt):
                fw.op("dve", lambda: V.tensor_scalar(m[:], t[:], float(-np.pi), None, op0=ALU.is_lt), reads=[bt], writes=[bm])
                fw.op("dve", lambda: V.scalar_tensor_tensor(t[:], m[:], TWO_PI, t[:], op0=ALU.mult, op1=ALU.add), reads=[bm, bt], writes=[bt])
                fw.op("dve", lambda: V.tensor_scalar(m[:], t[:], float(np.pi), None, op0=ALU.is_gt), reads=[bt], writes=[bm])
                fw.op("dve", lambda: V.scalar_tensor_tensor(t[:], m[:], -TWO_PI, t[:], op0=ALU.mult, op1=ALU.add), reads=[bm, bt], writes=[bt])
                fw.op("dve", lambda: V.tensor_scalar(t[:], t[:], float(-np.pi), float(np.pi), op0=ALU.max, op1=ALU.min), reads=[bt], writes=[bt])

            wrap(r, br)
            fw.op("act", lambda: A.activation(sin2[:], r[:], AF.Sin), reads=[br], writes=[b_cs])
            fw.op("dve", lambda: V.tensor_scalar(y[:], r[:], float(np.pi / 2), None, op0=ALU.add), reads=[br], writes=[by])
            wrap(y, by)
            fw.op("act", lambda: A.activation(cos2[:], y[:], AF.Sin), reads=[by], writes=[b_cs])

        def featmajor_norm(c, ps_list, bps_list, outT, b_out, psb, bpsb):
            cf_t, bcf = c["cfm"].next()
            sq_t, bsq = c["sq"].next()
            for i in range(4):
                fw.op("dve", lambda i=i: V.tensor_copy(cf_t[:, i * 512:(i + 1) * 512], ps_list[i][:]), reads=[bps_list[i]],
                      **({"writes": [bcf]} if i == 0 else {"acc_writes": [bcf]}))
                fw.op("act", lambda i=i: A.activation(sq_t[:, i * 512:(i + 1) * 512], ps_list[i][:], AF.Square), reads=[bps_list[i]],
                      **({"writes": [bsq]} if i == 0 else {"acc_writes": [bsq]}))
            for i in range(4):
                fw.op("pe", lambda i=i: T.matmul(psb[:], ones, sq_t[:, i * 512:(i + 1) * 512], start=(i == 0), stop=(i == 3)),
                      reads=[bsq, b_cst], **({"writes": [bpsb]} if i == 0 else {"acc_writes": [bpsb]}))
            rs, brs = c["rsb"].next()
            fw.op("act", lambda: A.activation(rs[:], psb[:], AF.Sqrt, bias=EPS, scale=1.0 / 512), reads=[bpsb], writes=[brs])
            fw.op("dve", lambda: V.reciprocal(rs[:], rs[:]), reads=[brs], writes=[brs])
            for i in range(4):
                fw.op("dve", lambda i=i: V.tensor_tensor(outT[:, i, :], cf_t[:, i * 512:(i + 1) * 512], rs[:], ALU.mult), reads=[bcf, brs],
                      **({"writes": [b_out]} if i == 0 else {"acc_writes": [b_out]}))

        with ExitStack() as ph:
            alloc = lambda n, sh, dt: ph.enter_context(nc.sbuf_tensor(uniq(n), sh, dt))
            palloc = lambda n, sh, dt: ph.enter_context(nc.psum_tensor(uniq(n), sh, dt))
            c = {
                "xs": Ring(alloc, "xs", 2, [128, D], F32), "ss": Ring(alloc, "ss", 4, [128, 1], F32),
                "junk": Ring(alloc, "junk", 1, [128, D], BF16), "xn": Ring(alloc, "xn", 2, [128, D], BF16),
                "tp": Ring(palloc, "tp", 2, [128, 1024], BF16, excl=True),
                "posi": Ring(alloc, "posi", 2, [64, 512], I32), "ang": Ring(alloc, "ang", 4, [64, 512], F32),
                "cfm": Ring(alloc, "cfm", 1, [128, 2048], F32), "sq": Ring(alloc, "sq", 1, [128, 2048], BF16),
                "rsb": Ring(alloc, "rsb", 1, [128, 512], F32),
            }
            hTr = [(alloc(f"hT{i}", [128, 16, 512], BF16), [Buf(f"hT{i}_{s}") for s in range(4)]) for i in range(2)]
            wring = Ring(alloc, "wr", 2, [128, 16 * 512], BF16)
            psr = Ring(palloc, "ps", 6, [128, 512], F32, excl=True)
            wsm = {n: alloc("w_" + n, [128, 4 * 1024], BF16) for n in ("wkn", "wvm", "wqn", "wqr")}
            wkr_t = alloc("w_wkr", [128, 16 * 128], BF16)
            b_wsm = Buf("wsm")
            for n in wsm:
                fw.dma("sp", wsm[n][:].rearrange("p (k n) -> p k n", k=4), wb[n], reads=[b_wb[n]], acc_writes=[b_wsm])
            fw.dma("sp", wkr_t[:].rearrange("p (k n) -> p k n", k=16), wb["wkr"], reads=[b_wb["wkr"]], acc_writes=[b_wsm])
            wsmv = {n: wsm[n][:].rearrange("p (k n) -> p k n", k=4) for n in wsm}
            wkrv = wkr_t[:].rearrange("p (k n) -> p k n", k=16)
            cos2, sin2 = alloc("cos2", [64, 512], F32), alloc("sin2", [64, 512], F32)
            b_cs = Buf("cs")
            stg8 = Ring(alloc, "stg8", 2, [128, 8 * 512], BF16)
            stgv = Ring(alloc, "stgv", 2, [128, 1024], BF16)
            cnT = alloc("cnT", [128, 4, 512], BF16)
            b_cn = Buf("cnT")
            rt = Ring(alloc, "rt", 2, [64, 512], F32)
            stgr = Ring(alloc, "stgr", 2, [64, 8 * 512], BF16)
            evt = [0]

            def evac(dst, src, bsrc, kw):
                evt[0] += 1
                if evt[0] % 2:
                    fw.op("act", lambda: A.copy(dst, src), reads=[bsrc], **kw)
                else:
                    fw.op("dve", lambda: V.tensor_copy(dst, src), reads=[bsrc], **kw)

            def load_w(name, kc_n, col0, ncols):
                wt, bw = wring.next()
                v = wt[:, 0:kc_n * ncols].rearrange("p (k n) -> p k n", k=kc_n)
                fw.dma("sp", v, wb[name][:, :, col0:col0 + ncols], reads=[b_wb[name]], writes=[bw])
                return v, bw

            def proj_feat(wv, bw, wcols, KCn, rhsT, rbufs, M=128):
                pt, bp = psr.next()
                for kc in range(KCn):
                    fw.op("pe", lambda kc=kc: T.matmul(pt[0:M, :], wv[:, kc, wcols[0]:wcols[1]], rhsT[:, kc, :], start=(kc == 0), stop=(kc == KCn - 1)),
                          reads=[bw] + rbufs, **({"writes": [bp]} if kc == 0 else {"acc_writes": [bp]}))
                return pt, bp

            def proj_tok(wv, bw, wcols, KCn, lhsT, lbufs, st):
                pt, bp = psr.next()
                for kc in range(KCn):
                    fw.op("pe", lambda kc=kc: T.matmul(pt[:], lhsT[:, kc, st * 128:(st + 1) * 128], wv[:, kc, wcols[0]:wcols[1]], start=(kc == 0), stop=(kc == KCn - 1)),
                          reads=[bw] + lbufs, **({"writes": [bp]} if kc == 0 else {"acc_writes": [bp]}))
                return pt, bp

            def rope_combine(pa, bpa, pb, bpb, dst, kw):
                t1, b1 = rt.next()
                t2, b2 = rt.next()
                fw.op("dve", lambda: V.tensor_tensor(t1[:], pa[0:64, :], cos2[:], ALU.mult), reads=[bpa, b_cs], writes=[b1])
                fw.op("dve", lambda: V.tensor_tensor(t2[:], pb[0:64, :], sin2[:], ALU.mult), reads=[bpb, b_cs], writes=[b2])
                fw.op("dve", lambda: V.tensor_tensor(dst, t1[:], t2[:], ALU.add), reads=[b1, b2], **kw)

            for kt in range(cfg["nkt"] if "A" in cfg["phases"] else 0):
                hT, hb = hTr[kt % 2]
                nkt_ = cfg["nkt"]
                if kt == 0:
                    front_end(c, lambda st: x_all[st * 128:(st + 1) * 128, :], 4, hT, hb)

                def fe_next(st, kt=kt):
                    if kt + 1 < nkt_:
                        hT2, hb2 = hTr[(kt + 1) % 2]
                        front_end_sub(c, lambda s_, kt=kt: x_all[(kt + 1) * 512 + s_ * 128: (kt + 1) * 512 + (s_ + 1) * 128, :], st, hT2, hb2)
                if cfg.get('astop', 99) < 2:
                    continue
                rope_tables(c, pos_all[:, kt * 512:(kt + 1) * 512], cos2, sin2, b_cs)
                if cfg.get('astop', 99) < 3:
                    continue
                sk, bsk = stg8.next()
                for cg in range(2):
                    wv, bw = load_w("wk", 16, cg * 512, 512)
                    for hh in range(4):
                        pt, bp = proj_feat(wv, bw, (hh * 128, hh * 128 + 128), 16, hT, hb)
                        h = cg * 4 + hh
                        evac(sk[:, h * 512:(h + 1) * 512], pt[:], bp, {"writes": [bsk]} if h == 0 else {"acc_writes": [bsk]})
                    fe_next(cg)
                fw.dma("pool", Kscr[:, :, kt * 512:(kt + 1) * 512], sk[:].rearrange("p (h t) -> p h t", h=8), reads=[bsk], acc_writes=[b_scr["K"]])
                if cfg.get('astop', 99) < 4:
                    continue
                wvs = [load_w("wv", 16, cg * 512, 512) for cg in range(2)]
                for st in range(4):
                    sv, bsv = stgv.next()
                    for cg in range(2):
                        pt, bp = proj_tok(wvs[cg][0], wvs[cg][1], (0, 512), 16, hT, [hb[st]], st)
                        evac(sv[:, cg * 512:(cg + 1) * 512], pt[:], bp, {"writes": [bsv]} if cg == 0 else {"acc_writes": [bsv]})
                    fw.dma("pool", Vscr.rearrange("h p k d -> p h k d")[:, :, kt * 4 + st, :], sv[:].rearrange("p (h d) -> p h d", h=8),
                           reads=[bsv], acc_writes=[b_scr["V"]])
                if cfg.get('astop', 99) < 5:
                    continue
                fe_next(2)
                wv, bw = load_w("wckv", 16, 0, 512)
                pl = [proj_feat(wv, bw, (i * 128, i * 128 + 128), 16, hT, hb) for i in range(4)]
                psb, bpsb = psr.next()
                featmajor_norm(c, [p[0] for p in pl], [p[1] for p in pl], cnT, b_cn, psb, bpsb)
                if cfg.get('astop', 99) < 6:
                    continue
                fe_next(3)
                pa, bpa = proj_feat(wkrv, b_wsm, (0, 64), 16, hT, hb, M=64)
                pb, bpb = proj_feat(wkrv, b_wsm, (64, 128), 16, hT, hb, M=64)
                sr, bsr = stgr.next()
                rope_combine(pa, bpa, pb, bpb, sr[:, 0:512], {"writes": [bsr]})
                fw.dma("pool", KRscr[:, kt * 512:(kt + 1) * 512], sr[:, 0:512], reads=[bsr], acc_writes=[b_scr["KR"]])
                if cfg.get('astop', 99) < 7:
                    continue
                sk, bsk = stg8.next()
                for h in range(8):
                    pt, bp = proj_feat(wsmv["wkn"], b_wsm, (h * 128, h * 128 + 128), 4, cnT, [b_cn])
                    evac(sk[:, h * 512:(h + 1) * 512], pt[:], bp, {"writes": [bsk]} if h == 0 else {"acc_writes": [bsk]})
                fw.dma("pool", KNscr[:, :, kt * 512:(kt + 1) * 512], sk[:].rearrange("p (h t) -> p h t", h=8), reads=[bsk], acc_writes=[b_scr["KN"]])
                if cfg.get('astop', 99) < 8:
                    continue
                for st in range(4):
                    sv, bsv = stgv.next()
                    for cg in range(2):
                        pt, bp = proj_tok(wsmv["wvm"], b_wsm, (cg * 512, cg * 512 + 512), 4, cnT, [b_cn], st)
                        evac(sv[:, cg * 512:(cg + 1) * 512], pt[:], bp, {"writes": [bsv]} if cg == 0 else {"acc_writes": [bsv]})
                    fw.dma("pool", VMscr.rearrange("h p k d -> p h k d")[:, :, kt * 4 + st, :], sv[:].rearrange("p (h d) -> p h d", h=8),
                           reads=[bsv], acc_writes=[b_scr["VM"]])
            for k in range(cfg["nslotA"] if "A" in cfg["phases"] else 0):
                hT, hb = hTr[k % 2]
                nsl_ = cfg["nslotA"]
                if k == 0:
                    front_end(c, lambda st: x_own[st * 128:(st + 1) * 128, :], 4, hT, hb)

                def feq_next(st, k=k):
                    if k + 1 < nsl_:
                        hT2, hb2 = hTr[(k + 1) % 2]
                        front_end_sub(c, lambda s_, k=k: x_own[(k + 1) * 512 + s_ * 128: (k + 1) * 512 + (s_ + 1) * 128, :], st, hT2, hb2)
                rope_tables(c, pos_own[:, k * 512:(k + 1) * 512], cos2, sin2, b_cs)
                sk, bsk = stg8.next()
                for cg in range(2):
                    wv, bw = load_w("wq", 16, cg * 512, 512)
                    for hh in range(4):
                        pt, bp = proj_feat(wv, bw, (hh * 128, hh * 128 + 128), 16, hT, hb)
                        h = cg * 4 + hh
                        evac(sk[:, h * 512:(h + 1) * 512], pt[:], bp, {"writes": [bsk]} if h == 0 else {"acc_writes": [bsk]})
                    feq_next(cg)
                fw.dma("pool", Qscr[:, :, k * 512:(k + 1) * 512], sk[:].rearrange("p (h t) -> p h t", h=8), reads=[bsk], acc_writes=[b_scr["Q"]])
                wv, bw = load_w("wcq", 16, 0, 512)
                pl = [proj_feat(wv, bw, (i * 128, i * 128 + 128), 16, hT, hb) for i in range(4)]
                feq_next(2)
                psb, bpsb = psr.next()
                featmajor_norm(c, [p[0] for p in pl], [p[1] for p in pl], cnT, b_cn, psb, bpsb)
                feq_next(3)
                sk, bsk = stg8.next()
                for h in range(8):
                    pt, bp = proj_feat(wsmv["wqn"], b_wsm, (h * 128, h * 128 + 128), 4, cnT, [b_cn])
                    evac(sk[:, h * 512:(h + 1) * 512], pt[:], bp, {"writes": [bsk]} if h == 0 else {"acc_writes": [bsk]})
                fw.dma("pool", QNscr[:, :, k * 512:(k + 1) * 512], sk[:].rearrange("p (h t) -> p h t", h=8), reads=[bsk], acc_writes=[b_scr["QN"]])
                sr, bsr = stgr.next()
                for h in range(8):
                    pa, bpa = proj_feat(wsmv["wqr"], b_wsm, (h * 128, h * 128 + 64), 4, cnT, [b_cn], M=64)
                    pb, bpb = proj_feat(wsmv["wqr"], b_wsm, (h * 128 + 64, h * 128 + 128), 4, cnT, [b_cn], M=64)
                    rope_combine(pa, bpa, pb, bpb, sr[:, h * 512:(h + 1) * 512], {"writes": [bsr]} if h == 0 else {"acc_writes": [bsr]})
                fw.dma("pool", QRscr[:, :, k * 512:(k + 1) * 512], sr[:].rearrange("p (h t) -> p h t", h=8), reads=[bsr], acc_writes=[b_scr["QR"]])
            fw.barrier()
            fw.emit()

        def load_masks(alloc, src, dstname):
            mt = alloc(dstname, [128, 16 * 512], BF16)
            bm = Buf(dstname)
            stg = Ring(alloc, dstname + "s", 2, [128, 2048], F32)
            for i in range(4):
                t, b = stg.next()
                fw.dma("sp", t[:], src[:, i * 2048:(i + 1) * 2048], writes=[b])
                fw.op("dve", lambda t=t, i=i: V.tensor_copy(mt[:, i * 2048:(i + 1) * 2048], t[:]), reads=[b],
                      **({"writes": [bm]} if i == 0 else {"acc_writes": [bm]}))
            return mt, bm

        with ExitStack() as ph:
            alloc = lambda n, sh, dt: ph.enter_context(nc.sbuf_tensor(uniq(n), sh, dt))
            palloc = lambda n, sh, dt: ph.enter_context(nc.psum_tensor(uniq(n), sh, dt))
            do_sb = "B" in cfg["phases"]
            do_ml = "M" in cfg["phases"]
            msk, bmsk = load_masks(alloc, mask_sb, "msb")
            mskm, bmskm = load_masks(alloc, mask_ml, "mml")
            KT, bK = alloc("KT", [128, S], BF16), Buf("KT")
            Vt, bV = alloc("Vt", [128, 64 * 128], BF16), Buf("Vt")
            QT, bQ = alloc("QT", [128, S // 2], BF16), Buf("QT")
            KN, bKN = alloc("KN", [128, S], BF16), Buf("KN")
            VMt, bVM = alloc("VMt", [128, 64 * 128], BF16), Buf("VMt")
            QN, bQN = alloc("QN", [128, S // 2], BF16), Buf("QN")
            QR, bQR = alloc("QR", [64, S // 2], BF16), Buf("QR")
            KRt, bKR = alloc("KRt", [64, S], BF16), Buf("KRt")
            if do_ml:
                fw.dma("sp", KRt[:], KRscr, reads=[b_scr["KR"]], writes=[bKR])
            zr = Ring(palloc, "zps", 2, [128, 512], F32, excl=True)
            Rp, bR = palloc("Rps", [128, 512], F32), Buf("R", True)
            Os, bOs = palloc("Ops", [128, 512], F32), Buf("O", True)
            sr_ = Ring(palloc, "sps", 2, [128, 512], F32, excl=True)
            Om, bOm = palloc("Omps", [128, 512], F32), Buf("Om", True)
            Dp, bD = palloc("Dps", [128, 512], F32), Buf("Dm", True)
            Er = Ring(alloc, "Ef", 3, [128, 512], F32)
            Lr = Ring(alloc, "Lb", 3, [128, 512], BF16)
            Gr = Ring(alloc, "Gf", 2, [128, 512], F32)
            Wr = Ring(alloc, "wbt", 3, [128, 512], BF16)
            Pr = Ring(alloc, "Pb", 4, [128, 512], BF16)
            rdr = Ring(alloc, "rd", 2, [128, 512], F32)
            osr = Ring(alloc, "ost", 2, [128, 512], BF16)
            omr = Ring(alloc, "omt", 2, [128, 512], BF16)
            for h in range(max(cfg["hsb"] if do_sb else 0, cfg["hml"] if do_ml else 0)):
                sb_on = do_sb and h < cfg["hsb"]
                ml_on = do_ml and h < cfg["hml"]
                if sb_on:
                    fw.dma("sp", KT[:], Kscr[:, h, :], reads=[b_scr["K"]], writes=[bK])
                    fw.dma("sp", QT[:], Qscr[:, h, :], reads=[b_scr["Q"]], writes=[bQ])
                    for q4 in range(4):
                        fw.dma("sp", Vt[:, q4 * 2048:(q4 + 1) * 2048].rearrange("p (k d) -> p k d", k=16), Vscr[h, :, q4 * 16:(q4 + 1) * 16, :],
                               reads=[b_scr["V"]], **({"writes": [bV]} if q4 == 0 else {"acc_writes": [bV]}))
                if ml_on:
                    fw.dma("sp", KN[:], KNscr[:, h, :], reads=[b_scr["KN"]], writes=[bKN])
                    fw.dma("sp", QN[:], QNscr[:, h, :], reads=[b_scr["QN"]], writes=[bQN])
                    fw.dma("sp", QR[:], QRscr[:, h, :], reads=[b_scr["QR"]], writes=[bQR])
                    for q4 in range(4):
                        fw.dma("sp", VMt[:, q4 * 2048:(q4 + 1) * 2048].rearrange("p (k d) -> p k d", k=16), VMscr[h, :, q4 * 16:(q4 + 1) * 16, :],
                               reads=[b_scr["VM"]], **({"writes": [bVM]} if q4 == 0 else {"acc_writes": [bVM]}))
                for k in range(cfg["nslotB"]):
                    nb = 8 * (k + 1)
                    par = k % 2
                    st_ = {}

                    def stA(i, k=k, nb=nb, par=par, st_=st_):
                        kb = nb - 1 - i
                        zp, bz = zr.next()
                        bnd = kb >= 8 * k
                        fw.op("pe", lambda: T.matmul(zp[:], KT[:, kb * 128:(kb + 1) * 128], QT[:, k * 512:(k + 1) * 512], start=True, stop=not bnd),
                              reads=[bK, bQ], writes=[bz])
                        if bnd:
                            m0 = (par * 8 + kb - 8 * k) * 512
                            fw.op("pe", lambda: T.matmul(zp[:], ident, msk[:, m0:m0 + 512], start=False, stop=True),
                                  reads=[bmsk, b_cst], acc_writes=[bz])
                        st_[("z", i)] = (zp, bz)

                    def stB(i, st_=st_):
                        zp, bz = st_.pop(("z", i))
                        (Ef, bE), (Lb, bL) = Er.next(), Lr.next()
                        fw.op("act", lambda: A.activation(Ef[:], zp[:], AF.Exp, scale=SB_SCALE), reads=[bz], writes=[bE])
                        fw.op("act", lambda: A.activation(Lb[:], Ef[:], AF.Ln, bias=1.0, scale=1.0), reads=[bE], writes=[bL])
                        st_[("E", i)] = (Ef, bE)
                        st_[("L", i)] = (Lb, bL)

                    def stC(i, st_=st_):
                        Lb, bL = st_[("L", i)]
                        fw.op("pe", lambda: T.matmul(Rp[:], uincl, Lb[:], start=(i == 0), stop=True, skip_group_check=True),
                              reads=[bL, b_cst], writes=[bR])

                    def stD(i, st_=st_):
                        Gf, bG = Gr.next()
                        fw.op("act", lambda: A.activation(Gf[:], Rp[:], AF.Exp, scale=-1.0), reads=[bR], writes=[bG])
                        st_[("G", i)] = (Gf, bG)

                    def stE(i, st_=st_):
                        (Ef, bE), (Gf, bG) = st_.pop(("E", i)), st_.pop(("G", i))
                        wt_, bw_ = Wr.next()
                        fw.op("dve", lambda: V.tensor_tensor(wt_[:], Ef[:], Gf[:], ALU.mult), reads=[bE, bG], writes=[bw_])
                        st_[("w", i)] = (wt_, bw_)

                    def stF1(i, nb=nb, st_=st_):
                        Lb, bL = st_.pop(("L", i))
                        if i < nb - 1:
                            fw.op("pe", lambda: T.matmul(Rp[:], ubar, Lb[:], start=False, stop=True, skip_group_check=True),
                                  reads=[bL, b_cst], writes=[bR])

                    def stF2(i, nb=nb, st_=st_):
                        kb = nb - 1 - i
                        wt_, bw_ = st_.pop(("w", i))
                        fw.op("pe", lambda: T.matmul(Os[:], Vt[:, kb * 128:(kb + 1) * 128], wt_[:], start=(i == 0), stop=(i == nb - 1), skip_group_check=True),
                              reads=[bV, bw_], **({"writes": [bOs]} if i == 0 else {"acc_writes": [bOs]}))

                    def mA(i, k=k, par=par, st_=st_):
                        kb = i
                        sp_, bs = sr_.next()
                        bnd = kb >= 8 * k
                        fw.op("pe", lambda: T.matmul(sp_[:], KN[:, kb * 128:(kb + 1) * 128], QN[:, k * 512:(k + 1) * 512], start=True, stop=False),
                              reads=[bKN, bQN], writes=[bs])
                        fw.op("pe", lambda: T.matmul(sp_[:], KRt[:, kb * 128:(kb + 1) * 128], QR[:, k * 512:(k + 1) * 512], start=False, stop=not bnd),
                              reads=[bKR, bQR], acc_writes=[bs])
                        if bnd:
                            m0 = (par * 8 + kb - 8 * k) * 512
                            fw.op("pe", lambda: T.matmul(sp_[:], ident, mskm[:, m0:m0 + 512], start=False, stop=True),
                                  reads=[bmskm, b_cst], acc_writes=[bs])
                        st_[("s", i)] = (sp_, bs)

                    def mB(i, st_=st_):
                        sp_, bs = st_.pop(("s", i))
                        Pb, bP = Pr.next()
                        fw.op("act", lambda: A.activation(Pb[:], sp_[:], AF.Exp, scale=MLA_SCALE), reads=[bs], writes=[bP])
                        st_[("P", i)] = (Pb, bP)

                    def mC(i, nb=nb, st_=st_):
                        kb = i
                        Pb, bP = st_.pop(("P", i))
                        fw.op("pe", lambda: T.matmul(Om[:], VMt[:, kb * 128:(kb + 1) * 128], Pb[:], start=(i == 0), stop=(i == nb - 1)),
                              reads=[bVM, bP], **({"writes": [bOm]} if i == 0 else {"acc_writes": [bOm]}))
                        fw.op("pe", lambda: T.matmul(Dp[:], ones, Pb[:], start=(i == 0), stop=(i == nb - 1)),
                              reads=[bP, b_cst], **({"writes": [bD]} if i == 0 else {"acc_writes": [bD]}))

                    for t in range(nb + 2):
                        if t < nb:
                            if sb_on:
                                stA(t)
                            if ml_on:
                                mA(t)
                            if sb_on:
                                stB(t)
                        if ml_on and 0 <= t - 1 < nb:
                            mB(t - 1)
                        if sb_on and 0 <= t - 2 < nb:
                            stF1(t - 2)
                        if sb_on and 0 <= t - 1 < nb:
                            stC(t - 1)
                            stD(t - 1)
                            stE(t - 1)
                        if sb_on and 0 <= t - 2 < nb:
                            stF2(t - 2)
                        if ml_on and 0 <= t - 2 < nb:
                            mC(t - 2)
                    if sb_on:
                        ot, bo = osr.next()
                        fw.op("dve", lambda ot=ot: V.tensor_copy(ot[:], Os[:]), reads=[bOs], writes=[bo])
                        fw.dma("pool", OSscr[:, h, k * 512:(k + 1) * 512], ot[:], reads=[bo], acc_writes=[b_scr["OS"]])
                    if ml_on:
                        rd, brd = rdr.next()
                        ot, bo = omr.next()
                        fw.op("dve", lambda rd=rd: V.reciprocal(rd[:], Dp[:]), reads=[bD], writes=[brd])
                        fw.op("dve", lambda rd=rd, ot=ot: V.tensor_tensor(ot[:], Om[:], rd[:], ALU.mult), reads=[bOm, brd], writes=[bo])
                        fw.dma("pool", OMscr[:, h, k * 512:(k + 1) * 512], ot[:], reads=[bo], acc_writes=[b_scr["OM"]])
            fw.barrier()
            fw.emit()

        NS = TC // 128
        with ExitStack() as ph:
            alloc = lambda n, sh, dt: ph.enter_context(nc.sbuf_tensor(uniq(n), sh, dt))
            palloc = lambda n, sh, dt: ph.enter_context(nc.psum_tensor(uniq(n), sh, dt))
            xnr = Ring(alloc, "xn", 2, [128, D], BF16)
            c = {
                "ss": Ring(alloc, "ss", 4, [128, 1], F32),
                "junk": xnr, "xn": xnr,
                "tp": Ring(palloc, "tp", 2, [128, 1024], BF16, excl=True),
            }
            xres = alloc("xres", [128, NS, D], F32)
            bxr = [Buf(f"xr{i}") for i in range(NS)]
            ysb = alloc("ysb", [128, NS, D], F32)
            bys = [Buf(f"ys{i}") for i in range(NS)]
            ysbf = ysb[:].rearrange("p s d -> p (s d)")
            big = alloc("big", [128, 32 * TC], BF16)
            b_big = Buf("big")
            osT = big[:, 0:8 * TC].rearrange("p (k t) -> p k t", k=8)
            omT = big[:, 8 * TC:16 * TC].rearrange("p (k t) -> p k t", k=8)
            uT = big[:].rearrange("p (k t) -> p k t", k=32)
            actT = [(alloc(f"aT{i}", [128, 16, TC], BF16), [Buf(f"aT{i}_{s}") for s in range(NS)]) for i in range(2)]
            gbr = Ring(alloc, "gb", 1, [128, D], F32)
            wring = Ring(alloc, "wr", 3, [128, 16 * 512], BF16)
            psr = Ring(palloc, "ps", 6, [128, 512], F32, excl=True)
            sgr = Ring(alloc, "sg", 3, [128, 512], F32)
            pf, bpf = alloc("pf", [128, NS, 256], F32), Buf("pf")
            pbf, bpbf = alloc("pbf", [128, NS, 256], BF16), Buf("pbf")
            pT, bpT = alloc("pT", [128, 2, TC], BF16), Buf("pT")
            evt = [0]

            def evac(dst, src, bsrc, kw, reads=()):
                evt[0] += 1
                if evt[0] % 2:
                    fw.op("act", lambda: A.copy(dst, src), reads=[bsrc] + list(reads), **kw)
                else:
                    fw.op("dve", lambda: V.tensor_copy(dst, src), reads=[bsrc] + list(reads), **kw)

            def load_w(name, k0, kn, col0, ncols):
                wt, bw = wring.next()
                v = wt[:, 0:kn * ncols].rearrange("p (k n) -> p k n", k=kn)
                fw.dma("sp", v, wb[name][:, k0:k0 + kn, col0:col0 + ncols], reads=[b_wb[name]], writes=[bw])
                return v, bw

            def load_g(i):
                gt, bg = gbr.next()
                fw.dma("sp", gt[:], grow[i:i + 1, :].partition_broadcast(128), writes=[bg])
                return gt, bg

            def proj_feat(wv, bw, wcols, KCn, rhsT, rbufs):
                pt, bp = psr.next()
                for kc in range(KCn):
                    fw.op("pe", lambda kc=kc: T.matmul(pt[:, 0:TC], wv[:, kc, wcols[0]:wcols[1]], rhsT[:, kc, :], start=(kc == 0), stop=(kc == KCn - 1)),
                          reads=[bw] + rbufs, **({"writes": [bp]} if kc == 0 else {"acc_writes": [bp]}))
                return pt, bp

            def tok_mm(pt, bp, lhsT, lbufs, st, wv, bw, kcs, first, last):
                for j, (kc_l, kc_w) in enumerate(kcs):
                    fw.op("pe", lambda kc_l=kc_l, kc_w=kc_w, j=j: T.matmul(pt[:], lhsT[:, kc_l, st * 128:(st + 1) * 128], wv[:, kc_w, :],
                                                                  start=(first and j == 0), stop=(last and j == len(kcs) - 1), skip_group_check=True),
                          reads=[bw] + lbufs, **({"writes": [bp]} if (first and j == 0) else {"acc_writes": [bp]}))

            def post_norm(gi):
                gt, bg = load_g(gi)
                for st in range(NS):
                    ss, bss = c["ss"].next()
                    jk, bjk = c["junk"].next()
                    fw.op("dve", lambda ss=ss: V.memset(ss[:], 0.0), writes=[bss])
                    fw.op("act", lambda jk=jk, ss=ss, st=st: A.activation(jk[:], ysb[:, st, :], AF.Square, accum_out=ss[:]), reads=[bys[st], bss], writes=[bjk, bss])
                    fw.op("act", lambda ss=ss: A.activation(ss[:], ss[:], AF.Sqrt, bias=EPS, scale=1.0 / D), reads=[bss], writes=[bss])
                    fw.op("dve", lambda ss=ss: V.reciprocal(ss[:], ss[:]), reads=[bss], writes=[bss])
                    fw.op("dve", lambda ss=ss, st=st: V.scalar_tensor_tensor(ysb[:, st, :], ysb[:, st, :], ss[:, 0:1], gt[:], op0=ALU.mult, op1=ALU.mult),
                          reads=[bys[st], bss, bg], writes=[bys[st]])

            def residual_add():
                for st in range(NS):
                    fw.op("dve", lambda st=st: V.tensor_tensor(xres[:, st, :], xres[:, st, :], ysb[:, st, :], ALU.add), reads=[bxr[st], bys[st]], writes=[bxr[st]])

            for cs in range(cfg["ncs"] if "C" in cfg["phases"] else 0):
                r0 = cs * TC
                xd = [(xres[:, st, :], bxr[st]) for st in range(NS)]
                hT, hb = actT[0]
                front_end(c, lambda st, r0=r0: x_own[r0 + st * 128: r0 + (st + 1) * 128, :], NS, hT, hb, xdst=xd)
                fw.dma("sp", osT, OSscr[:, :, r0:r0 + TC], reads=[b_scr["OS"]], writes=[b_big])
                fw.dma("sp", omT, OMscr[:, :, r0:r0 + TC], reads=[b_scr["OM"]], acc_writes=[b_big])
                mT, mb = actT[1]
                bm_all = Buf("mixedT")
                for og in range(4):
                    wo, bwo = wring.next()
                    wov = wo[:].rearrange("p (a k n) -> p a k n", a=2, k=8)
                    fw.dma("sp", wov[:, 0], wb["wsbo"][:, :, og * 512:(og + 1) * 512], reads=[b_wb["wsbo"]], writes=[bwo])
                    fw.dma("sp", wov[:, 1], wb["wmlao"][:, :, og * 512:(og + 1) * 512], reads=[b_wb["wmlao"]], acc_writes=[bwo])
                    for gsel in range(2):
                        wg, bwg = load_w("wgs" if gsel == 0 else "wgm", 0, 16, og * 512, 512)
                        for oo in range(4):
                            oc = og * 4 + oo
                            cols = (oo * 128, oo * 128 + 128)
                            pa, bpa = proj_feat(wov[:, gsel], bwo, cols, 8, osT if gsel == 0 else omT, [b_big])
                            pg, bpg = proj_feat(wg, bwg, cols, 16, hT, hb)
                            sg, bsg = sgr.next()
                            fw.op("act", lambda sg=sg, pg=pg: A.activation(sg[:, 0:TC], pg[:, 0:TC], AF.Sigmoid), reads=[bpg], writes=[bsg])
                            fw.op("dve", lambda sg=sg, pa=pa: V.tensor_tensor(sg[:, 0:TC], sg[:, 0:TC], pa[:, 0:TC], ALU.mult), reads=[bsg, bpa], writes=[bsg])
                            if gsel == 0:
                                fw.op("dve", lambda sg=sg, oc=oc: V.tensor_copy(ysbf[:, oc * TC:(oc + 1) * TC], sg[:, 0:TC]), reads=[bsg],
                                      acc_writes=[bys[0]])
                            else:
                                fw.op("dve", lambda sg=sg, oc=oc: V.tensor_tensor(mT[:, oc, :], sg[:, 0:TC], ysbf[:, oc * TC:(oc + 1) * TC], ALU.add),
                                      reads=[bsg, bys[0]], acc_writes=[bm_all])
                if dbgC and cs == 0:
                    fw.dma("pool", d_mixed, mT[:], reads=[bm_all], acc_writes=[b_dbg])
                for cg in range(4):
                    wv, bw = load_w("wout", 0, 16, cg * 512, 512)
                    for st in range(NS):
                        pt, bp = psr.next()
                        tok_mm(pt, bp, mT, [bm_all], st, wv, bw, [(kc, kc) for kc in range(16)], True, True)
                        evac(ysb[:, st, cg * 512:(cg + 1) * 512], pt[:], bp, {"writes": [bys[st]]} if cg == 0 else {"acc_writes": [bys[st]]},
                             reads=[bm_all] if cg == 0 else [])
                if dbgC and cs == 0:
                    fw.dma("pool", d_y, ysb[:], reads=bys, acc_writes=[b_dbg])
                post_norm(0)
                residual_add()
                if dbgC and cs == 0:
                    fw.dma("pool", d_x1, xres[:], reads=bxr, acc_writes=[b_dbg])
                h2T, h2b = actT[0]
                front_end(c, None, NS, h2T, h2b, xdst=xd)
                for fh in range(2):
                    for fg in range(8):
                        wv, bw = load_w("wup", 0, 16, (fh * 8 + fg) * 512, 512)
                        for fc in range(4):
                            pt, bp = proj_feat(wv, bw, (fc * 128, fc * 128 + 128), 16, h2T, h2b)
                            sg, bsg = sgr.next()
                            fw.op("act", lambda sg=sg, pt=pt: A.activation(sg[:, 0:TC], pt[:, 0:TC], AF.Relu), reads=[bp], writes=[bsg])
                            fw.op("dve", lambda sg=sg, f=fg * 4 + fc: V.tensor_tensor(uT[:, f, :], sg[:, 0:TC], sg[:, 0:TC], ALU.mult), reads=[bsg],
                                  **({"writes": [b_big]} if (fg == 0 and fc == 0) else {"acc_writes": [b_big]}))
                    for cg in range(4):
                        accs = [psr.next() for _ in range(NS)]
                        for pc in range(4):
                            wv, bw = load_w("wdown", fh * 32 + pc * 8, 8, cg * 512, 512)
                            for st in range(NS):
                                tok_mm(accs[st][0], accs[st][1], uT, [b_big], st, wv, bw, [(pc * 8 + j, j) for j in range(8)], pc == 0, pc == 3)
                        for st in range(NS):
                            dst = ysb[:, st, cg * 512:(cg + 1) * 512]
                            if fh == 0:
                                evac(dst, accs[st][0][:], accs[st][1], {"writes": [bys[st]]} if cg == 0 else {"acc_writes": [bys[st]]})
                            else:
                                fw.op("dve", lambda dst=dst, ps_=accs[st][0]: V.tensor_tensor(dst, ps_[:], dst, ALU.add), reads=[accs[st][1], bys[st]],
                                      **({"writes": [bys[st]]} if cg == 0 else {"acc_writes": [bys[st]]}))
                post_norm(1)
                residual_add()
                if dbgC and cs == 0:
                    fw.dma("pool", d_x2, xres[:], reads=bxr, acc_writes=[b_dbg])
                fw.dma("sp", pf[:], p_own[r0:r0 + TC, :].rearrange("(s p) d -> p s d", p=128), writes=[bpf])
                fw.op("dve", lambda: V.tensor_copy(pbf[:], pf[:]), reads=[bpf], writes=[bpbf])
                tp, btp = c["tp"].next()
                for st in range(NS):
                    for kc in range(2):
                        j = st * 2 + kc
                        fw.op("pe", lambda st=st, kc=kc, j=j, tp=tp: T.transpose(tp[:, j * 128:(j + 1) * 128], pbf[:, st, kc * 128:(kc + 1) * 128], ident),
                              reads=[bpbf, b_cst], **({"writes": [btp]} if j == 0 else {"acc_writes": [btp]}))
                for st in range(NS):
                    fw.op("act", lambda st=st, tp=tp: A.copy(pT[:, :, st * 128:(st + 1) * 128], tp[:, st * 256:(st + 1) * 256].rearrange("p (k t) -> p k t", k=2)),
                          reads=[btp], **({"writes": [bpT]} if st == 0 else {"acc_writes": [bpT]}))
                for cg in range(4):
                    wv, bw = load_w("wple", 0, 2, cg * 512, 512)
                    for st in range(NS):
                        pt, bp = psr.next()
                        tok_mm(pt, bp, pT, [bpT], st, wv, bw, [(0, 0), (1, 1)], True, True)
                        evac(ysb[:, st, cg * 512:(cg + 1) * 512], pt[:], bp, {"writes": [bys[st]]} if cg == 0 else {"acc_writes": [bys[st]]})
                post_norm(2)
                if dbgC and cs == 0:
                    fw.dma("pool", d_e, ysb[:], reads=bys, acc_writes=[b_dbg])
                x2T, x2b = actT[1]
                front_end(c, None, NS, x2T, x2b, xdst=xd, norm=False)
                for cg in range(4):
                    wv, bw = load_w("wpg", 0, 16, cg * 512, 512)
                    for st in range(NS):
                        pt, bp = psr.next()
                        tok_mm(pt, bp, x2T, [x2b[st]], st, wv, bw, [(kc, kc) for kc in range(16)], True, True)
                        sg, bsg = sgr.next()
                        sl = slice(cg * 512, (cg + 1) * 512)
                        fw.op("act", lambda sg=sg, pt=pt: A.activation(sg[:], pt[:], AF.Sigmoid), reads=[bp], writes=[bsg])
                        fw.op("dve", lambda sg=sg, st=st, sl=sl: V.tensor_tensor(sg[:], sg[:], ysb[:, st, sl], ALU.mult), reads=[bsg, bys[st]], writes=[bsg])
                        fw.op("dve", lambda sg=sg, st=st, sl=sl: V.tensor_tensor(ysb[:, st, sl], sg[:], xres[:, st, sl], ALU.add), reads=[bsg, bxr[st]], writes=[bys[st]])
                for st in range(NS):
                    fw.dma("pool", out[r0 + st * 128: r0 + (st + 1) * 128, :], ysb[:, st, :], reads=[bys[st]], acc_writes=[b_scr["out"]])
            fw.barrier()
            st = fw.emit()
            print("program stats", st, flush=True)
    return nc


def _arr(w, kc):
    k, n = w.shape
    return np.ascontiguousarray(w.reshape(kc, 128, n).transpose(1, 0, 2))


def _masks(j, strict):
    sidx = np.arange(128)[:, None]
    tq = np.arange(512)[None, :]
    out = np.zeros((128, 16, 512), np.float32)
    for q in range(2):
        is_max = (j == 1) if q == 0 else (j == 0)
        for r in range(8):
            if is_max:
                m = None if r < 4 else r - 4
                allneg = False
            else:
                m = r if r < 4 else None
                allneg = r >= 4
            if allneg:
                blk = np.full((128, 512), NEG, np.float32)
            elif m is None:
                blk = np.zeros((128, 512), np.float32)
            else:
                vis = (128 * m + sidx) < tq if strict else (128 * m + sidx) <= tq
                blk = np.where(vis, 0.0, NEG).astype(np.float32)
            out[:, q * 8 + r, :] = blk
    return out.reshape(128, 16 * 512)


_PROG = {}


def _prep(x, p, positions, g_pre_mix, w_in, g_cq, g_ckv, w_q_up, w_kv_up, w_sb_o, w_mla_o, w_out,
           g_post_mix, g_pre_mlp, w_up, w_down, g_post_mlp, w_ple, g_ple, w_ple_gate):
    f = lambda a: np.asarray(a, dtype=np.float32)
    x, p = f(x), f(p)
    positions = np.asarray(positions).astype(np.int32)
    w_in0 = f(w_in)[0]
    kr = w_in0[:, 4096:4160]
    wqu = f(w_q_up)[0].reshape(512, 8, 192)
    rope = wqu[:, :, 128:192]
    wkv = f(w_kv_up)[0].reshape(512, 8, 256)
    shared = {
        "wq_f": _arr(w_in0[:, 0:1024], 16), "wk_f": _arr(w_in0[:, 1024:2048], 16), "wv_f": _arr(w_in0[:, 2048:3072], 16),
        "wcq_f": _arr(w_in0[:, 3072:3584], 16), "wckv_f": _arr(w_in0[:, 3584:4096], 16),
        "wkr_f": _arr(np.concatenate([kr, kr[:, 32:64], kr[:, 0:32]], axis=1), 16),
        "wgs_f": _arr(w_in0[:, 4160:6208], 16), "wgm_f": _arr(w_in0[:, 6208:8256], 16),
        "wqn_f": _arr(np.ascontiguousarray(wqu[:, :, 0:128]).reshape(512, 1024), 4),
        "wqr_f": _arr(np.concatenate([rope, rope[:, :, 32:64], rope[:, :, 0:32]], axis=2).reshape(512, 1024), 4),
        "wkn_f": _arr(np.ascontiguousarray(wkv[:, :, 0:128]).reshape(512, 1024), 4),
        "wvm_f": _arr(np.ascontiguousarray(wkv[:, :, 128:256]).reshape(512, 1024), 4),
        "wsbo_f": _arr(f(w_sb_o)[0], 8), "wmlao_f": _arr(f(w_mla_o)[0], 8), "wout_f": _arr(f(w_out)[0], 16),
        "wup_f": _arr(f(w_up)[0], 16), "wdown_f": _arr(f(w_down)[0], 64), "wple_f": _arr(f(w_ple)[0], 2),
        "wpg_f": _arr(f(w_ple_gate)[0], 16),
        "gpm": np.ascontiguousarray(f(g_pre_mix)[0].reshape(16, 128).T), "gcq": np.ascontiguousarray(f(g_cq)[0].reshape(4, 128).T),
        "gckv": np.ascontiguousarray(f(g_ckv)[0].reshape(4, 128).T), "gmlp": np.ascontiguousarray(f(g_pre_mlp)[0].reshape(16, 128).T),
        "grow": np.stack([f(g_post_mix)[0], f(g_post_mlp)[0], f(g_ple)[0]], axis=0),
    }
    jj = np.arange(128)[:, None]
    ss_ = np.arange(128)[None, :]
    shared["consts"] = np.concatenate([np.eye(128), (jj >= ss_), (jj < ss_), np.ones((128, 128))], axis=1).astype(np.float32)
    inv_freq = (np.float32(10000.0) ** (-np.arange(32, dtype=np.float32) / np.float32(32))).astype(np.float32)
    shared["invf"] = np.concatenate([inv_freq, inv_freq])[:, None].astype(np.float32)
    in_maps = []
    for c in range(NCORES):
        b, j = c // 2, c % 2
        tiles = SLOT_TILES[j]
        rows = np.concatenate([np.arange(t * 512, (t + 1) * 512) for t in tiles])
        m = dict(shared)
        m["x_all"] = np.ascontiguousarray(x[b])
        m["x_own"] = np.ascontiguousarray(x[b][rows])
        m["p_own"] = np.ascontiguousarray(p[0, b][rows])
        m["pos_all"] = np.ascontiguousarray(positions[b][None, :])
        m["pos_own"] = np.ascontiguousarray(positions[b][rows][None, :])
        m["mask_sb"] = _masks(j, True)
        m["mask_ml"] = _masks(j, False)
        in_maps.append(m)
    return in_maps


def kernel(**inputs):
    in_maps = _prep(**inputs)
    if "nc" not in _PROG:
        _PROG["nc"] = build_program()
    res = run_bass_kernel_spmd(_PROG["nc"], in_maps, core_ids=list(range(NCORES)))
    outp = np.empty((4, S, D), np.float32)
    for c in range(NCORES):
        b, j = c // 2, c % 2
        o = np.asarray(res.results[c]["out"])
        for i, t in enumerate(SLOT_TILES[j]):
            outp[b, t * 512:(t + 1) * 512] = o[i * 512:(i + 1) * 512]
    return outp
```
